# Optimizing a Trainium2 kernel written in Bass

```python
import math
import jax, jax.numpy as jnp
from jax import lax
import numpy as np

D_MODEL = 1024
BATCH = 16
SEQ = 4096
DEPTH = 2
DEC_BATCH = 8
DEC_SEQ = 8192
PAST_LEN = 128

D_MIX = D_MODEL
CONV_W = D_MIX // 2
DN_W = D_MIX - CONV_W
DN_HEADS = 4
DK = DN_W // DN_HEADS
DV = DK
CONV_K = 31
SC_K = 5
CHUNK = 64
D_FF = 256 * ((8 * D_MODEL // 3 + 255) // 256)
ALPHA = (2 * DEPTH) ** 0.25
DN_BETA = (8 * DEPTH) ** -0.25
LN_EPS = 1e-5
NORM_EPS = 1e-6

OFF_CV = 0
OFF_CG = OFF_CV + CONV_W
OFF_Q = OFF_CG + CONV_W
OFF_K = OFF_Q + DN_W
OFF_V = OFF_K + DN_W
OFF_Z = OFF_V + DN_W
OFF_B = OFF_Z + DN_W
OFF_A = OFF_B + 2 * DN_HEADS
D_IN = OFF_A + 2 * DN_HEADS

kernel_name = "hymba_conformer_gdn_encoder"


def layer_norm(x, g, b):
    xf = x.astype(jnp.float32)
    mu = jnp.mean(xf, -1, keepdims=True)
    var = jnp.mean(jnp.square(xf - mu), -1, keepdims=True)
    y = (xf - mu) * lax.rsqrt(var + LN_EPS) * g.astype(jnp.float32) + b.astype(jnp.float32)
    return y.astype(x.dtype)


def l2norm(t):
    return t * lax.rsqrt(jnp.sum(jnp.square(t), -1, keepdims=True) + NORM_EPS)


def depthwise_conv(x, w):
    K, C = w.shape
    pad = K // 2
    return lax.conv_general_dilated(
        x, w[:, None, :].astype(x.dtype), window_strides=(1,), padding=[(pad, pad)],
        dimension_numbers=("NWC", "WIO", "NWC"), feature_group_count=C)


def swiglu(x, wg, wu, wd):
    return (jax.nn.silu(x @ wg) * (x @ wu)) @ wd


def gated_delta_rule_chunked(q, k, v, g, beta):
    B, L, H, _ = q.shape
    N = L // CHUNK

    def chunks(t):
        t = t.reshape((B, N, CHUNK, H) + t.shape[3:])
        return jnp.moveaxis(t, 3, 1)

    q, k, v, g, beta = (chunks(t.astype(jnp.float32)) for t in (q, k, v, g, beta))
    gc = jnp.cumsum(g, axis=-1)
    incl = jnp.tril(jnp.ones((CHUNK, CHUNK), bool))
    strict = jnp.tril(jnp.ones((CHUNK, CHUNK), bool), -1)
    diff = gc[..., :, None] - gc[..., None, :]
    decay = jnp.where(incl, jnp.exp(jnp.where(incl, diff, 0.0)), 0.0)
    kb = k * beta[..., None]
    vb = v * beta[..., None]
    lower = jnp.where(strict, jnp.einsum('bhnid,bhnjd->bhnij', kb, k) * decay, 0.0)
    rhs = jnp.concatenate([vb, kb * jnp.exp(gc)[..., None]], -1)
    sol = lax.linalg.triangular_solve(lower, rhs, left_side=True, lower=True, unit_diagonal=True)
    u, w = sol[..., :DV], sol[..., DV:]
    qk = jnp.einsum('bhnid,bhnjd->bhnij', q, k) * decay
    q_dec = q * jnp.exp(gc)[..., None]
    k_tail = k * jnp.exp(gc[..., -1:] - gc)[..., None]
    g_last = jnp.exp(gc[..., -1])

    def step(S, xs):
        u_c, w_c, qk_c, qd_c, kt_c, gl_c = xs
        v_new = u_c - jnp.einsum('bhck,bhkv->bhcv', w_c, S)
        o = jnp.einsum('bhck,bhkv->bhcv', qd_c, S) + jnp.einsum('bhij,bhjv->bhiv', qk_c, v_new)
        S = S * gl_c[..., None, None] + jnp.einsum('bhck,bhcv->bhkv', kt_c, v_new)
        return S, o

    xs = tuple(jnp.moveaxis(t, 2, 0) for t in (u, w, qk, q_dec, k_tail, g_last))
    S0 = jnp.zeros((B, H, DK, DV), jnp.float32)
    _, o = lax.scan(step, S0, xs)
    return jnp.transpose(o, (1, 0, 3, 2, 4)).reshape(B, L, H, DV)


def token_mixer(h, w_in, conv_w, conv_b, conv_ln_g, conv_ln_b, sconv_w, a_log, dt_bias, o_norm_w, w_out):
    B, L, _ = h.shape
    H = DN_HEADS
    p = h @ w_in
    cv = p[..., OFF_CV:OFF_CG] * jax.nn.sigmoid(p[..., OFF_CG:OFF_Q])
    cv = depthwise_conv(cv, conv_w) + conv_b
    cv = jax.nn.silu(layer_norm(cv, conv_ln_g, conv_ln_b))
    qkv = jax.nn.silu(depthwise_conv(p[..., OFF_Q:OFF_Z], sconv_w)).astype(jnp.float32)
    q = qkv[..., :DN_W].reshape(B, L, H, DK)
    k = qkv[..., DN_W:2 * DN_W].reshape(B, L, H, DK)
    v = qkv[..., 2 * DN_W:].reshape(B, L, H, DV)
    q = l2norm(q) * (DK ** -0.5)
    k = l2norm(k)
    z = p[..., OFF_Z:OFF_B].reshape(B, L, H, DV).astype(jnp.float32)
    beta = jax.nn.sigmoid(p[..., OFF_B:OFF_A].astype(jnp.float32)).reshape(B, L, 2, H)
    a = p[..., OFF_A:D_IN].astype(jnp.float32).reshape(B, L, 2, H)
    g = -jnp.exp(a_log.astype(jnp.float32)) * jax.nn.softplus(a + dt_bias.astype(jnp.float32))
    o_fwd = gated_delta_rule_chunked(q, k, v, g[:, :, 0], beta[:, :, 0])
    flip = lambda t: jnp.flip(t, axis=1)
    o_bwd = flip(gated_delta_rule_chunked(flip(q), flip(k), flip(v), flip(g[:, :, 1]), flip(beta[:, :, 1])))
    o = o_fwd + o_bwd
    o = o * lax.rsqrt(jnp.mean(jnp.square(o), -1, keepdims=True) + NORM_EPS) * o_norm_w.astype(jnp.float32)
    o = (o * jax.nn.silu(z)).reshape(B, L, DN_W).astype(h.dtype)
    mix = jnp.concatenate([cv, o], -1)
    return mix @ w_out


def trunk(x, ffn1_wg, ffn1_wu, ffn1_wd, ln1_g, ln1_b, w_in, conv_w, conv_b, conv_ln_g, conv_ln_b,
          sconv_w, a_log, dt_bias, o_norm_w, w_out, ln2_g, ln2_b, ffn2_wg, ffn2_wu, ffn2_wd, ln3_g, ln3_b):
    for l in range(DEPTH):
        x = layer_norm(ALPHA * x + 0.5 * swiglu(x, ffn1_wg[l], ffn1_wu[l], ffn1_wd[l]), ln1_g[l], ln1_b[l])
        x = layer_norm(ALPHA * x + token_mixer(x, w_in[l], conv_w[l], conv_b[l], conv_ln_g[l], conv_ln_b[l],
                                               sconv_w[l], a_log[l], dt_bias[l], o_norm_w[l], w_out[l]),
                       ln2_g[l], ln2_b[l])
        x = layer_norm(ALPHA * x + 0.5 * swiglu(x, ffn2_wg[l], ffn2_wu[l], ffn2_wd[l]), ln3_g[l], ln3_b[l])
    return x


def setup_inputs(seed: int = 0) -> dict:
    key = jax.random.key(seed)
    ks = iter(jax.random.split(key, 40))
    nrm = lambda shape, s: jax.random.normal(next(ks), shape, jnp.float32) * s
    gain = lambda shape: 1.0 + nrm(shape, 0.02)
    x_prompt = jax.random.normal(next(ks), (BATCH, SEQ, D_MODEL), jnp.float32)
    x_sample = jax.random.normal(next(ks), (DEC_BATCH, DEC_SEQ, D_MODEL), jnp.float32)
    col_scale = jnp.ones((D_IN,), jnp.float32)
    col_scale = col_scale.at[OFF_CV:OFF_CG].set(DN_BETA).at[OFF_V:OFF_Z].set(DN_BETA)
    w_in = nrm((DEPTH, D_MODEL, D_IN), D_MODEL ** -0.5) * col_scale
    a_log = jnp.log(jax.random.uniform(next(ks), (DEPTH, 2, DN_HEADS), jnp.float32, 1.0, 16.0))
    dt = jnp.exp(jax.random.uniform(next(ks), (DEPTH, 2, DN_HEADS), jnp.float32,
                                    math.log(1e-3), math.log(1e-1)))
    dt_bias = dt + jnp.log(-jnp.expm1(-dt))
    return {
        "x_prompt": x_prompt,
        "x_sample": x_sample,
        "ffn1_wg": nrm((DEPTH, D_MODEL, D_FF), D_MODEL ** -0.5),
        "ffn1_wu": nrm((DEPTH, D_MODEL, D_FF), D_MODEL ** -0.5),
        "ffn1_wd": nrm((DEPTH, D_FF, D_MODEL), D_FF ** -0.5 * DN_BETA),
        "ln1_g": gain((DEPTH, D_MODEL)),
        "ln1_b": nrm((DEPTH, D_MODEL), 0.02),
        "w_in": w_in,
        "conv_w": nrm((DEPTH, CONV_K, CONV_W), CONV_K ** -0.5),
        "conv_b": nrm((DEPTH, CONV_W), 0.02),
        "conv_ln_g": gain((DEPTH, CONV_W)),
        "conv_ln_b": nrm((DEPTH, CONV_W), 0.02),
        "sconv_w": nrm((DEPTH, SC_K, 3 * DN_W), SC_K ** -0.5),
        "a_log": a_log,
        "dt_bias": dt_bias,
        "o_norm_w": gain((DEPTH, DV)),
        "w_out": nrm((DEPTH, D_MIX, D_MODEL), D_MIX ** -0.5 * DN_BETA),
        "ln2_g": gain((DEPTH, D_MODEL)),
        "ln2_b": nrm((DEPTH, D_MODEL), 0.02),
        "ffn2_wg": nrm((DEPTH, D_MODEL, D_FF), D_MODEL ** -0.5),
        "ffn2_wu": nrm((DEPTH, D_MODEL, D_FF), D_MODEL ** -0.5),
        "ffn2_wd": nrm((DEPTH, D_FF, D_MODEL), D_FF ** -0.5 * DN_BETA),
        "ln3_g": gain((DEPTH, D_MODEL)),
        "ln3_b": nrm((DEPTH, D_MODEL), 0.02),
    }


def reference(x_prompt, x_sample, ffn1_wg, ffn1_wu, ffn1_wd, ln1_g, ln1_b, w_in, conv_w, conv_b,
              conv_ln_g, conv_ln_b, sconv_w, a_log, dt_bias, o_norm_w, w_out, ln2_g, ln2_b,
              ffn2_wg, ffn2_wu, ffn2_wd, ln3_g, ln3_b):
    y_prompt = trunk(x_prompt, ffn1_wg, ffn1_wu, ffn1_wd, ln1_g, ln1_b, w_in, conv_w, conv_b, conv_ln_g,
                     conv_ln_b, sconv_w, a_log, dt_bias, o_norm_w, w_out, ln2_g, ln2_b,
                     ffn2_wg, ffn2_wu, ffn2_wd, ln3_g, ln3_b)
    y_sample = trunk(x_sample, ffn1_wg, ffn1_wu, ffn1_wd, ln1_g, ln1_b, w_in, conv_w, conv_b, conv_ln_g,
                     conv_ln_b, sconv_w, a_log, dt_bias, o_norm_w, w_out, ln2_g, ln2_b,
                     ffn2_wg, ffn2_wu, ffn2_wd, ln3_g, ln3_b)
    return (y_prompt, y_sample)
```

```python
import numpy as np
from contextlib import ExitStack
import concourse.bass as bass
import concourse.mybir as mybir
from concourse.bass_utils import run_bass_kernel_spmd

F32 = mybir.dt.float32
BF16 = mybir.dt.bfloat16
AF = mybir.ActivationFunctionType
ALU = mybir.AluOpType


class Buf:
    __slots__ = ("name", "ap", "w", "r", "const")

    def __init__(self, name, ap=None, const=False):
        self.name = name
        self.ap = ap
        self.w = None
        self.r = {}
        self.const = const


class DSem:
    __slots__ = ("name", "handle", "count")

    def __init__(self, name):
        self.name = name
        self.handle = None
        self.count = 0


class Rec:
    __slots__ = ("eng", "fn", "deps", "signal", "semval", "dsem")

    def __init__(self, eng, fn, dsem=None):
        self.eng = eng
        self.fn = fn
        self.deps = []
        self.signal = False
        self.semval = 0
        self.dsem = dsem


ENGS = ("pe", "dve", "act", "pool", "sp")


class Prog:
    def __init__(self, nc):
        self.nc = nc
        self.stack = ExitStack()
        self.recs = {e: [] for e in ENGS}
        self.dsems = []
        self.nbuf = 0

    def sbuf(self, name, shape, dtype, const=False):
        t = self.stack.enter_context(self.nc.sbuf_tensor(name, list(shape), dtype))
        return Buf(name, t, const)

    def psum(self, name, shape, dtype):
        t = self.stack.enter_context(self.nc.psum_tensor(name, list(shape), dtype))
        return Buf(name, t)

    def dram(self, name, shape, dtype):
        t = self.nc.dram_tensor(name, list(shape), dtype, kind="Internal").ap()
        return Buf(name, t)

    def logical(self, name, const=False):
        return Buf(name, None, const)

    def dsem(self, name):
        s = DSem(name)
        self.dsems.append(s)
        return s

    def _record(self, rec, reads, writes):
        eng = rec.eng
        deps = {}

        def add(d, raw):
            if d is None:
                return
            if d.dsem is None and d.eng == eng:
                if eng == "pe" or not raw:
                    return
            deps[id(d)] = d

        for b in reads:
            add(b.w, True)
        for b in writes:
            add(b.w, False)
            for r in b.r.values():
                add(r, False)
        rec.deps = list(deps.values())
        key = eng if rec.dsem is None else ("d", id(rec.dsem))
        for b in reads:
            if not b.const:
                b.r[key] = rec
        for b in writes:
            b.w = rec
            b.r = {}
        self.recs[eng].append(rec)
        return rec

    def op(self, eng, fn, reads=(), writes=()):
        return self._record(Rec(eng, fn), reads, writes)

    def dma(self, eng, out, in_, dsem, reads=(), writes=()):
        rec = Rec(eng, lambda e: e.dma_start(out=out, in_=in_), dsem=dsem)
        dsem.count += 16
        rec.semval = dsem.count
        return self._record(rec, reads, writes)

    def barrier(self):
        comp = []
        for e in ENGS:
            for rec in reversed(self.recs[e]):
                if rec.dsem is None and rec.fn is not None:
                    comp.append(rec)
                    break
        dfakes = []
        for s in self.dsems:
            if s.count > 0:
                r = Rec("sp", None, dsem=s)
                r.semval = s.count
                dfakes.append(r)
        for e in ENGS:
            rec = Rec(e, None)
            rec.deps = [x for x in comp if x.eng != e] + dfakes
            self.recs[e].append(rec)

    def emit(self):
        nc = self.nc
        for e in ENGS:
            for rec in self.recs[e]:
                for d in rec.deps:
                    if d.dsem is None:
                        d.signal = True
        for e in ENGS:
            c = 0
            for rec in self.recs[e]:
                if rec.dsem is None and rec.signal:
                    c += 1
                    rec.semval = c
        esem = {}
        for e in ENGS:
            esem[e] = self.stack.enter_context(nc.semaphore("es_" + e))
        for s in self.dsems:
            s.handle = self.stack.enter_context(nc.semaphore("ds_" + s.name))
        recs = self.recs
        dsems = self.dsems

        def run(engname, e):
            seen = {}
            for rec in recs[engname]:
                need = {}
                for d in rec.deps:
                    if d.dsem is None:
                        key, h = d.eng, esem[d.eng]
                    else:
                        key, h = id(d.dsem), d.dsem.handle
                    v = d.semval
                    if seen.get(key, 0) >= v:
                        continue
                    if key not in need or need[key][1] < v:
                        need[key] = (h, v)
                for key, (h, v) in need.items():
                    e.wait_ge(h, v)
                    seen[key] = v
                if rec.fn is None:
                    continue
                ins = rec.fn(e)
                if rec.dsem is not None:
                    ins.then_inc(rec.dsem.handle, 16)
                elif rec.signal:
                    ins.then_inc(esem[engname], 1)
            if engname == "sp":
                for s in dsems:
                    if s.count > 0:
                        e.wait_ge(s.handle, s.count)

        with nc.Block() as block:
            @block.sync
            def _(e):
                run("sp", e)

            @block.tensor
            def _(e):
                run("pe", e)

            @block.vector
            def _(e):
                run("dve", e)

            @block.scalar
            def _(e):
                run("act", e)

            @block.gpsimd
            def _(e):
                run("pool", e)
        self.stack.close()


D = 1024
FF = 2816
NF = 22
DIN = 3088
TT = 512
DEPTH = 2
ALPHA = float((2 * DEPTH) ** 0.25)
LN_EPS = 1e-5
NORM_EPS = 1e-6
AX = mybir.AxisListType.X
EXP1 = False

WSHAPES = [
    ("ffn1_wg", (DEPTH, D, FF)), ("ffn1_wu", (DEPTH, D, FF)), ("ffn1_wd", (DEPTH, FF, D)),
    ("ln1_g", (DEPTH, D)), ("ln1_b", (DEPTH, D)), ("w_in", (DEPTH, D, DIN)),
    ("conv_w", (DEPTH, 31, 512)), ("conv_b", (DEPTH, 512)), ("conv_ln_g", (DEPTH, 512)),
    ("conv_ln_b", (DEPTH, 512)), ("sconv_w", (DEPTH, 5, 1536)), ("a_log", (DEPTH, 2, 4)),
    ("dt_bias", (DEPTH, 2, 4)), ("o_norm_w", (DEPTH, 128)), ("w_out", (DEPTH, D, D)),
    ("ln2_g", (DEPTH, D)), ("ln2_b", (DEPTH, D)), ("ffn2_wg", (DEPTH, D, FF)),
    ("ffn2_wu", (DEPTH, D, FF)), ("ffn2_wd", (DEPTH, FF, D)), ("ln3_g", (DEPTH, D)), ("ln3_b", (DEPTH, D)),
]


def make_consts():
    p = np.arange(128)[:, None]
    i = np.arange(128)[None, :]
    c = np.zeros((20, 128, 128), np.float32)
    c[0] = (p == i)
    c[1] = 1.0
    c[2] = (p <= i)
    c[3] = (p >= i)
    c[4] = (p > i)
    c[5] = (p < i)
    for lv in range(7):
        sz = 1 << lv
        m = ((p // (2 * sz)) == (i // (2 * sz))) & ((p // sz) % 2 == 1) & ((i // sz) % 2 == 0)
        c[6 + lv] = m
        c[13 + lv] = m.T
    return c


class Arena:
    def __init__(self, P, nbytes):
        self.P = P
        self.t = P.stack.enter_context(P.nc.sbuf_tensor("arena", [128, nbytes // 4], F32))
        self.cap = nbytes
        self.off = 0

    def alloc(self, name, shape, dtype, const=False):
        n = 1
        for s in shape:
            n *= s
        esz = 4 if dtype == F32 else 2
        nb = (n * esz + 63) // 64 * 64
        assert self.off + nb <= self.cap, (name, self.off, nb, self.cap)
        ap = self.t[:, self.off // 4:(self.off + nb) // 4]
        if dtype != F32:
            ap = ap.bitcast(dtype)
        ap = ap[:, 0:n]
        if len(shape) == 2:
            ap = ap.rearrange("p (a b) -> p a b", a=shape[0])
        elif len(shape) == 3:
            ap = ap.rearrange("p (a b c) -> p a b c", a=shape[0], b=shape[1])
        elif len(shape) == 4:
            ap = ap.rearrange("p (a b c d) -> p a b c d", a=shape[0], b=shape[1], c=shape[2])
        self.off += nb
        return Buf(name, ap, const)


def build(seq_lens, depth=DEPTH, debug=False):
    nc = bass.Bass("TRN2", target_bir_lowering=False)
    NTOK = sum(seq_lens)

    def din(name, shape):
        return nc.dram_tensor(name, list(shape), F32, kind="ExternalInput").ap()

    x_in = din("x", [NTOK, D])
    y_out = nc.dram_tensor("y", [NTOK, D], F32, kind="ExternalOutput").ap()
    W = {name: din(name, shape) for name, shape in WSHAPES}
    consts_in = din("consts", [20, 128, 128])

    P = Prog(nc)
    AR = Arena(P, 204 * 1024)
    dsn = {}

    def DS(name):
        if name not in dsn:
            dsn[name] = P.dsem(name)
        return dsn[name]

    ps = [P.psum(f"ps{i}", [128, 512], F32) for i in range(8)]
    pctr = [0]

    def nps():
        b = ps[pctr[0] % 8]
        pctr[0] += 1
        return b

    rr = [0]

    def evac_copy(out_ap, in_ap, reads, writes, engs=("act", "dve")):
        e = engs[rr[0] % len(engs)]
        rr[0] += 1
        if e == "act":
            P.op("act", lambda e_: e_.activation(out=out_ap, in_=in_ap, func=AF.Copy), reads=reads, writes=writes)
        else:
            P.op(e, lambda e_: e_.tensor_copy(out=out_ap, in_=in_ap), reads=reads, writes=writes)

    def mm(pb, out_ap, lhsT, rhs, start, stop, reads):
        P.op("pe", lambda e: e.matmul(out_ap, lhsT, rhs, start=start, stop=stop), reads=reads, writes=[pb])

    def tr(pb, out_ap, in_ap, ident_ap, reads):
        P.op("pe", lambda e: e.transpose(out_ap, in_ap, ident_ap), reads=reads, writes=[pb])

    def tt(eng, out, in0, in1, op, reads, writes):
        P.op(eng, lambda e: e.tensor_tensor(out=out, in0=in0, in1=in1, op=op), reads=reads, writes=writes)

    def bc3(ap2, n, inner):
        return ap2.unsqueeze(2).to_broadcast([128, n, inner])

    def bcm(ap2, n, inner):
        return ap2.unsqueeze(1).to_broadcast([128, n, inner])

    def dscr(name, shape, dt):
        return nc.dram_tensor(name, list(shape), dt, kind=("ExternalOutput" if debug else "Internal")).ap()

    WGU = [[dscr(f"WGU{l}{k}", [NF, 128, 2, 8, 128], BF16) for k in range(2)] for l in range(depth)]
    WGUb = [[[P.logical("wgu") for _ in range(16)] for k in range(2)] for l in range(depth)]
    WD = [[dscr(f"WD{l}{k}", [2, NF, 128, 512], BF16) for k in range(2)] for l in range(depth)]
    WDb = [[[P.logical("wd") for _ in range(22)] for k in range(2)] for l in range(depth)]
    WIN = [dscr(f"WIN{l}", [20, 128, 8, 128], BF16) for l in range(depth)]
    WZ = [dscr(f"WZ{l}", [128, 8, 512], BF16) for l in range(depth)]
    WBA = [dscr(f"WBA{l}", [128, 8, 16], BF16) for l in range(depth)]
    WINb = [[P.logical("win") for _ in range(8)] for l in range(depth)]
    WO = [dscr(f"WO{l}", [128, 8, 1024], BF16) for l in range(depth)]
    WOb = [[P.logical("wo") for _ in range(8)] for l in range(depth)]
    X1 = dscr("X1", [NTOK, D], F32)
    XL = dscr("XL", [NTOK, D], F32)
    PRE = dscr("PRE", [16, 128, NTOK], BF16)
    ZS = dscr("ZS", [NTOK, 512], F32)
    CVN = dscr("CVN", [4, 128, NTOK], BF16)
    KT = dscr("KT", [4, 128, NTOK], BF16)
    QT = dscr("QT", [4, 128, NTOK], BF16)
    KTOK = dscr("KTOK", [NTOK, 512], BF16)
    VTOK = dscr("VTOK", [NTOK, 512], BF16)
    OFB = [dscr("OF", [NTOK, 512], F32), dscr("OB", [NTOK, 512], F32)]
    DBG = dscr("DBG", [2, 10, 128, 512], F32) if debug else None
    DBG4 = dscr("DBG4", [NTOK, D], F32) if debug else None
    DBG5 = dscr("DBG5", [4, 128, 2 * NTOK], BF16) if debug else None
    DBG6 = dscr("DBG6", [NTOK // TT, 128, 8, 1024], BF16) if debug else None
    DBG2 = dscr("DBG2", [NTOK, D], F32) if debug else None
    DBG3 = dscr("DBG3", [NTOK, 512], F32) if debug else None
    NT_ALL = NTOK // TT
    NB_ALL = NTOK // 128
    X1b = [P.logical("x1") for _ in range(NT_ALL)]
    XLb = [P.logical("xl") for _ in range(NT_ALL)]
    PREb = [P.logical("pre") for _ in range(NT_ALL)]
    ZSb = [P.logical("zs") for _ in range(NT_ALL)]
    CVNb = [P.logical("cvn") for _ in range(NT_ALL)]
    QKVb = [P.logical("qkv") for _ in range(NT_ALL)]
    OFBb = [[P.logical("of") for _ in range(NB_ALL)] for _ in range(2)]

    cst = AR.alloc("cst", [20, 128], F32, const=True)
    ident = cst.ap[:, 0, :]
    ones_f = cst.ap[:, 1, :]
    UT = cst.ap[:, 2, :]
    LT = cst.ap[:, 3, :]
    SL = cst.ap[:, 4, :]
    SU = cst.ap[:, 5, :]
    P.dma("sp", cst.ap, consts_in.rearrange("c p j -> p c j"), DS("cst"), writes=[cst])
    epsb = AR.alloc("epsb", [1, 8], F32)
    EPS_FFN, EPS_LN, EPS_NORM, ONE = 0, 1, 2, 3
    for col, val in ((EPS_FFN, 4 * LN_EPS), (EPS_LN, LN_EPS), (EPS_NORM, NORM_EPS), (ONE, 1.0)):
        P.op("pool", lambda e, col=col, val=val: e.memset(epsb.ap[:, 0, col:col + 1], val), writes=[epsb])

    def epsap(col):
        return epsb.ap[:, 0, col:col + 1]

    NBMAX = max(seq_lens) // 128
    BETA_all = AR.alloc("beta_all", [NBMAX, 8], F32)
    G_all = AR.alloc("g_all", [NBMAX, 8], F32)
    lnp = [AR.alloc(f"lnp{i}", [2, D], F32) for i in range(3)]
    cw = AR.alloc("cw", [4, 34], F32)
    scw = AR.alloc("scw", [12, 5], F32)
    onw = AR.alloc("onw", [1, 128], F32)
    nA = AR.alloc("nA", [1, 8], F32)
    dtb = AR.alloc("dtb", [1, 8], F32)
    BASE = AR.off

    def stage0():
        AR.off = BASE
        sf = [AR.alloc(f"stgf{i}", [1, DIN], F32) for i in range(2)]
        sb = [AR.alloc(f"stgb{i}", [1, DIN], BF16) for i in range(2)]
        cnt = [0]
        cengs = ("act", "dve", "pool")

        def conv(src_ap, ncols, stores, view=None):
            i = cnt[0] % 2
            f, b = sf[i], sb[i]
            fa = f.ap[:, 0, 0:ncols]
            ba = b.ap[:, 0, 0:ncols]
            dst_f = fa if view is None else view(fa)
            P.dma("sp", dst_f, src_ap, DS(f"stgf{i}"), writes=[f])
            ce = cengs[cnt[0] % 3]
            if ce == "act":
                P.op("act", lambda e: e.activation(out=ba, in_=fa, func=AF.Copy), reads=[f], writes=[b])
            else:
                P.op(ce, lambda e: e.tensor_copy(out=ba, in_=fa), reads=[f], writes=[b])
            for (dst, srcfn, lb) in stores:
                P.dma("act", dst, srcfn(ba), DS(f"stgb{i}"), reads=[b], writes=[lb])
            cnt[0] += 1

        for l in range(depth):
            for k, pre in enumerate(("ffn1", "ffn2")):
                for a, nm in enumerate(("_wg", "_wu")):
                    for kc in range(8):
                        conv(W[pre + nm][l, kc * 128:(kc + 1) * 128, :], FF,
                             [(WGU[l][k][:, :, a, kc, :].rearrange("f p j -> p f j"),
                               lambda ba: ba.rearrange("p (f j) -> p f j", j=128), WGUb[l][k][a * 8 + kc])])
                for f0 in range(0, NF, 2):
                    conv(W[pre + "_wd"][l, f0 * 128:(f0 + 2) * 128, :].rearrange("(f p) j -> p f j", p=128), 2048,
                         [(WD[l][k][h, f0:f0 + 2].rearrange("f p j -> p f j"),
                           (lambda ba, h=h: ba.rearrange("p (f j) -> p f j", f=2)[:, :, h * 512:(h + 1) * 512]),
                           WDb[l][k][h * 11 + f0 // 2]) for h in range(2)],
                         view=lambda fa: fa.rearrange("p (f j) -> p f j", f=2))
            for kc in range(8):
                conv(W["w_in"][l, kc * 128:(kc + 1) * 128, :], DIN,
                     [(WIN[l][:, :, kc, :].rearrange("c p j -> p c j"),
                       lambda ba: ba[:, 0:2560].rearrange("p (c j) -> p c j", j=128), WINb[l][kc]),
                      (WZ[l][:, kc, :], lambda ba: ba[:, 2560:3072], P.logical("wz")),
                      (WBA[l][:, kc, :], lambda ba: ba[:, 3072:3088], P.logical("wba"))])
            for c in range(8):
                conv(W["w_out"][l, c * 128:(c + 1) * 128, :], D,
                     [(WO[l][:, c, :], lambda ba: ba, WOb[l][c])])

    def load_layer_params(l):
        P.barrier()
        AR.off = BASE
        craw = AR.alloc("craw", [1, 512], F32)
        sraw = AR.alloc("sraw", [1, 1536], F32)
        for i, nm in enumerate(("ln1", "ln2", "ln3")):
            P.dma("sp", lnp[i].ap[:, 0, :], W[nm + "_g"][l].partition_broadcast(128), DS(f"lnp{i}"), writes=[lnp[i]])
            P.dma("sp", lnp[i].ap[:, 1, :], W[nm + "_b"][l].partition_broadcast(128), DS(f"lnp{i}"), writes=[lnp[i]])
        P.dma("sp", craw.ap[0:31, 0, :], W["conv_w"][l], DS("craw"), writes=[craw])
        P.dma("sp", craw.ap[31:32, 0, :], W["conv_b"][l:l + 1, :], DS("craw"), writes=[craw])
        P.dma("sp", craw.ap[32:33, 0, :], W["conv_ln_g"][l:l + 1, :], DS("craw"), writes=[craw])
        P.dma("sp", craw.ap[33:34, 0, :], W["conv_ln_b"][l:l + 1, :], DS("craw"), writes=[craw])
        P.dma("sp", sraw.ap[0:5, 0, :], W["sconv_w"][l], DS("sraw"), writes=[sraw])
        P.dma("sp", onw.ap[:, 0, :], W["o_norm_w"][l].partition_broadcast(128), DS("onw"), writes=[onw])
        P.dma("sp", nA.ap[:, 0, :], W["a_log"][l].rearrange("a h -> (a h)").partition_broadcast(128), DS("nA"), writes=[nA])
        P.dma("sp", dtb.ap[:, 0, :], W["dt_bias"][l].rearrange("a h -> (a h)").partition_broadcast(128), DS("dtb"), writes=[dtb])
        P.op("act", lambda e: e.activation(out=nA.ap[:, 0, :], in_=nA.ap[:, 0, :], func=AF.Exp), reads=[nA], writes=[nA])
        P.op("dve", lambda e: e.tensor_scalar(out=nA.ap[:, 0, :], in0=nA.ap[:, 0, :], scalar1=-1.0, scalar2=None, op0=ALU.mult),
             reads=[nA], writes=[nA])
        for c in range(4):
            pb = nps()
            tr(pb, pb.ap[:, 0:34], craw.ap[0:34, 0, c * 128:(c + 1) * 128], cst.ap[0:34, 0, 0:34], [craw, cst])
            P.op("dve", lambda e, c=c, pb=pb: e.tensor_copy(out=cw.ap[:, c, :], in_=pb.ap[:, 0:34]), writes=[pb, cw])
        for c in range(12):
            pb = nps()
            tr(pb, pb.ap[:, 0:5], sraw.ap[0:5, 0, c * 128:(c + 1) * 128], cst.ap[0:5, 0, 0:5], [sraw, cst])
            P.op("dve", lambda e, c=c, pb=pb: e.tensor_copy(out=scw.ap[:, c, :], in_=pb.ap[:, 0:5]), writes=[pb, scw])
        P.barrier()

    def make_xT(xr, xT):
        for kc in range(8):
            pb = nps()
            for s in range(4):
                tr(pb, pb.ap[:, s * 128:(s + 1) * 128], xr.ap[:, s, kc * 128:(kc + 1) * 128], ident, [xr, cst])
            evac_copy(xT.ap[:, kc, :], pb.ap[:, :], [], [pb, xT])

    def ln_apply(xr, lp, epscol, tmp):
        st, mv, rs = tmp
        for s in range(4):
            for c in range(2):
                P.op("dve", lambda e, s=s, c=c: e.bn_stats(out=st.ap[:, s, c, :], in_=xr.ap[:, s, c * 512:(c + 1) * 512]),
                     reads=[xr], writes=[st])
            P.op("dve", lambda e, s=s: e.bn_aggr(out=mv.ap[:, s, :], in_=st.ap[:, s, :, :].rearrange("p c k -> p (c k)")),
                 reads=[st], writes=[mv])
        P.op("act", lambda e: e.activation(out=rs.ap[:, 0, :], in_=mv.ap[:, :, 1], func=AF.Sqrt, bias=epsap(epscol), scale=1.0),
             reads=[mv, epsb], writes=[rs])
        P.op("dve", lambda e: e.reciprocal(out=rs.ap[:, 0, :], in_=rs.ap[:, 0, :]), reads=[rs], writes=[rs])
        for s in range(4):
            P.op("dve", lambda e, s=s: e.tensor_scalar(out=xr.ap[:, s, :], in0=xr.ap[:, s, :], scalar1=mv.ap[:, s, 0:1],
                                                       scalar2=rs.ap[:, 0, s:s + 1], op0=ALU.subtract, op1=ALU.mult),
                 reads=[xr, mv, rs], writes=[xr])
        tt("pool", xr.ap, xr.ap, bcm(lp.ap[:, 0, :], 4, D), ALU.mult, [xr, lp], [xr])
        tt("pool", xr.ap, xr.ap, bcm(lp.ap[:, 1, :], 4, D), ALU.add, [xr, lp], [xr])

    def ffn_ln(l, k, xr, lp, fb):
        xT, hT, wgu, wdb, sg, lntmp = fb
        make_xT(xr, xT)

        def load_wgu(g):
            b = wgu[g % 2]
            P.dma("sp", b.ap.rearrange("p f a k j -> p f (a k j)"),
                  WGU[l][k][2 * g:2 * g + 2].rearrange("f p a k j -> p f (a k j)"), DS(f"wgu{g % 2}"),
                  reads=WGUb[l][k], writes=[b])

        load_wgu(0)
        for g in range(11):
            if g + 1 < 11:
                load_wgu(g + 1)
            b = wgu[g % 2]
            for f in range(2):
                ffc = 2 * g + f
                pg, pu = nps(), nps()
                for kc in range(8):
                    mm(pg, pg.ap[:, :], b.ap[:, f, 0, kc, :], xT.ap[:, kc, :], kc == 0, kc == 7, [b, xT])
                for kc in range(8):
                    mm(pu, pu.ap[:, :], b.ap[:, f, 1, kc, :], xT.ap[:, kc, :], kc == 0, kc == 7, [b, xT])
                s_ = sg[ffc % 2]
                P.op("act", lambda e, s_=s_, pg=pg: e.activation(out=s_.ap[:, 0, :], in_=pg.ap[:, :], func=AF.Silu),
                     writes=[pg, s_])
                tt("dve", hT.ap[:, ffc, :], pu.ap[:, :], s_.ap[:, 0, :], ALU.mult, [s_], [pu, hT])
        groups = [(0, 4), (4, 4), (8, 4), (12, 4), (16, 4), (20, 2)]
        seqs = [(h, gi) for h in range(2) for gi in range(len(groups))]

        def load_wd(idx):
            h, gi = seqs[idx]
            f0, n = groups[gi]
            b = wdb[idx % 2]
            P.dma("sp", b.ap[:, 0:n, :], WD[l][k][h, f0:f0 + n].rearrange("f p j -> p f j"), DS(f"wd{idx % 2}"),
                  reads=WDb[l][k], writes=[b])

        load_wd(0)
        idx = 0
        for h in range(2):
            py = [nps() for _ in range(4)]
            for gi, (f0, n) in enumerate(groups):
                if idx + 1 < len(seqs):
                    load_wd(idx + 1)
                b = wdb[idx % 2]
                for f in range(n):
                    ffc = f0 + f
                    for s in range(4):
                        mm(py[s], py[s].ap[:, :], hT.ap[:, ffc, s * 128:(s + 1) * 128], b.ap[:, f, :],
                           ffc == 0, ffc == NF - 1, [hT, b])
                idx += 1
            for s in range(4):
                P.op("dve", lambda e, s=s, h=h, p_=py[s]: e.scalar_tensor_tensor(
                    out=xr.ap[:, s, h * 512:(h + 1) * 512], in0=xr.ap[:, s, h * 512:(h + 1) * 512], scalar=2.0 * ALPHA,
                    in1=p_.ap[:, :], op0=ALU.mult, op1=ALU.add), reads=[xr], writes=[xr, py[s]])
        ln_apply(xr, lp, EPS_FFN, lntmp)

    def alloc_ffn():
        xT = AR.alloc("xT", [8, TT], BF16)
        hT = AR.alloc("hT", [NF, TT], BF16)
        wgu = [AR.alloc(f"wgu{i}", [2, 2, 8, 128], BF16) for i in range(2)]
        wdb = [AR.alloc(f"wdb{i}", [4, 512], BF16) for i in range(2)]
        sg = [AR.alloc(f"sg{i}", [1, TT], F32) for i in range(2)]
        st = AR.alloc("lnst", [4, 2, 6], F32)
        mv = AR.alloc("lnmv", [4, 2], F32)
        rs = AR.alloc("lnrs", [1, 4], F32)
        return (xT, hT, wgu, wdb, sg, (st, mv, rs))

    def stageA(l, sq0, L):
        P.barrier()
        AR.off = BASE
        fb = alloc_ffn()
        xT, wgu = fb[0], fb[2]
        xres = [AR.alloc(f"xres{i}", [4, D], F32) for i in range(2)]
        pre = AR.alloc("pre", [16, TT], BF16)
        zs = AR.alloc("zs", [4, TT], F32)
        sgm = AR.alloc("sgm", [4, TT], F32)
        wz = AR.alloc("wz", [8, 512], BF16)
        wba = AR.alloc("wba", [8, 16], BF16)
        t8 = AR.alloc("t8", [4, 8], F32)
        src = x_in if l == 0 else XL
        srcb = None if l == 0 else XLb
        P.dma("sp", wz.ap, WZ[l], DS("wz"), reads=WINb[l], writes=[wz])
        P.dma("sp", wba.ap, WBA[l], DS("wba"), reads=WINb[l], writes=[wba])
        nT = L // TT
        for t in range(nT):
            tok0 = sq0 + t * TT
            gt = tok0 // TT
            xr = xres[t % 2]
            P.dma("sp", xr.ap, src[tok0:tok0 + TT, :].rearrange("(s p) d -> p s d", p=128), DS(f"xres{t % 2}"),
                  reads=([] if srcb is None else [srcb[gt]]), writes=[xr])
            ffn_ln(l, 0, xr, lnp[0], fb)
            P.dma("act", X1[tok0:tok0 + TT, :].rearrange("(s p) d -> p s d", p=128), xr.ap, DS(f"xst{t % 2}"),
                  reads=[xr], writes=[X1b[gt]])
            make_xT(xr, xT)
            gorder = [1, 0, 2, 3, 4]

            def load_win(i):
                b = wgu[i % 2]
                c0 = gorder[i] * 4
                P.dma("sp", b.ap.rearrange("p f a k j -> p (f a) (k j)"),
                      WIN[l][c0:c0 + 4].rearrange("c p k j -> p c (k j)"), DS(f"wgu{i % 2}"),
                      reads=WINb[l], writes=[b])

            load_win(0)
            for i in range(5):
                if i + 1 < 5:
                    load_win(i + 1)
                b = wgu[i % 2]
                bv = b.ap.rearrange("p f a k j -> p (f a) k j")
                for cc in range(4):
                    c = gorder[i] * 4 + cc
                    pb = nps()
                    for kc in range(8):
                        mm(pb, pb.ap[:, :], bv[:, cc, kc, :], xT.ap[:, kc, :], kc == 0, kc == 7, [b, xT])
                    if gorder[i] == 1:
                        P.op("act", lambda e, cc=cc, pb=pb: e.activation(out=sgm.ap[:, cc, :], in_=pb.ap[:, :], func=AF.Sigmoid),
                             writes=[pb, sgm])
                    elif gorder[i] == 0:
                        tt("dve", pre.ap[:, cc, :], pb.ap[:, :], sgm.ap[:, cc, :], ALU.mult, [sgm], [pb, pre])
                    else:
                        evac_copy(pre.ap[:, c - 4, :], pb.ap[:, :], [], [pb, pre])
            P.dma("act", PRE[:, :, tok0:tok0 + TT].rearrange("c p n -> p c n"), pre.ap, DS("prest"),
                  reads=[pre], writes=[PREb[gt]])
            for s in range(4):
                pb = nps()
                for kc in range(8):
                    mm(pb, pb.ap[:, :], xT.ap[:, kc, s * 128:(s + 1) * 128], wz.ap[:, kc, :], kc == 0, kc == 7, [xT, wz])
                P.op("act", lambda e, s=s, pb=pb: e.activation(out=zs.ap[:, s, :], in_=pb.ap[:, :], func=AF.Silu),
                     writes=[pb, zs])
            P.dma("act", ZS[tok0:tok0 + TT, :].rearrange("(s p) n -> p s n", p=128), zs.ap, DS("zsst"),
                  reads=[zs], writes=[ZSb[gt]])
            pb = nps()
            for s in range(4):
                for kc in range(8):
                    mm(pb, pb.ap[:, s * 16:(s + 1) * 16], xT.ap[:, kc, s * 128:(s + 1) * 128], wba.ap[:, kc, :],
                       kc == 0, kc == 7, [xT, wba])
            pv = pb.ap[:, 0:64].rearrange("p (s c) -> p s c", s=4)
            b0 = (t * TT) // 128
            P.op("act", lambda e, pv=pv, b0=b0: e.activation(out=BETA_all.ap[:, b0:b0 + 4, :], in_=pv[:, :, 0:8], func=AF.Sigmoid),
                 writes=[pb, BETA_all])
            tt("dve", t8.ap, pv[:, :, 8:16], bcm(dtb.ap[:, 0, :], 4, 8), ALU.add, [dtb], [pb, t8])
            P.op("act", lambda e: e.activation(out=t8.ap, in_=t8.ap, func=AF.Exp), reads=[t8], writes=[t8])
            P.op("act", lambda e: e.activation(out=t8.ap, in_=t8.ap, func=AF.Ln, bias=epsap(ONE), scale=1.0),
                 reads=[t8, epsb], writes=[t8])
            tt("dve", G_all.ap[:, b0:b0 + 4, :], t8.ap, bcm(nA.ap[:, 0, :], 4, 8), ALU.mult, [t8, nA], [G_all])

    def stageB(l, sq0, L):
        P.barrier()
        AR.off = BASE
        wc = AR.alloc("wc", [4, TT + 30], BF16)
        wq = AR.alloc("wq", [12, TT + 4], BF16)
        cacc = AR.alloc("cacc", [4, TT], F32)
        csq = AR.alloc("csq", [4, TT], F32)
        mean = AR.alloc("mean", [1, TT], F32)
        var = AR.alloc("var", [1, TT], F32)
        cvn = AR.alloc("cvn", [4, TT], BF16)
        xf = [AR.alloc(f"xf{i}", [1, TT], F32) for i in range(2)]
        sqt = AR.alloc("sqt", [1, TT], F32)
        rn = AR.alloc("rn", [1, TT], F32)
        xn = AR.alloc("xn", [1, TT], F32)
        qTo = AR.alloc("qTo", [4, TT], BF16)
        kTo = AR.alloc("kTo", [4, TT], BF16)
        ktok = AR.alloc("ktok", [4, 512], BF16)
        vtok = AR.alloc("vtok", [4, 512], BF16)
        nT = L // TT
        sq1 = sq0 + L
        for t in range(nT):
            tok0 = sq0 + t * TT
            gt = tok0 // TT
            nb = [PREb[g] for g in (gt - 1, gt, gt + 1) if sq0 // TT <= g < sq1 // TT]
            for (w, c0, c1, halo) in ((wc, 0, 4, 15), (wq, 4, 16, 2)):
                lo, hi = max(tok0 - halo, sq0), min(tok0 + TT + halo, sq1)
                a = lo - (tok0 - halo)
                if a > 0:
                    P.op("pool", lambda e, w=w, a=a: e.memset(w.ap[:, :, 0:a], 0.0), writes=[w])
                if hi < tok0 + TT + halo:
                    P.op("pool", lambda e, w=w, hi=hi, lo=lo, a=a, halo=halo: e.memset(w.ap[:, :, a + hi - lo:TT + 2 * halo], 0.0), writes=[w])
                P.dma("sp", w.ap[:, :, a:a + hi - lo], PRE[c0:c1, :, lo:hi].rearrange("c p n -> p c n"), DS("w" + str(c0)),
                      reads=nb, writes=[w])
            for c in range(4):
                P.op("dve", lambda e, c=c: e.tensor_scalar(out=cacc.ap[:, c, :], in0=wc.ap[:, c, 0:TT], scalar1=cw.ap[:, c, 0:1],
                                                           scalar2=cw.ap[:, c, 31:32], op0=ALU.mult, op1=ALU.add),
                     reads=[wc, cw], writes=[cacc])
                for j in range(1, 31):
                    P.op("dve", lambda e, c=c, j=j: e.scalar_tensor_tensor(out=cacc.ap[:, c, :], in0=wc.ap[:, c, j:j + TT],
                                                                            scalar=cw.ap[:, c, j:j + 1], in1=cacc.ap[:, c, :],
                                                                            op0=ALU.mult, op1=ALU.add),
                         reads=[wc, cw, cacc], writes=[cacc])
            tt("pool", csq.ap, cacc.ap, cacc.ap, ALU.mult, [cacc], [csq])
            p1, p2 = nps(), nps()
            for c in range(4):
                mm(p1, p1.ap[:, :], ones_f, cacc.ap[:, c, :], c == 0, c == 3, [cst, cacc])
            for c in range(4):
                mm(p2, p2.ap[:, :], ones_f, csq.ap[:, c, :], c == 0, c == 3, [cst, csq])
            P.op("act", lambda e, p1=p1: e.activation(out=mean.ap[:, 0, :], in_=p1.ap[:, :], func=AF.Copy, scale=1.0 / 512),
                 writes=[p1, mean])
            tt("pool", var.ap[:, 0, :], mean.ap[:, 0, :], mean.ap[:, 0, :], ALU.mult, [mean], [var])
            P.op("dve", lambda e, p2=p2: e.scalar_tensor_tensor(out=var.ap[:, 0, :], in0=p2.ap[:, :], scalar=1.0 / 512,
                                                               in1=var.ap[:, 0, :], op0=ALU.mult, op1=ALU.subtract),
                 reads=[var], writes=[var, p2])
            P.op("act", lambda e: e.activation(out=var.ap[:, 0, :], in_=var.ap[:, 0, :], func=AF.Sqrt, bias=epsap(EPS_LN), scale=1.0),
                 reads=[var, epsb], writes=[var])
            P.op("dve", lambda e: e.reciprocal(out=var.ap[:, 0, :], in_=var.ap[:, 0, :]), reads=[var], writes=[var])
            tt("dve", cacc.ap, cacc.ap, bcm(mean.ap[:, 0, :], 4, TT), ALU.subtract, [cacc, mean], [cacc])
            tt("pool", cacc.ap, cacc.ap, bcm(var.ap[:, 0, :], 4, TT), ALU.mult, [cacc, var], [cacc])
            for c in range(4):
                P.op("act", lambda e, c=c: e.activation(out=cvn.ap[:, c, :], in_=cacc.ap[:, c, :], func=AF.Silu,
                                                        scale=cw.ap[:, c, 32:33], bias=cw.ap[:, c, 33:34]),
                     reads=[cacc, cw], writes=[cvn])
            P.dma("act", CVN[:, :, tok0:tok0 + TT].rearrange("c p n -> p c n"), cvn.ap, DS("cvnst"),
                  reads=[cvn], writes=[CVNb[gt]])
            for i in range(12):
                x_ = xf[i % 2]
                xa = x_.ap[:, 0, :]
                P.op("dve", lambda e, i=i, xa=xa: e.tensor_scalar(out=xa, in0=wq.ap[:, i, 0:TT], scalar1=scw.ap[:, i, 0:1],
                                                                  scalar2=None, op0=ALU.mult), reads=[wq, scw], writes=[x_])
                for j in range(1, 5):
                    P.op("dve", lambda e, i=i, j=j, xa=xa: e.scalar_tensor_tensor(out=xa, in0=wq.ap[:, i, j:j + TT],
                                                                                  scalar=scw.ap[:, i, j:j + 1], in1=xa,
                                                                                  op0=ALU.mult, op1=ALU.add),
                         reads=[wq, scw, x_], writes=[x_])
                P.op("act", lambda e, xa=xa: e.activation(out=xa, in_=xa, func=AF.Silu), reads=[x_], writes=[x_])
                h = i % 4
                if i >= 8:
                    pb = nps()
                    for s in range(4):
                        tr(pb, pb.ap[:, s * 128:(s + 1) * 128], x_.ap[:, 0, s * 128:(s + 1) * 128], ident, [x_, cst])
                    evac_copy(vtok.ap[:, :, h * 128:(h + 1) * 128], pb.ap[:, :].rearrange("p (s c) -> p s c", s=4), [], [pb, vtok])
                    continue
                tt("pool", sqt.ap[:, 0, :], xa, xa, ALU.mult, [x_], [sqt])
                pb = nps()
                mm(pb, pb.ap[:, :], ones_f, sqt.ap[:, 0, :], True, True, [cst, sqt])
                P.op("act", lambda e, pb=pb: e.activation(out=rn.ap[:, 0, :], in_=pb.ap[:, :], func=AF.Sqrt, bias=epsap(EPS_NORM), scale=1.0),
                     reads=[epsb], writes=[pb, rn])
                P.op("dve", lambda e: e.reciprocal(out=rn.ap[:, 0, :], in_=rn.ap[:, 0, :]), reads=[rn], writes=[rn])
                if i < 4:
                    P.op("dve", lambda e, h=h, xa=xa: e.scalar_tensor_tensor(out=qTo.ap[:, h, :], in0=xa, scalar=float(128 ** -0.5),
                                                                             in1=rn.ap[:, 0, :], op0=ALU.mult, op1=ALU.mult),
                         reads=[x_, rn], writes=[qTo])
                else:
                    tt("dve", xn.ap[:, 0, :], xa, rn.ap[:, 0, :], ALU.mult, [x_, rn], [xn])
                    P.op("act", lambda e, h=h: e.activation(out=kTo.ap[:, h, :], in_=xn.ap[:, 0, :], func=AF.Copy), reads=[xn], writes=[kTo])
                    pb = nps()
                    for s in range(4):
                        tr(pb, pb.ap[:, s * 128:(s + 1) * 128], xn.ap[:, 0, s * 128:(s + 1) * 128], ident, [xn, cst])
                    evac_copy(ktok.ap[:, :, h * 128:(h + 1) * 128], pb.ap[:, :].rearrange("p (s c) -> p s c", s=4), [], [pb, ktok])
            lb = QKVb[gt]
            P.dma("act", QT[:, :, tok0:tok0 + TT].rearrange("h p n -> p h n"), qTo.ap, DS("qst"), reads=[qTo], writes=[P.logical("x")])
            P.dma("act", KT[:, :, tok0:tok0 + TT].rearrange("h p n -> p h n"), kTo.ap, DS("kst"), reads=[kTo], writes=[P.logical("x")])
            P.dma("act", KTOK[tok0:tok0 + TT, :].rearrange("(s p) n -> p s n", p=128), ktok.ap, DS("ktst"), reads=[ktok], writes=[P.logical("x")])
            P.dma("act", VTOK[tok0:tok0 + TT, :].rearrange("(s p) n -> p s n", p=128), vtok.ap, DS("vtst"), reads=[vtok], writes=[lb])

    def stageC(l, sq0, L):
        P.barrier()
        AR.off = BASE
        nB = L // 128
        H4 = [4, 128]
        S = [AR.alloc(f"S{d}", H4, F32) for d in range(2)]
        Sbf = [AR.alloc(f"Sbf{d}", H4, BF16) for d in range(2)]
        for d in range(2):
            P.op("pool", lambda e, d=d: e.memset(S[d].ap, 0.0), writes=[S[d]])
            P.op("pool", lambda e, d=d: e.memset(Sbf[d].ap, 0.0), writes=[Sbf[d]])
        opnd = [[{nm: AR.alloc(f"{nm}{d}{i}", H4, BF16) for nm in ("kT", "qT", "kt", "vt")} for i in range(2)] for d in range(2)]
        T = [{} for _ in range(2)]
        for d in range(2):
            for nm in ("G2", "Dm", "DT", "KKm", "M0", "N0", "Mo", "No", "Y1", "Y2", "T0", "T1", "TT0", "TT1", "u", "tmp"):
                T[d][nm] = AR.alloc(f"{nm}{d}", H4, F32)
            for nm in ("qkdT", "TTb", "vb", "rw", "ktl", "wT", "vnew"):
                T[d][nm] = AR.alloc(f"{nm}{d}", H4, BF16)
            T[d]["o"] = [AR.alloc(f"o{d}{i}", H4, F32) for i in range(2)]
            T[d]["sc"] = AR.alloc(f"sc{d}", [1, 8], F32)
            for nm in ("egc", "etl", "gl", "nb", "be"):
                T[d][nm] = AR.alloc(f"{nm}{d}", [1, 4], F32)

        def load_ops(i):
            for d in range(2):
                blk = i if d == 0 else nB - 1 - i
                tb = sq0 + blk * 128
                o = opnd[d][i % 2]
                dsf = lambda nm: DS(f"op{nm}{d}{i % 2}")
                rb = [QKVb[tb // TT]]
                P.dma("sp", o["kT"].ap, KT[:, :, tb:tb + 128].rearrange("h p n -> p h n"), dsf("kT"), reads=rb, writes=[o["kT"]])
                P.dma("sp", o["qT"].ap, QT[:, :, tb:tb + 128].rearrange("h p n -> p h n"), dsf("qT"), reads=rb, writes=[o["qT"]])
                P.dma("sp", o["kt"].ap, KTOK[tb:tb + 128, :].rearrange("p (h c) -> p h c", h=4), dsf("kt"), reads=rb, writes=[o["kt"]])
                P.dma("sp", o["vt"].ap, VTOK[tb:tb + 128, :].rearrange("p (h c) -> p h c", h=4), dsf("vt"), reads=rb, writes=[o["vt"]])

        def both(fn):
            for d in range(2):
                fn(d)

        load_ops(0)
        for i in range(nB):
            if i + 1 < nB:
                load_ops(i + 1)
            blks = [i, nB - 1 - i]
            O = [opnd[d][i % 2] for d in range(2)]
            tri = [UT, LT]
            smask = [SL, SU]

            def scal(d):
                t = T[d]
                g = G_all.ap[:, blks[d], d * 4:(d + 1) * 4]
                beta = BETA_all.ap[:, blks[d], d * 4:(d + 1) * 4]
                pb = nps()
                mm(pb, pb.ap[:, 0:4], tri[d], g, True, True, [cst, G_all])
                mm(pb, pb.ap[:, 4:8], ones_f, g, True, True, [cst, G_all])
                sc = t["sc"]
                P.op("dve", lambda e: e.tensor_copy(out=sc.ap[:, 0, :], in_=pb.ap[:, 0:8]), writes=[pb, sc])
                P.op("act", lambda e: e.activation(out=t["egc"].ap[:, 0, :], in_=sc.ap[:, 0, 0:4], func=AF.Exp), reads=[sc], writes=[t["egc"]])
                tt("dve", t["etl"].ap[:, 0, :], sc.ap[:, 0, 4:8], sc.ap[:, 0, 0:4], ALU.subtract, [sc], [t["etl"]])
                P.op("act", lambda e: e.activation(out=t["etl"].ap[:, 0, :], in_=t["etl"].ap[:, 0, :], func=AF.Exp), reads=[t["etl"]], writes=[t["etl"]])
                P.op("act", lambda e: e.activation(out=t["gl"].ap[:, 0, :], in_=sc.ap[:, 0, 4:8], func=AF.Exp), reads=[sc], writes=[t["gl"]])
                P.op("dve", lambda e: e.tensor_scalar(out=t["nb"].ap[:, 0, :], in0=beta, scalar1=-1.0, scalar2=None, op0=ALU.mult),
                     reads=[BETA_all], writes=[t["nb"]])
                tt("dve", t["be"].ap[:, 0, :], beta, t["egc"].ap[:, 0, :], ALU.mult, [BETA_all, t["egc"]], [t["be"]])
                tt("pool", t["G2"].ap, bcm(smask[d], 4, 128), bc3(g, 4, 128), ALU.mult, [cst, G_all], [t["G2"]])
                tt("pool", t["vb"].ap, O[d]["vt"].ap, bc3(beta, 4, 128), ALU.mult, [O[d]["vt"], BETA_all], [t["vb"]])
            both(scal)

            def decay(d):
                t = T[d]
                pD, pDT, pK, pQ = nps(), nps(), nps(), nps()
                for h in range(4):
                    mm(pD, pD.ap[:, h * 128:(h + 1) * 128], tri[d], t["G2"].ap[:, h, :], True, True, [cst, t["G2"]])
                for h in range(4):
                    mm(pDT, pDT.ap[:, h * 128:(h + 1) * 128], t["G2"].ap[:, h, :], tri[d], True, True, [cst, t["G2"]])
                for h in range(4):
                    mm(pK, pK.ap[:, h * 128:(h + 1) * 128], O[d]["kT"].ap[:, h, :], O[d]["kT"].ap[:, h, :], True, True, [O[d]["kT"]])
                for h in range(4):
                    mm(pQ, pQ.ap[:, h * 128:(h + 1) * 128], O[d]["kT"].ap[:, h, :], O[d]["qT"].ap[:, h, :], True, True, [O[d]["kT"], O[d]["qT"]])
                v4 = lambda p_: p_.ap[:, :].rearrange("p (h c) -> p h c", h=4)
                P.op("act", lambda e: e.activation(out=t["Dm"].ap, in_=v4(pD), func=AF.Exp), writes=[pD, t["Dm"]])
                P.op("act", lambda e: e.activation(out=t["DT"].ap, in_=v4(pDT), func=AF.Exp), writes=[pDT, t["DT"]])
                tt("pool", t["DT"].ap, t["DT"].ap, bcm(tri[d], 4, 128), ALU.mult, [t["DT"], cst], [t["DT"]])
                tt("dve", t["KKm"].ap, v4(pK), bcm(smask[d], 4, 128), ALU.mult, [cst], [pK, t["KKm"]])
                tt("pool", t["KKm"].ap, t["KKm"].ap, t["Dm"].ap, ALU.mult, [t["KKm"], t["Dm"]], [t["KKm"]])
                tt("pool", t["M0"].ap, t["KKm"].ap, bc3(t["nb"].ap[:, 0, :], 4, 128), ALU.mult, [t["KKm"], t["nb"]], [t["M0"]])
                tt("dve", t["qkdT"].ap, v4(pQ), t["DT"].ap, ALU.mult, [t["DT"]], [pQ, t["qkdT"]])
                tt("pool", t["rw"].ap, O[d]["kt"].ap, bc3(t["be"].ap[:, 0, :], 4, 128), ALU.mult, [O[d]["kt"], t["be"]], [t["rw"]])
                tt("pool", t["ktl"].ap, O[d]["kt"].ap, bc3(t["etl"].ap[:, 0, :], 4, 128), ALU.mult, [O[d]["kt"], t["etl"]], [t["ktl"]])
                pN = nps()
                for h in range(4):
                    tr(pN, pN.ap[:, h * 128:(h + 1) * 128], t["M0"].ap[:, h, :], ident, [t["M0"], cst])
                P.op("act", lambda e: e.activation(out=t["N0"].ap, in_=v4(pN), func=AF.Copy), writes=[pN, t["N0"]])
            both(decay)

            cur = 0
            for lv in range(7):
                nxt = 1 - cur
                for d in range(2):
                    t = T[d]
                    v4 = lambda p_: p_.ap[:, :].rearrange("p (h c) -> p h c", h=4)
                    mk = cst.ap[:, 6 + lv, :] if d == 0 else cst.ap[:, 13 + lv, :]
                    mkT = cst.ap[:, 13 + lv, :] if d == 0 else cst.ap[:, 6 + lv, :]
                    Tc, TTc = t[f"T{cur}"], t[f"TT{cur}"]
                    Tn, TTn = t[f"T{nxt}"], t[f"TT{nxt}"]
                    tt("pool", t["Mo"].ap, t["M0"].ap, bcm(mk, 4, 128), ALU.mult, [t["M0"], cst], [t["Mo"]])
                    tt("pool", t["No"].ap, t["N0"].ap, bcm(mkT, 4, 128), ALU.mult, [t["N0"], cst], [t["No"]])
                    if lv == 0:
                        tt("dve", TTn.ap, t["No"].ap, bcm(ident, 4, 128), ALU.add, [t["No"], cst], [TTn])
                        tt("dve", Tn.ap, t["Mo"].ap, bcm(ident, 4, 128), ALU.add, [t["Mo"], cst], [Tn])
                        continue
                    pY1 = nps()
                    for h in range(4):
                        mm(pY1, pY1.ap[:, h * 128:(h + 1) * 128], t["Mo"].ap[:, h, :], TTc.ap[:, h, :], True, True, [t["Mo"], TTc])
                    if lv < 6:
                        pY2 = nps()
                        for h in range(4):
                            mm(pY2, pY2.ap[:, h * 128:(h + 1) * 128], t["No"].ap[:, h, :], Tc.ap[:, h, :], True, True, [t["No"], Tc])
                    P.op("act", lambda e, t=t, pY1=pY1, v4=v4: e.activation(out=t["Y1"].ap, in_=v4(pY1), func=AF.Copy), writes=[pY1, t["Y1"]])
                    if lv < 6:
                        P.op("act", lambda e, t=t, pY2=pY2, v4=v4: e.activation(out=t["Y2"].ap, in_=v4(pY2), func=AF.Copy), writes=[pY2, t["Y2"]])
                    pZ = nps()
                    for h in range(4):
                        mm(pZ, pZ.ap[:, h * 128:(h + 1) * 128], Tc.ap[:, h, :], t["Y1"].ap[:, h, :], True, True, [Tc, t["Y1"]])
                    if lv < 6:
                        pZ2 = nps()
                        for h in range(4):
                            mm(pZ2, pZ2.ap[:, h * 128:(h + 1) * 128], TTc.ap[:, h, :], t["Y2"].ap[:, h, :], True, True, [TTc, t["Y2"]])
                    tt("dve", TTn.ap, TTc.ap, v4(pZ), ALU.add, [TTc], [pZ, TTn])
                    if lv < 6:
                        tt("dve", Tn.ap, Tc.ap, v4(pZ2), ALU.add, [Tc], [pZ2, Tn])
                cur = nxt

            def apply_T(d):
                t = T[d]
                Xf = t[f"TT{cur}"]
                v4 = lambda p_: p_.ap[:, :].rearrange("p (h c) -> p h c", h=4)
                P.op("act", lambda e: e.activation(out=t["TTb"].ap, in_=Xf.ap, func=AF.Copy), reads=[Xf], writes=[t["TTb"]])
                pU, pW = nps(), nps()
                for h in range(4):
                    mm(pU, pU.ap[:, h * 128:(h + 1) * 128], t["TTb"].ap[:, h, :], t["vb"].ap[:, h, :], True, True, [t["TTb"], t["vb"]])
                for h in range(4):
                    mm(pW, pW.ap[:, h * 128:(h + 1) * 128], t["rw"].ap[:, h, :], t["TTb"].ap[:, h, :], True, True, [t["TTb"], t["rw"]])
                P.op("act", lambda e: e.activation(out=t["u"].ap, in_=v4(pU), func=AF.Copy), writes=[pU, t["u"]])
                P.op("dve", lambda e: e.tensor_copy(out=t["wT"].ap, in_=v4(pW)), writes=[pW, t["wT"]])
            both(apply_T)

            def scan(d):
                t = T[d]
                v4 = lambda p_: p_.ap[:, :].rearrange("p (h c) -> p h c", h=4)
                ob = t["o"][i % 2]
                pWS, pQS = nps(), nps()
                for h in range(4):
                    mm(pWS, pWS.ap[:, h * 128:(h + 1) * 128], t["wT"].ap[:, h, :], Sbf[d].ap[:, h, :], True, True, [t["wT"], Sbf[d]])
                for h in range(4):
                    mm(pQS, pQS.ap[:, h * 128:(h + 1) * 128], O[d]["qT"].ap[:, h, :], Sbf[d].ap[:, h, :], True, True, [O[d]["qT"], Sbf[d]])
                tt("dve", t["vnew"].ap, t["u"].ap, v4(pWS), ALU.subtract, [t["u"]], [pWS, t["vnew"]])
                pO2, pDS = nps(), nps()
                for h in range(4):
                    mm(pO2, pO2.ap[:, h * 128:(h + 1) * 128], t["qkdT"].ap[:, h, :], t["vnew"].ap[:, h, :], True, True, [t["qkdT"], t["vnew"]])
                for h in range(4):
                    mm(pDS, pDS.ap[:, h * 128:(h + 1) * 128], t["ktl"].ap[:, h, :], t["vnew"].ap[:, h, :], True, True, [t["ktl"], t["vnew"]])
                tt("dve", t["tmp"].ap, v4(pQS), bc3(t["egc"].ap[:, 0, :], 4, 128), ALU.mult, [t["egc"]], [pQS, t["tmp"]])
                tt("dve", ob.ap, t["tmp"].ap, v4(pO2), ALU.add, [t["tmp"]], [pO2, ob])
                tb = sq0 + blks[d] * 128
                P.dma("act", OFB[d][tb:tb + 128, :].rearrange("p (h c) -> p h c", h=4), ob.ap, DS(f"ost{d}{i % 2}"),
                      reads=[ob], writes=[OFBb[d][tb // 128]])
                tt("pool", S[d].ap, S[d].ap, bc3(t["gl"].ap[:, 0, :], 4, 128), ALU.mult, [S[d], t["gl"]], [S[d]])
                tt("dve", S[d].ap, S[d].ap, v4(pDS), ALU.add, [S[d]], [pDS, S[d]])
                P.op("act", lambda e: e.activation(out=Sbf[d].ap, in_=S[d].ap, func=AF.Copy), reads=[S[d]], writes=[Sbf[d]])
            both(scan)
            if debug and i == 0:
                for d in range(2):
                    for n_, nm in enumerate(("G2", "Dm", "DT", "KKm", "M0", "N0", "TT1", "u", "tmp")):
                        P.dma("act", DBG[d, n_].rearrange("p (h c) -> p h c", h=4), T[d][nm].ap, DS("dbg"), reads=[T[d][nm]], writes=[P.logical("x")])
                    P.dma("act", DBG[d, 9][:, 0:8], T[d]["sc"].ap[:, 0, :], DS("dbg"), reads=[T[d]["sc"]], writes=[P.logical("x")])
                    P.dma("act", DBG[d, 9][:, 8:12], T[d]["egc"].ap[:, 0, :], DS("dbg"), reads=[T[d]["egc"]], writes=[P.logical("x")])
                    P.dma("act", DBG[d, 9][:, 12:16], T[d]["be"].ap[:, 0, :], DS("dbg"), reads=[T[d]["be"]], writes=[P.logical("x")])
                    P.dma("act", DBG[d, 9][:, 16:24], BETA_all.ap[:, blks[d], :], DS("dbg"), reads=[BETA_all], writes=[P.logical("x")])
                    P.dma("act", DBG[d, 9][:, 24:32], G_all.ap[:, blks[d], :], DS("dbg"), reads=[G_all], writes=[P.logical("x")])

    def stageD(l, sq0, L):
        P.barrier()
        AR.off = BASE
        fb = alloc_ffn()
        lntmp = fb[5]
        xres = [AR.alloc(f"xres{i}", [4, D], F32) for i in range(2)]
        of = AR.alloc("of", [4, 512], F32)
        ob = AR.alloc("ob", [4, 512], F32)
        zs = AR.alloc("zs", [4, 512], F32)
        cvn = AR.alloc("cvn", [4, TT], BF16)
        oT = AR.alloc("oT", [4, TT], BF16)
        wo = AR.alloc("wo", [8, D], BF16)
        ssm = AR.alloc("ssm", [1, 16], F32)
        dst = XL if l == 0 else y_out
        nT = L // TT
        for t in range(nT):
            tok0 = sq0 + t * TT
            gt = tok0 // TT
            xr = xres[t % 2]
            if EXP1:
                P.barrier()
            P.dma("sp", xr.ap, X1[tok0:tok0 + TT, :].rearrange("(s p) d -> p s d", p=128), DS(f"xres{t % 2}"),
                  reads=[X1b[gt]], writes=[xr])
            bl = range(tok0 // 128, tok0 // 128 + 4)
            P.dma("sp", of.ap, OFB[0][tok0:tok0 + TT, :].rearrange("(s p) n -> p s n", p=128), DS("ofl"),
                  reads=[OFBb[0][b] for b in bl], writes=[of])
            P.dma("sp", ob.ap, OFB[1][tok0:tok0 + TT, :].rearrange("(s p) n -> p s n", p=128), DS("obl"),
                  reads=[OFBb[1][b] for b in bl], writes=[ob])
            P.dma("sp", zs.ap, ZS[tok0:tok0 + TT, :].rearrange("(s p) n -> p s n", p=128), DS("zsl"), reads=[ZSb[gt]], writes=[zs])
            P.dma("sp", cvn.ap, CVN[:, :, tok0:tok0 + TT].rearrange("c p n -> p c n"), DS("cvnl"), reads=[CVNb[gt]], writes=[cvn])
            if t == 0:
                P.dma("sp", wo.ap, WO[l], DS("wo"), reads=WOb[l], writes=[wo])
            tt("pool", of.ap, of.ap, ob.ap, ALU.add, [of, ob], [of])
            tt("pool", ob.ap, of.ap, of.ap, ALU.mult, [of], [ob])
            v16 = lambda b_: b_.ap.rearrange("p s (h c) -> p (s h) c", h=4)
            P.op("dve", lambda e: e.tensor_reduce(out=ssm.ap[:, 0, :], in_=v16(ob), axis=AX, op=ALU.add), reads=[ob], writes=[ssm])
            P.op("act", lambda e: e.activation(out=ssm.ap[:, 0, :], in_=ssm.ap[:, 0, :], func=AF.Sqrt, bias=epsap(EPS_NORM), scale=1.0 / 128),
                 reads=[ssm, epsb], writes=[ssm])
            P.op("dve", lambda e: e.reciprocal(out=ssm.ap[:, 0, :], in_=ssm.ap[:, 0, :]), reads=[ssm], writes=[ssm])
            tt("pool", v16(of), v16(of), bc3(ssm.ap[:, 0, :], 16, 128), ALU.mult, [of, ssm], [of])
            tt("pool", v16(of), v16(of), bcm(onw.ap[:, 0, :], 16, 128), ALU.mult, [of, onw], [of])
            tt("dve", of.ap, of.ap, zs.ap, ALU.mult, [of, zs], [of])
            for h in range(4):
                pb = nps()
                for s in range(4):
                    tr(pb, pb.ap[:, s * 128:(s + 1) * 128], of.ap[:, s, h * 128:(h + 1) * 128], ident, [of, cst])
                evac_copy(oT.ap[:, h, :], pb.ap[:, :], [], [pb, oT])
            for hf in range(2):
                for s in range(4):
                    pb = nps()
                    for c in range(8):
                        lhs = cvn.ap[:, c, s * 128:(s + 1) * 128] if c < 4 else oT.ap[:, c - 4, s * 128:(s + 1) * 128]
                        mm(pb, pb.ap[:, :], lhs, wo.ap[:, c, hf * 512:(hf + 1) * 512], c == 0, c == 7, [cvn, oT, wo])
                    P.op("dve", lambda e, s=s, hf=hf, pb=pb, xr=xr: e.scalar_tensor_tensor(
                        out=xr.ap[:, s, hf * 512:(hf + 1) * 512], in0=xr.ap[:, s, hf * 512:(hf + 1) * 512], scalar=ALPHA,
                        in1=pb.ap[:, :], op0=ALU.mult, op1=ALU.add), reads=[xr], writes=[xr, pb])
            if debug:
                P.dma("act", DBG4[tok0:tok0 + TT, :].rearrange("(s p) d -> p s d", p=128), xr.ap, DS("dbg4"), reads=[xr], writes=[P.logical("x")])
                P.dma("act", DBG5[:, :, tok0:tok0 + TT].rearrange("c p n -> p c n"), cvn.ap, DS("dbg5"), reads=[cvn], writes=[P.logical("x")])
                P.dma("act", DBG5[:, :, NTOK + tok0:NTOK + tok0 + TT].rearrange("c p n -> p c n"), oT.ap, DS("dbg5"), reads=[oT], writes=[P.logical("x")])
                P.dma("act", DBG6[gt], wo.ap, DS("dbg5"), reads=[wo], writes=[P.logical("x")])
            ln_apply(xr, lnp[1], EPS_LN, lntmp)
            if debug:
                P.dma("act", DBG2[tok0:tok0 + TT, :].rearrange("(s p) d -> p s d", p=128), xr.ap, DS("dbg2"), reads=[xr], writes=[P.logical("x")])
                P.dma("act", DBG3[tok0:tok0 + TT, :].rearrange("(s p) d -> p s d", p=128), of.ap, DS("dbg3"), reads=[of], writes=[P.logical("x")])
            ffn_ln(l, 1, xr, lnp[2], fb)
            P.dma("act", dst[tok0:tok0 + TT, :].rearrange("(s p) d -> p s d", p=128), xr.ap, DS(f"xst{t % 2}"),
                  reads=[xr], writes=[XLb[gt]])

    stage0()
    P.barrier()
    for l in range(depth):
        load_layer_params(l)
        sq0 = 0
        for L in seq_lens:
            stageA(l, sq0, L)
            stageB(l, sq0, L)
            stageC(l, sq0, L)
            stageD(l, sq0, L)
            sq0 += L
    P.emit()
    return nc


_NC_CACHE = {}


def kernel(**inputs):
    xp = np.ascontiguousarray(inputs["x_prompt"], dtype=np.float32)
    xs = np.ascontiguousarray(inputs["x_sample"], dtype=np.float32)
    n = 8
    seq_lens = (xp.shape[1], xp.shape[1], xs.shape[1])
    nc = build(list(seq_lens))
    consts = make_consts()
    in_maps = []
    for c in range(n):
        xc = np.concatenate([xp[2 * c], xp[2 * c + 1], xs[c]], axis=0)
        m = {"x": np.ascontiguousarray(xc), "consts": consts}
        for name, _ in WSHAPES:
            m[name] = np.ascontiguousarray(inputs[name], dtype=np.float32)
        in_maps.append(m)
    res = run_bass_kernel_spmd(nc, in_maps, core_ids=list(range(n)))
    yp = np.empty_like(xp)
    ys = np.empty_like(xs)
    Lp = xp.shape[1]
    for c in range(n):
        y = res.results[c]["y"]
        yp[2 * c] = y[0:Lp]
        yp[2 * c + 1] = y[Lp:2 * Lp]
        ys[c] = y[2 * Lp:]
    return (yp, ys)
```

```python
import numpy as np
from contextlib import ExitStack
import concourse.bass as bass
import concourse.mybir as mybir
from concourse.bass_utils import run_bass_kernel_spmd

F32 = mybir.dt.float32
BF16 = mybir.dt.bfloat16
AF = mybir.ActivationFunctionType
ALU = mybir.AluOpType


class Buf:
    __slots__ = ("name", "ap", "w", "r", "const")

    def __init__(self, name, ap=None, const=False):
        self.name = name
        self.ap = ap
        self.w = None
        self.r = {}
        self.const = const


class DSem:
    __slots__ = ("name", "handle", "count")

    def __init__(self, name):
        self.name = name
        self.handle = None
        self.count = 0


class Rec:
    __slots__ = ("eng", "fn", "deps", "signal", "semval", "dsem")

    def __init__(self, eng, fn, dsem=None):
        self.eng = eng
        self.fn = fn
        self.deps = []
        self.signal = False
        self.semval = 0
        self.dsem = dsem


ENGS = ("pe", "dve", "act", "pool", "sp")


class Prog:
    def __init__(self, nc):
        self.nc = nc
        self.stack = ExitStack()
        self.recs = {e: [] for e in ENGS}
        self.dsems = []
        self.nbuf = 0

    def sbuf(self, name, shape, dtype, const=False):
        t = self.stack.enter_context(self.nc.sbuf_tensor(name, list(shape), dtype))
        return Buf(name, t, const)

    def psum(self, name, shape, dtype):
        t = self.stack.enter_context(self.nc.psum_tensor(name, list(shape), dtype))
        return Buf(name, t)

    def dram(self, name, shape, dtype):
        t = self.nc.dram_tensor(name, list(shape), dtype, kind="Internal").ap()
        return Buf(name, t)

    def logical(self, name, const=False):
        return Buf(name, None, const)

    def dsem(self, name):
        s = DSem(name)
        self.dsems.append(s)
        return s

    def _record(self, rec, reads, writes):
        eng = rec.eng
        deps = {}

        def add(d, raw):
            if d is None:
                return
            if d.dsem is None and d.eng == eng:
                if eng == "pe" or not raw:
                    return
            deps[id(d)] = d

        for b in reads:
            add(b.w, True)
        for b in writes:
            add(b.w, False)
            for r in b.r.values():
                add(r, False)
        rec.deps = list(deps.values())
        key = eng if rec.dsem is None else ("d", id(rec.dsem))
        for b in reads:
            if not b.const:
                b.r[key] = rec
        for b in writes:
            b.w = rec
            b.r = {}
        self.recs[eng].append(rec)
        return rec

    def op(self, eng, fn, reads=(), writes=()):
        return self._record(Rec(eng, fn), reads, writes)

    def dma(self, eng, out, in_, dsem, reads=(), writes=()):
        rec = Rec(eng, lambda e: e.dma_start(out=out, in_=in_), dsem=dsem)
        dsem.count += 16
        rec.semval = dsem.count
        return self._record(rec, reads, writes)

    def barrier(self):
        comp = []
        for e in ENGS:
            for rec in reversed(self.recs[e]):
                if rec.dsem is None and rec.fn is not None:
                    comp.append(rec)
                    break
        dfakes = []
        for s in self.dsems:
            if s.count > 0:
                r = Rec("sp", None, dsem=s)
                r.semval = s.count
                dfakes.append(r)
        for e in ENGS:
            rec = Rec(e, None)
            rec.deps = [x for x in comp if x.eng != e] + dfakes
            self.recs[e].append(rec)

    def emit(self):
        nc = self.nc
        for e in ENGS:
            for rec in self.recs[e]:
                for d in rec.deps:
                    if d.dsem is None:
                        d.signal = True
        for e in ENGS:
            c = 0
            for rec in self.recs[e]:
                if rec.dsem is None and rec.signal:
                    c += 1
                    rec.semval = c
        esem = {}
        for e in ENGS:
            esem[e] = self.stack.enter_context(nc.semaphore("es_" + e))
        for s in self.dsems:
            s.handle = self.stack.enter_context(nc.semaphore("ds_" + s.name))
        recs = self.recs
        dsems = self.dsems

        def run(engname, e):
            seen = {}
            for rec in recs[engname]:
                need = {}
                for d in rec.deps:
                    if d.dsem is None:
                        key, h = d.eng, esem[d.eng]
                    else:
                        key, h = id(d.dsem), d.dsem.handle
                    v = d.semval
                    if seen.get(key, 0) >= v:
                        continue
                    if key not in need or need[key][1] < v:
                        need[key] = (h, v)
                for key, (h, v) in need.items():
                    e.wait_ge(h, v)
                    seen[key] = v
                if rec.fn is None:
                    continue
                ins = rec.fn(e)
                if rec.dsem is not None:
                    ins.then_inc(rec.dsem.handle, 16)
                elif rec.signal:
                    ins.then_inc(esem[engname], 1)
            if engname == "sp":
                for s in dsems:
                    if s.count > 0:
                        e.wait_ge(s.handle, s.count)

        with nc.Block() as block:
            @block.sync
            def _(e):
                run("sp", e)

            @block.tensor
            def _(e):
                run("pe", e)

            @block.vector
            def _(e):
                run("dve", e)

            @block.scalar
            def _(e):
                run("act", e)

            @block.gpsimd
            def _(e):
                run("pool", e)
        self.stack.close()


D = 1024
FF = 2816
NF = 22
DIN = 3088
TT = 512
DEPTH = 2
ALPHA = float((2 * DEPTH) ** 0.25)
LN_EPS = 1e-5
NORM_EPS = 1e-6
AX = mybir.AxisListType.X
F32R = mybir.dt.float32r
USE_F32R = False
EXP1 = False
STAGES = "0ABCD"

WSHAPES = [
    ("ffn1_wg", (DEPTH, D, FF)), ("ffn1_wu", (DEPTH, D, FF)), ("ffn1_wd", (DEPTH, FF, D)),
    ("ln1_g", (DEPTH, D)), ("ln1_b", (DEPTH, D)), ("w_in", (DEPTH, D, DIN)),
    ("conv_w", (DEPTH, 31, 512)), ("conv_b", (DEPTH, 512)), ("conv_ln_g", (DEPTH, 512)),
    ("conv_ln_b", (DEPTH, 512)), ("sconv_w", (DEPTH, 5, 1536)), ("a_log", (DEPTH, 2, 4)),
    ("dt_bias", (DEPTH, 2, 4)), ("o_norm_w", (DEPTH, 128)), ("w_out", (DEPTH, D, D)),
    ("ln2_g", (DEPTH, D)), ("ln2_b", (DEPTH, D)), ("ffn2_wg", (DEPTH, D, FF)),
    ("ffn2_wu", (DEPTH, D, FF)), ("ffn2_wd", (DEPTH, FF, D)), ("ln3_g", (DEPTH, D)), ("ln3_b", (DEPTH, D)),
]


def make_consts():
    p = np.arange(128)[:, None]
    i = np.arange(128)[None, :]
    c = np.zeros((20, 128, 128), np.float32)
    c[0] = (p == i)
    c[1] = 1.0
    c[2] = (p <= i)
    c[3] = (p >= i)
    c[4] = (p > i)
    c[5] = (p < i)
    for lv in range(7):
        sz = 1 << lv
        m = ((p // (2 * sz)) == (i // (2 * sz))) & ((p // sz) % 2 == 1) & ((i // sz) % 2 == 0)
        c[6 + lv] = m
        c[13 + lv] = m.T
    return c


class Arena:
    def __init__(self, P, nbytes):
        self.P = P
        self.t = P.stack.enter_context(P.nc.sbuf_tensor("arena", [128, nbytes // 4], F32))
        self.cap = nbytes
        self.off = 0

    def alloc(self, name, shape, dtype, const=False):
        n = 1
        for s in shape:
            n *= s
        esz = 4 if dtype == F32 else 2
        nb = (n * esz + 63) // 64 * 64
        assert self.off + nb <= self.cap, (name, self.off, nb, self.cap)
        ap = self.t[:, self.off // 4:(self.off + nb) // 4]
        if dtype != F32:
            ap = ap.bitcast(dtype)
        ap = ap[:, 0:n]
        if len(shape) == 2:
            ap = ap.rearrange("p (a b) -> p a b", a=shape[0])
        elif len(shape) == 3:
            ap = ap.rearrange("p (a b c) -> p a b c", a=shape[0], b=shape[1])
        elif len(shape) == 4:
            ap = ap.rearrange("p (a b c d) -> p a b c d", a=shape[0], b=shape[1], c=shape[2])
        self.off += nb
        return Buf(name, ap, const)


def build(seq_lens, depth=DEPTH, debug=False):
    nc = bass.Bass("TRN2", target_bir_lowering=False)
    NTOK = sum(seq_lens)

    def din(name, shape):
        return nc.dram_tensor(name, list(shape), F32, kind="ExternalInput").ap()

    x_in = din("x", [NTOK, D])
    y_out = nc.dram_tensor("y", [NTOK, D], F32, kind="ExternalOutput").ap()
    W = {name: din(name, shape) for name, shape in WSHAPES}
    consts_in = din("consts", [20, 128, 128])

    P = Prog(nc)
    AR = Arena(P, 204 * 1024)
    dsn = {}

    def DS(name):
        if name not in dsn:
            dsn[name] = P.dsem(name)
        return dsn[name]

    ps = [P.psum(f"ps{i}", [128, 512], F32) for i in range(8)]
    pctr = [0]

    def nps():
        b = ps[pctr[0] % 8]
        pctr[0] += 1
        return b

    rr = [0]

    def evac_copy(out_ap, in_ap, reads, writes, engs=("act", "dve")):
        e = engs[rr[0] % len(engs)]
        rr[0] += 1
        if e == "act":
            P.op("act", lambda e_: e_.activation(out=out_ap, in_=in_ap, func=AF.Copy), reads=reads, writes=writes)
        else:
            P.op(e, lambda e_: e_.tensor_copy(out=out_ap, in_=in_ap), reads=reads, writes=writes)

    def mm(pb, out_ap, lhsT, rhs, start, stop, reads):
        P.op("pe", lambda e: e.matmul(out_ap, lhsT, rhs, start=start, stop=stop), reads=reads, writes=[pb])

    def mmr(pb, out_ap, lhsT, rhs, reads):
        if USE_F32R:
            lhsT, rhs = lhsT.bitcast(F32R), rhs.bitcast(F32R)
        P.op("pe", lambda e: e.matmul(out_ap, lhsT, rhs, start=True, stop=True), reads=reads, writes=[pb])

    def r32(ap):
        return ap.bitcast(F32R) if USE_F32R else ap

    def tr(pb, out_ap, in_ap, ident_ap, reads):
        P.op("pe", lambda e: e.transpose(out_ap, in_ap, ident_ap), reads=reads, writes=[pb])

    def tt(eng, out, in0, in1, op, reads, writes):
        P.op(eng, lambda e: e.tensor_tensor(out=out, in0=in0, in1=in1, op=op), reads=reads, writes=writes)

    def bc3(ap2, n, inner):
        return ap2.unsqueeze(2).to_broadcast([128, n, inner])

    def bcm(ap2, n, inner):
        return ap2.unsqueeze(1).to_broadcast([128, n, inner])

    def dscr(name, shape, dt):
        return nc.dram_tensor(name, list(shape), dt, kind=("ExternalOutput" if debug else "Internal")).ap()

    WGU = [[dscr(f"WGU{l}{k}", [NF, 128, 2, 8, 128], BF16) for k in range(2)] for l in range(depth)]
    WGUb = [[[P.logical("wgu") for _ in range(16)] for k in range(2)] for l in range(depth)]
    WD = [[dscr(f"WD{l}{k}", [2, NF, 128, 512], BF16) for k in range(2)] for l in range(depth)]
    WDb = [[[P.logical("wd") for _ in range(22)] for k in range(2)] for l in range(depth)]
    WIN = [dscr(f"WIN{l}", [20, 128, 8, 128], BF16) for l in range(depth)]
    WZ = [dscr(f"WZ{l}", [128, 8, 512], BF16) for l in range(depth)]
    WBA = [dscr(f"WBA{l}", [128, 8, 16], BF16) for l in range(depth)]
    WINb = [[P.logical("win") for _ in range(8)] for l in range(depth)]
    WO = [dscr(f"WO{l}", [128, 8, 1024], BF16) for l in range(depth)]
    WOb = [[P.logical("wo") for _ in range(8)] for l in range(depth)]
    X1 = dscr("X1", [NTOK, D], F32)
    XL = dscr("XL", [NTOK, D], F32)
    PRE = dscr("PRE", [16, 128, NTOK], BF16)
    ZS = dscr("ZS", [NTOK, 512], F32)
    CVN = dscr("CVN", [4, 128, NTOK], BF16)
    KT = dscr("KT", [4, 128, NTOK], BF16)
    QT = dscr("QT", [4, 128, NTOK], BF16)
    KTOK = dscr("KTOK", [NTOK, 512], BF16)
    VTOK = dscr("VTOK", [NTOK, 512], BF16)
    OFB = [dscr("OF", [NTOK, 512], F32), dscr("OB", [NTOK, 512], F32)]
    DBG = dscr("DBG", [2, 10, 128, 512], F32) if debug else None
    DBG4 = dscr("DBG4", [NTOK, D], F32) if debug else None
    DBG5 = dscr("DBG5", [4, 128, 2 * NTOK], BF16) if debug else None
    DBG6 = dscr("DBG6", [NTOK // TT, 128, 8, 1024], BF16) if debug else None
    DBG2 = dscr("DBG2", [NTOK, D], F32) if debug else None
    DBG3 = dscr("DBG3", [NTOK, 512], F32) if debug else None
    NT_ALL = NTOK // TT
    NB_ALL = NTOK // 128
    X1b = [P.logical("x1") for _ in range(NT_ALL)]
    XLb = [P.logical("xl") for _ in range(NT_ALL)]
    PREb = [P.logical("pre") for _ in range(NT_ALL)]
    ZSb = [P.logical("zs") for _ in range(NT_ALL)]
    CVNb = [P.logical("cvn") for _ in range(NT_ALL)]
    QKVb = [P.logical("qkv") for _ in range(NT_ALL)]
    OFBb = [[P.logical("of") for _ in range(NB_ALL)] for _ in range(2)]

    cst = AR.alloc("cst", [20, 128], F32, const=True)
    ident = cst.ap[:, 0, :]
    ones_f = cst.ap[:, 1, :]
    UT = cst.ap[:, 2, :]
    LT = cst.ap[:, 3, :]
    SL = cst.ap[:, 4, :]
    SU = cst.ap[:, 5, :]
    P.dma("sp", cst.ap, consts_in.rearrange("c p j -> p c j"), DS("cst"), writes=[cst])
    epsb = AR.alloc("epsb", [1, 8], F32)
    EPS_FFN, EPS_LN, EPS_NORM, ONE = 0, 1, 2, 3
    for col, val in ((EPS_FFN, 4 * LN_EPS), (EPS_LN, LN_EPS), (EPS_NORM, NORM_EPS), (ONE, 1.0)):
        P.op("pool", lambda e, col=col, val=val: e.memset(epsb.ap[:, 0, col:col + 1], val), writes=[epsb])

    def epsap(col):
        return epsb.ap[:, 0, col:col + 1]

    NBMAX = max(seq_lens) // 128
    BETA_all = AR.alloc("beta_all", [NBMAX, 8], F32)
    G_all = AR.alloc("g_all", [NBMAX, 8], F32)
    lnp = [AR.alloc(f"lnp{i}", [2, D], F32) for i in range(3)]
    cw = AR.alloc("cw", [4, 34], F32)
    scw = AR.alloc("scw", [12, 5], F32)
    onw = AR.alloc("onw", [1, 128], F32)
    nA = AR.alloc("nA", [1, 8], F32)
    dtb = AR.alloc("dtb", [1, 8], F32)
    BASE = AR.off

    def stage0():
        AR.off = BASE
        sf = [AR.alloc(f"stgf{i}", [1, DIN], F32) for i in range(2)]
        sb = [AR.alloc(f"stgb{i}", [1, DIN], BF16) for i in range(2)]
        cnt = [0]
        cengs = ("act", "dve", "pool")

        def conv(src_ap, ncols, stores, view=None):
            i = cnt[0] % 2
            f, b = sf[i], sb[i]
            fa = f.ap[:, 0, 0:ncols]
            ba = b.ap[:, 0, 0:ncols]
            dst_f = fa if view is None else view(fa)
            P.dma("sp", dst_f, src_ap, DS(f"stgf{i}"), writes=[f])
            ce = cengs[cnt[0] % 3]
            if ce == "act":
                P.op("act", lambda e: e.activation(out=ba, in_=fa, func=AF.Copy), reads=[f], writes=[b])
            else:
                P.op(ce, lambda e: e.tensor_copy(out=ba, in_=fa), reads=[f], writes=[b])
            for (dst, srcfn, lb) in stores:
                P.dma("act", dst, srcfn(ba), DS(f"stgb{i}"), reads=[b], writes=[lb])
            cnt[0] += 1

        for l in range(depth):
            for k, pre in enumerate(("ffn1", "ffn2")):
                for a, nm in enumerate(("_wg", "_wu")):
                    for kc in range(8):
                        conv(W[pre + nm][l, kc * 128:(kc + 1) * 128, :], FF,
                             [(WGU[l][k][:, :, a, kc, :].rearrange("f p j -> p f j"),
                               lambda ba: ba.rearrange("p (f j) -> p f j", j=128), WGUb[l][k][a * 8 + kc])])
                for f0 in range(0, NF, 2):
                    conv(W[pre + "_wd"][l, f0 * 128:(f0 + 2) * 128, :].rearrange("(f p) j -> p f j", p=128), 2048,
                         [(WD[l][k][h, f0:f0 + 2].rearrange("f p j -> p f j"),
                           (lambda ba, h=h: ba.rearrange("p (f j) -> p f j", f=2)[:, :, h * 512:(h + 1) * 512]),
                           WDb[l][k][h * 11 + f0 // 2]) for h in range(2)],
                         view=lambda fa: fa.rearrange("p (f j) -> p f j", f=2))
            for kc in range(8):
                conv(W["w_in"][l, kc * 128:(kc + 1) * 128, :], DIN,
                     [(WIN[l][:, :, kc, :].rearrange("c p j -> p c j"),
                       lambda ba: ba[:, 0:2560].rearrange("p (c j) -> p c j", j=128), WINb[l][kc]),
                      (WZ[l][:, kc, :], lambda ba: ba[:, 2560:3072], P.logical("wz")),
                      (WBA[l][:, kc, :], lambda ba: ba[:, 3072:3088], P.logical("wba"))])
            for c in range(8):
                conv(W["w_out"][l, c * 128:(c + 1) * 128, :], D,
                     [(WO[l][:, c, :], lambda ba: ba, WOb[l][c])])

    def load_layer_params(l):
        P.barrier()
        AR.off = BASE
        craw = AR.alloc("craw", [1, 512], F32)
        sraw = AR.alloc("sraw", [1, 1536], F32)
        for i, nm in enumerate(("ln1", "ln2", "ln3")):
            P.dma("sp", lnp[i].ap[:, 0, :], W[nm + "_g"][l].partition_broadcast(128), DS(f"lnp{i}"), writes=[lnp[i]])
            P.dma("sp", lnp[i].ap[:, 1, :], W[nm + "_b"][l].partition_broadcast(128), DS(f"lnp{i}"), writes=[lnp[i]])
        P.dma("sp", craw.ap[0:31, 0, :], W["conv_w"][l], DS("craw"), writes=[craw])
        P.dma("sp", craw.ap[31:32, 0, :], W["conv_b"][l:l + 1, :], DS("craw"), writes=[craw])
        P.dma("sp", craw.ap[32:33, 0, :], W["conv_ln_g"][l:l + 1, :], DS("craw"), writes=[craw])
        P.dma("sp", craw.ap[33:34, 0, :], W["conv_ln_b"][l:l + 1, :], DS("craw"), writes=[craw])
        P.dma("sp", sraw.ap[0:5, 0, :], W["sconv_w"][l], DS("sraw"), writes=[sraw])
        P.dma("sp", onw.ap[:, 0, :], W["o_norm_w"][l].partition_broadcast(128), DS("onw"), writes=[onw])
        P.dma("sp", nA.ap[:, 0, :], W["a_log"][l].rearrange("a h -> (a h)").partition_broadcast(128), DS("nA"), writes=[nA])
        P.dma("sp", dtb.ap[:, 0, :], W["dt_bias"][l].rearrange("a h -> (a h)").partition_broadcast(128), DS("dtb"), writes=[dtb])
        P.op("act", lambda e: e.activation(out=nA.ap[:, 0, :], in_=nA.ap[:, 0, :], func=AF.Exp), reads=[nA], writes=[nA])
        P.op("dve", lambda e: e.tensor_scalar(out=nA.ap[:, 0, :], in0=nA.ap[:, 0, :], scalar1=-1.0, scalar2=None, op0=ALU.mult),
             reads=[nA], writes=[nA])
        for c in range(4):
            pb = nps()
            tr(pb, pb.ap[:, 0:34], craw.ap[0:34, 0, c * 128:(c + 1) * 128], cst.ap[0:34, 0, 0:34], [craw, cst])
            P.op("dve", lambda e, c=c, pb=pb: e.tensor_copy(out=cw.ap[:, c, :], in_=pb.ap[:, 0:34]), writes=[pb, cw])
        for c in range(12):
            pb = nps()
            tr(pb, pb.ap[:, 0:5], sraw.ap[0:5, 0, c * 128:(c + 1) * 128], cst.ap[0:5, 0, 0:5], [sraw, cst])
            P.op("dve", lambda e, c=c, pb=pb: e.tensor_copy(out=scw.ap[:, c, :], in_=pb.ap[:, 0:5]), writes=[pb, scw])
        P.barrier()

    def make_xT(xr, xT):
        for kc in range(8):
            pb = nps()
            for s in range(4):
                tr(pb, pb.ap[:, s * 128:(s + 1) * 128], xr.ap[:, s, kc * 128:(kc + 1) * 128], ident, [xr, cst])
            evac_copy(xT.ap[:, kc, :], pb.ap[:, :], [], [pb, xT])

    def ln_apply(xr, lp, epscol, tmp):
        st, mv, rs = tmp
        for s in range(4):
            for c in range(2):
                P.op("dve", lambda e, s=s, c=c: e.bn_stats(out=st.ap[:, s, c, :], in_=xr.ap[:, s, c * 512:(c + 1) * 512]),
                     reads=[xr], writes=[st])
            P.op("dve", lambda e, s=s: e.bn_aggr(out=mv.ap[:, s, :], in_=st.ap[:, s, :, :].rearrange("p c k -> p (c k)")),
                 reads=[st], writes=[mv])
        P.op("act", lambda e: e.activation(out=rs.ap[:, 0, :], in_=mv.ap[:, :, 1], func=AF.Sqrt, bias=epsap(epscol), scale=1.0),
             reads=[mv, epsb], writes=[rs])
        P.op("dve", lambda e: e.reciprocal(out=rs.ap[:, 0, :], in_=rs.ap[:, 0, :]), reads=[rs], writes=[rs])
        for s in range(4):
            P.op("dve", lambda e, s=s: e.tensor_scalar(out=xr.ap[:, s, :], in0=xr.ap[:, s, :], scalar1=mv.ap[:, s, 0:1],
                                                       scalar2=rs.ap[:, 0, s:s + 1], op0=ALU.subtract, op1=ALU.mult),
                 reads=[xr, mv, rs], writes=[xr])
        tt("pool", xr.ap, xr.ap, bcm(lp.ap[:, 0, :], 4, D), ALU.mult, [xr, lp], [xr])
        tt("pool", xr.ap, xr.ap, bcm(lp.ap[:, 1, :], 4, D), ALU.add, [xr, lp], [xr])

    def ffn_ln(l, k, xr, lp, fb):
        xT, hT, wgu, wdb, sg, lntmp = fb
        make_xT(xr, xT)

        def load_wgu(g):
            b = wgu[g % 2]
            P.dma("sp", b.ap.rearrange("p f a k j -> p f (a k j)"),
                  WGU[l][k][2 * g:2 * g + 2].rearrange("f p a k j -> p f (a k j)"), DS(f"wgu{g % 2}"),
                  reads=WGUb[l][k], writes=[b])

        load_wgu(0)
        for g in range(11):
            if g + 1 < 11:
                load_wgu(g + 1)
            b = wgu[g % 2]
            for f in range(2):
                ffc = 2 * g + f
                pg, pu = nps(), nps()
                for kc in range(8):
                    mm(pg, pg.ap[:, :], b.ap[:, f, 0, kc, :], xT.ap[:, kc, :], kc == 0, kc == 7, [b, xT])
                for kc in range(8):
                    mm(pu, pu.ap[:, :], b.ap[:, f, 1, kc, :], xT.ap[:, kc, :], kc == 0, kc == 7, [b, xT])
                s_ = sg[ffc % 2]
                P.op("act", lambda e, s_=s_, pg=pg: e.activation(out=s_.ap[:, 0, :], in_=pg.ap[:, :], func=AF.Silu),
                     writes=[pg, s_])
                tt("dve", hT.ap[:, ffc, :], pu.ap[:, :], s_.ap[:, 0, :], ALU.mult, [s_], [pu, hT])
        groups = [(0, 4), (4, 4), (8, 4), (12, 4), (16, 4), (20, 2)]
        seqs = [(h, gi) for h in range(2) for gi in range(len(groups))]

        def load_wd(idx):
            h, gi = seqs[idx]
            f0, n = groups[gi]
            b = wdb[idx % 2]
            P.dma("sp", b.ap[:, 0:n, :], WD[l][k][h, f0:f0 + n].rearrange("f p j -> p f j"), DS(f"wd{idx % 2}"),
                  reads=WDb[l][k], writes=[b])

        load_wd(0)
        idx = 0
        for h in range(2):
            py = [nps() for _ in range(4)]
            for gi, (f0, n) in enumerate(groups):
                if idx + 1 < len(seqs):
                    load_wd(idx + 1)
                b = wdb[idx % 2]
                for f in range(n):
                    ffc = f0 + f
                    for s in range(4):
                        mm(py[s], py[s].ap[:, :], hT.ap[:, ffc, s * 128:(s + 1) * 128], b.ap[:, f, :],
                           ffc == 0, ffc == NF - 1, [hT, b])
                idx += 1
            for s in range(4):
                P.op("dve", lambda e, s=s, h=h, p_=py[s]: e.scalar_tensor_tensor(
                    out=xr.ap[:, s, h * 512:(h + 1) * 512], in0=xr.ap[:, s, h * 512:(h + 1) * 512], scalar=2.0 * ALPHA,
                    in1=p_.ap[:, :], op0=ALU.mult, op1=ALU.add), reads=[xr], writes=[xr, py[s]])
        ln_apply(xr, lp, EPS_FFN, lntmp)

    def alloc_ffn():
        xT = AR.alloc("xT", [8, TT], BF16)
        hT = AR.alloc("hT", [NF, TT], BF16)
        wgu = [AR.alloc(f"wgu{i}", [2, 2, 8, 128], BF16) for i in range(2)]
        wdb = [AR.alloc(f"wdb{i}", [4, 512], BF16) for i in range(2)]
        sg = [AR.alloc(f"sg{i}", [1, TT], F32) for i in range(2)]
        st = AR.alloc("lnst", [4, 2, 6], F32)
        mv = AR.alloc("lnmv", [4, 2], F32)
        rs = AR.alloc("lnrs", [1, 4], F32)
        return (xT, hT, wgu, wdb, sg, (st, mv, rs))

    def stageA(l, sq0, L):
        P.barrier()
        AR.off = BASE
        fb = alloc_ffn()
        xT, wgu = fb[0], fb[2]
        xres = [AR.alloc(f"xres{i}", [4, D], F32) for i in range(2)]
        pre = AR.alloc("pre", [16, TT], BF16)
        zs = AR.alloc("zs", [4, TT], F32)
        sgm = AR.alloc("sgm", [4, TT], F32)
        wz = AR.alloc("wz", [8, 512], BF16)
        wba = AR.alloc("wba", [8, 16], BF16)
        t8 = AR.alloc("t8", [4, 8], F32)
        src = x_in if l == 0 else XL
        srcb = None if l == 0 else XLb
        P.dma("sp", wz.ap, WZ[l], DS("wz"), reads=WINb[l], writes=[wz])
        P.dma("sp", wba.ap, WBA[l], DS("wba"), reads=WINb[l], writes=[wba])
        nT = L // TT
        for t in range(nT):
            tok0 = sq0 + t * TT
            gt = tok0 // TT
            xr = xres[t % 2]
            P.dma("sp", xr.ap, src[tok0:tok0 + TT, :].rearrange("(s p) d -> p s d", p=128), DS(f"xres{t % 2}"),
                  reads=([] if srcb is None else [srcb[gt]]), writes=[xr])
            ffn_ln(l, 0, xr, lnp[0], fb)
            P.dma("act", X1[tok0:tok0 + TT, :].rearrange("(s p) d -> p s d", p=128), xr.ap, DS(f"xst{t % 2}"),
                  reads=[xr], writes=[X1b[gt]])
            make_xT(xr, xT)
            gorder = [1, 0, 2, 3, 4]

            def load_win(i):
                b = wgu[i % 2]
                c0 = gorder[i] * 4
                P.dma("sp", b.ap.rearrange("p f a k j -> p (f a) (k j)"),
                      WIN[l][c0:c0 + 4].rearrange("c p k j -> p c (k j)"), DS(f"wgu{i % 2}"),
                      reads=WINb[l], writes=[b])

            load_win(0)
            for i in range(5):
                if i + 1 < 5:
                    load_win(i + 1)
                b = wgu[i % 2]
                bv = b.ap.rearrange("p f a k j -> p (f a) k j")
                for cc in range(4):
                    c = gorder[i] * 4 + cc
                    pb = nps()
                    for kc in range(8):
                        mm(pb, pb.ap[:, :], bv[:, cc, kc, :], xT.ap[:, kc, :], kc == 0, kc == 7, [b, xT])
                    if gorder[i] == 1:
                        P.op("act", lambda e, cc=cc, pb=pb: e.activation(out=sgm.ap[:, cc, :], in_=pb.ap[:, :], func=AF.Sigmoid),
                             writes=[pb, sgm])
                    elif gorder[i] == 0:
                        tt("dve", pre.ap[:, cc, :], pb.ap[:, :], sgm.ap[:, cc, :], ALU.mult, [sgm], [pb, pre])
                    else:
                        evac_copy(pre.ap[:, c - 4, :], pb.ap[:, :], [], [pb, pre])
            P.dma("act", PRE[:, :, tok0:tok0 + TT].rearrange("c p n -> p c n"), pre.ap, DS("prest"),
                  reads=[pre], writes=[PREb[gt]])
            for s in range(4):
                pb = nps()
                for kc in range(8):
                    mm(pb, pb.ap[:, :], xT.ap[:, kc, s * 128:(s + 1) * 128], wz.ap[:, kc, :], kc == 0, kc == 7, [xT, wz])
                P.op("act", lambda e, s=s, pb=pb: e.activation(out=zs.ap[:, s, :], in_=pb.ap[:, :], func=AF.Silu),
                     writes=[pb, zs])
            P.dma("act", ZS[tok0:tok0 + TT, :].rearrange("(s p) n -> p s n", p=128), zs.ap, DS("zsst"),
                  reads=[zs], writes=[ZSb[gt]])
            pb = nps()
            for s in range(4):
                for kc in range(8):
                    mm(pb, pb.ap[:, s * 16:(s + 1) * 16], xT.ap[:, kc, s * 128:(s + 1) * 128], wba.ap[:, kc, :],
                       kc == 0, kc == 7, [xT, wba])
            pv = pb.ap[:, 0:64].rearrange("p (s c) -> p s c", s=4)
            b0 = (t * TT) // 128
            P.op("act", lambda e, pv=pv, b0=b0: e.activation(out=BETA_all.ap[:, b0:b0 + 4, :], in_=pv[:, :, 0:8], func=AF.Sigmoid),
                 writes=[pb, BETA_all])
            tt("dve", t8.ap, pv[:, :, 8:16], bcm(dtb.ap[:, 0, :], 4, 8), ALU.add, [dtb], [pb, t8])
            P.op("act", lambda e: e.activation(out=t8.ap, in_=t8.ap, func=AF.Exp), reads=[t8], writes=[t8])
            P.op("act", lambda e: e.activation(out=t8.ap, in_=t8.ap, func=AF.Ln, bias=epsap(ONE), scale=1.0),
                 reads=[t8, epsb], writes=[t8])
            tt("dve", G_all.ap[:, b0:b0 + 4, :], t8.ap, bcm(nA.ap[:, 0, :], 4, 8), ALU.mult, [t8, nA], [G_all])

    def stageB(l, sq0, L):
        P.barrier()
        AR.off = BASE
        wc = AR.alloc("wc", [4, TT + 30], BF16)
        wq = AR.alloc("wq", [12, TT + 4], BF16)
        cacc = AR.alloc("cacc", [4, TT], F32)
        csq = AR.alloc("csq", [4, TT], F32)
        mean = AR.alloc("mean", [1, TT], F32)
        var = AR.alloc("var", [1, TT], F32)
        cvn = AR.alloc("cvn", [4, TT], BF16)
        xf = [AR.alloc(f"xf{i}", [1, TT], F32) for i in range(12)]
        sqt_l = [AR.alloc(f"sqt{i}", [1, TT], F32) for i in range(2)]
        rn_l = [AR.alloc(f"rn{i}", [1, TT], F32) for i in range(8)]
        xn_l = [AR.alloc(f"xn{i}", [1, TT], F32) for i in range(4)]
        qTo = AR.alloc("qTo", [4, TT], BF16)
        kTo = AR.alloc("kTo", [4, TT], BF16)
        ktok = AR.alloc("ktok", [4, 512], BF16)
        vtok = AR.alloc("vtok", [4, 512], BF16)
        dg31 = AR.alloc("dg31", [4 * 31, 128], BF16)
        dg5 = AR.alloc("dg5", [12 * 5, 128], BF16)
        dctr = 0
        for c in range(4):
            for j in range(31):
                e_ = ("dve", "pool", "act")[dctr % 3]
                dctr += 1
                if e_ == "act":
                    P.op("act", lambda e, c=c, j=j: e.activation(out=dg31.ap[:, c * 31 + j, :], in_=ident, func=AF.Copy, scale=cw.ap[:, c, j:j + 1]),
                         reads=[cst, cw], writes=[dg31])
                else:
                    P.op(e_, lambda e, c=c, j=j: e.tensor_scalar(out=dg31.ap[:, c * 31 + j, :], in0=ident, scalar1=cw.ap[:, c, j:j + 1], scalar2=None, op0=ALU.mult),
                         reads=[cst, cw], writes=[dg31])
        for i in range(12):
            for j in range(5):
                e_ = ("dve", "pool", "act")[dctr % 3]
                dctr += 1
                if e_ == "act":
                    P.op("act", lambda e, i=i, j=j: e.activation(out=dg5.ap[:, i * 5 + j, :], in_=ident, func=AF.Copy, scale=scw.ap[:, i, j:j + 1]),
                         reads=[cst, scw], writes=[dg5])
                else:
                    P.op(e_, lambda e, i=i, j=j: e.tensor_scalar(out=dg5.ap[:, i * 5 + j, :], in0=ident, scalar1=scw.ap[:, i, j:j + 1], scalar2=None, op0=ALU.mult),
                         reads=[cst, scw], writes=[dg5])
        nT = L // TT
        sq1 = sq0 + L
        for t in range(nT):
            tok0 = sq0 + t * TT
            gt = tok0 // TT
            nb = [PREb[g] for g in (gt - 1, gt, gt + 1) if sq0 // TT <= g < sq1 // TT]
            for (w, c0, c1, halo) in ((wc, 0, 4, 15), (wq, 4, 16, 2)):
                lo, hi = max(tok0 - halo, sq0), min(tok0 + TT + halo, sq1)
                a = lo - (tok0 - halo)
                if a > 0:
                    P.op("pool", lambda e, w=w, a=a: e.memset(w.ap[:, :, 0:a], 0.0), writes=[w])
                if hi < tok0 + TT + halo:
                    P.op("pool", lambda e, w=w, hi=hi, lo=lo, a=a, halo=halo: e.memset(w.ap[:, :, a + hi - lo:TT + 2 * halo], 0.0), writes=[w])
                P.dma("sp", w.ap[:, :, a:a + hi - lo], PRE[c0:c1, :, lo:hi].rearrange("c p n -> p c n"), DS("w" + str(c0)),
                      reads=nb, writes=[w])
            for c in range(4):
                pb = nps()
                for j in range(31):
                    mm(pb, pb.ap[:, :], dg31.ap[:, c * 31 + j, :], wc.ap[:, c, j:j + TT], j == 0, j == 30, [dg31, wc])
                if c % 2 == 0:
                    P.op("act", lambda e, c=c, pb=pb: e.activation(out=cacc.ap[:, c, :], in_=pb.ap[:, :], func=AF.Identity,
                                                                    bias=cw.ap[:, c, 31:32], scale=1.0), reads=[cw], writes=[pb, cacc])
                else:
                    P.op("dve", lambda e, c=c, pb=pb: e.tensor_scalar(out=cacc.ap[:, c, :], in0=pb.ap[:, :], scalar1=cw.ap[:, c, 31:32],
                                                                       scalar2=None, op0=ALU.add), reads=[cw], writes=[pb, cacc])
            for i in range(12):
                x_ = xf[i]
                pc = nps()
                for j in range(5):
                    mm(pc, pc.ap[:, :], dg5.ap[:, i * 5 + j, :], wq.ap[:, i, j:j + TT], j == 0, j == 4, [dg5, wq])
                P.op("act", lambda e, x_=x_, pc=pc: e.activation(out=x_.ap[:, 0, :], in_=pc.ap[:, :], func=AF.Silu), writes=[pc, x_])
            tt("pool", csq.ap, cacc.ap, cacc.ap, ALU.mult, [cacc], [csq])
            p1, p2 = nps(), nps()
            for c in range(4):
                mm(p1, p1.ap[:, :], ones_f, cacc.ap[:, c, :], c == 0, c == 3, [cst, cacc])
            for c in range(4):
                mm(p2, p2.ap[:, :], ones_f, csq.ap[:, c, :], c == 0, c == 3, [cst, csq])
            P.op("act", lambda e, p1=p1: e.activation(out=mean.ap[:, 0, :], in_=p1.ap[:, :], func=AF.Copy, scale=1.0 / 512),
                 writes=[p1, mean])
            tt("dve", var.ap[:, 0, :], mean.ap[:, 0, :], mean.ap[:, 0, :], ALU.mult, [mean], [var])
            P.op("dve", lambda e, p2=p2: e.scalar_tensor_tensor(out=var.ap[:, 0, :], in0=p2.ap[:, :], scalar=1.0 / 512,
                                                               in1=var.ap[:, 0, :], op0=ALU.mult, op1=ALU.subtract),
                 reads=[var], writes=[var, p2])
            for i in range(8):
                x_, sqt, rn = xf[i], sqt_l[i % 2], rn_l[i]
                tt("pool", sqt.ap[:, 0, :], x_.ap[:, 0, :], x_.ap[:, 0, :], ALU.mult, [x_], [sqt])
                pb = nps()
                mm(pb, pb.ap[:, :], ones_f, sqt.ap[:, 0, :], True, True, [cst, sqt])
                P.op("act", lambda e, pb=pb, rn=rn: e.activation(out=rn.ap[:, 0, :], in_=pb.ap[:, :], func=AF.Ln, bias=epsap(EPS_NORM), scale=1.0),
                     reads=[epsb], writes=[pb, rn])
            P.op("act", lambda e: e.activation(out=var.ap[:, 0, :], in_=var.ap[:, 0, :], func=AF.Ln, bias=epsap(EPS_LN), scale=1.0),
                 reads=[var, epsb], writes=[var])
            for i in range(8):
                rn = rn_l[i]
                P.op("act", lambda e, rn=rn: e.activation(out=rn.ap[:, 0, :], in_=rn.ap[:, 0, :], func=AF.Exp, scale=-0.5), reads=[rn], writes=[rn])
            P.op("act", lambda e: e.activation(out=var.ap[:, 0, :], in_=var.ap[:, 0, :], func=AF.Exp, scale=-0.5), reads=[var], writes=[var])
            for i in range(8):
                x_, rn, h = xf[i], rn_l[i], i % 4
                if i < 4:
                    P.op("dve", lambda e, h=h, x_=x_, rn=rn: e.scalar_tensor_tensor(out=qTo.ap[:, h, :], in0=x_.ap[:, 0, :], scalar=float(128 ** -0.5),
                                                                                 in1=rn.ap[:, 0, :], op0=ALU.mult, op1=ALU.mult),
                         reads=[x_, rn], writes=[qTo])
                else:
                    xn = xn_l[h]
                    tt("dve", xn.ap[:, 0, :], x_.ap[:, 0, :], rn.ap[:, 0, :], ALU.mult, [x_, rn], [xn])
            tt("dve", cacc.ap, cacc.ap, bcm(mean.ap[:, 0, :], 4, TT), ALU.subtract, [cacc, mean], [cacc])
            tt("pool", cacc.ap, cacc.ap, bcm(var.ap[:, 0, :], 4, TT), ALU.mult, [cacc, var], [cacc])
            for h in range(4):
                xn = xn_l[h]
                P.op("act", lambda e, h=h, xn=xn: e.activation(out=kTo.ap[:, h, :], in_=xn.ap[:, 0, :], func=AF.Copy), reads=[xn], writes=[kTo])
                pb = nps()
                for s_ in range(4):
                    tr(pb, pb.ap[:, s_ * 128:(s_ + 1) * 128], xn.ap[:, 0, s_ * 128:(s_ + 1) * 128], ident, [xn, cst])
                evac_copy(ktok.ap[:, :, h * 128:(h + 1) * 128], pb.ap[:, :].rearrange("p (s c) -> p s c", s=4), [], [pb, ktok])
            for h in range(4):
                x_ = xf[8 + h]
                pb = nps()
                for s_ in range(4):
                    tr(pb, pb.ap[:, s_ * 128:(s_ + 1) * 128], x_.ap[:, 0, s_ * 128:(s_ + 1) * 128], ident, [x_, cst])
                evac_copy(vtok.ap[:, :, h * 128:(h + 1) * 128], pb.ap[:, :].rearrange("p (s c) -> p s c", s=4), [], [pb, vtok])
            for c in range(4):
                P.op("act", lambda e, c=c: e.activation(out=cvn.ap[:, c, :], in_=cacc.ap[:, c, :], func=AF.Silu,
                                                        scale=cw.ap[:, c, 32:33], bias=cw.ap[:, c, 33:34]),
                     reads=[cacc, cw], writes=[cvn])
            P.dma("act", CVN[:, :, tok0:tok0 + TT].rearrange("c p n -> p c n"), cvn.ap, DS("cvnst"),
                  reads=[cvn], writes=[CVNb[gt]])
            lb = QKVb[gt]
            P.dma("act", QT[:, :, tok0:tok0 + TT].rearrange("h p n -> p h n"), qTo.ap, DS("qst"), reads=[qTo], writes=[P.logical("x")])
            P.dma("act", KT[:, :, tok0:tok0 + TT].rearrange("h p n -> p h n"), kTo.ap, DS("kst"), reads=[kTo], writes=[P.logical("x")])
            P.dma("act", KTOK[tok0:tok0 + TT, :].rearrange("(s p) n -> p s n", p=128), ktok.ap, DS("ktst"), reads=[ktok], writes=[P.logical("x")])
            P.dma("act", VTOK[tok0:tok0 + TT, :].rearrange("(s p) n -> p s n", p=128), vtok.ap, DS("vtst"), reads=[vtok], writes=[lb])

    def stageC(l, sq0, L):
        P.barrier()
        AR.off = BASE
        nB = L // 128
        H4 = [4, 128]
        S = [AR.alloc(f"S{d}", H4, F32) for d in range(2)]
        Sbf = [AR.alloc(f"Sbf{d}", H4, BF16) for d in range(2)]
        for d in range(2):
            P.op("pool", lambda e, d=d: e.memset(S[d].ap, 0.0), writes=[S[d]])
            P.op("pool", lambda e, d=d: e.memset(Sbf[d].ap, 0.0), writes=[Sbf[d]])
        opnd = [[{nm: AR.alloc(f"{nm}{d}{i}", H4, BF16) for nm in ("kT", "qT", "kt", "vt")} for i in range(2)] for d in range(2)]
        T = [{} for _ in range(2)]
        for d in range(2):
            for nm in ("G2", "Dm", "DT", "KKm", "M0", "N0", "Mo", "No", "Y1", "Y2", "T0", "T1", "TT0", "TT1", "u", "tmp"):
                T[d][nm] = AR.alloc(f"{nm}{d}", H4, F32)
            for nm in ("qkdT", "TTb", "vb", "rw", "ktl", "wT", "vnew"):
                T[d][nm] = AR.alloc(f"{nm}{d}", H4, BF16)
            T[d]["o"] = [AR.alloc(f"o{d}{i}", H4, F32) for i in range(2)]
            T[d]["sc"] = AR.alloc(f"sc{d}", [1, 8], F32)
            for nm in ("egc", "etl", "gl", "nb", "be"):
                T[d][nm] = AR.alloc(f"{nm}{d}", [1, 4], F32)

        def load_ops(i):
            for d in range(2):
                blk = i if d == 0 else nB - 1 - i
                tb = sq0 + blk * 128
                o = opnd[d][i % 2]
                dsf = lambda nm: DS(f"op{nm}{d}{i % 2}")
                rb = [QKVb[tb // TT]]
                P.dma("sp", o["kT"].ap, KT[:, :, tb:tb + 128].rearrange("h p n -> p h n"), dsf("kT"), reads=rb, writes=[o["kT"]])
                P.dma("sp", o["qT"].ap, QT[:, :, tb:tb + 128].rearrange("h p n -> p h n"), dsf("qT"), reads=rb, writes=[o["qT"]])
                P.dma("sp", o["kt"].ap, KTOK[tb:tb + 128, :].rearrange("p (h c) -> p h c", h=4), dsf("kt"), reads=rb, writes=[o["kt"]])
                P.dma("sp", o["vt"].ap, VTOK[tb:tb + 128, :].rearrange("p (h c) -> p h c", h=4), dsf("vt"), reads=rb, writes=[o["vt"]])

        def both(fn):
            for d in range(2):
                fn(d)

        load_ops(0)
        for i in range(nB):
            if i + 1 < nB:
                load_ops(i + 1)
            blks = [i, nB - 1 - i]
            O = [opnd[d][i % 2] for d in range(2)]
            tri = [UT, LT]
            smask = [SL, SU]

            def scal(d):
                t = T[d]
                g = G_all.ap[:, blks[d], d * 4:(d + 1) * 4]
                beta = BETA_all.ap[:, blks[d], d * 4:(d + 1) * 4]
                pb = nps()
                mm(pb, pb.ap[:, 0:4], tri[d], g, True, True, [cst, G_all])
                mm(pb, pb.ap[:, 4:8], ones_f, g, True, True, [cst, G_all])
                sc = t["sc"]
                P.op("dve", lambda e: e.tensor_copy(out=sc.ap[:, 0, :], in_=pb.ap[:, 0:8]), writes=[pb, sc])
                P.op("act", lambda e: e.activation(out=t["egc"].ap[:, 0, :], in_=sc.ap[:, 0, 0:4], func=AF.Exp), reads=[sc], writes=[t["egc"]])
                tt("dve", t["etl"].ap[:, 0, :], sc.ap[:, 0, 4:8], sc.ap[:, 0, 0:4], ALU.subtract, [sc], [t["etl"]])
                P.op("act", lambda e: e.activation(out=t["etl"].ap[:, 0, :], in_=t["etl"].ap[:, 0, :], func=AF.Exp), reads=[t["etl"]], writes=[t["etl"]])
                P.op("act", lambda e: e.activation(out=t["gl"].ap[:, 0, :], in_=sc.ap[:, 0, 4:8], func=AF.Exp), reads=[sc], writes=[t["gl"]])
                P.op("dve", lambda e: e.tensor_scalar(out=t["nb"].ap[:, 0, :], in0=beta, scalar1=-1.0, scalar2=None, op0=ALU.mult),
                     reads=[BETA_all], writes=[t["nb"]])
                tt("dve", t["be"].ap[:, 0, :], beta, t["egc"].ap[:, 0, :], ALU.mult, [BETA_all, t["egc"]], [t["be"]])
                tt("pool", t["G2"].ap, bcm(smask[d], 4, 128), bc3(g, 4, 128), ALU.mult, [cst, G_all], [t["G2"]])
                tt("pool", t["vb"].ap, O[d]["vt"].ap, bc3(beta, 4, 128), ALU.mult, [O[d]["vt"], BETA_all], [t["vb"]])
            both(scal)

            def decay(d):
                t = T[d]
                pD, pDT, pK, pQ = nps(), nps(), nps(), nps()
                for h in range(4):
                    mm(pD, pD.ap[:, h * 128:(h + 1) * 128], tri[d], t["G2"].ap[:, h, :], True, True, [cst, t["G2"]])
                for h in range(4):
                    mm(pDT, pDT.ap[:, h * 128:(h + 1) * 128], t["G2"].ap[:, h, :], tri[d], True, True, [cst, t["G2"]])
                for h in range(4):
                    mm(pK, pK.ap[:, h * 128:(h + 1) * 128], O[d]["kT"].ap[:, h, :], O[d]["kT"].ap[:, h, :], True, True, [O[d]["kT"]])
                for h in range(4):
                    mm(pQ, pQ.ap[:, h * 128:(h + 1) * 128], O[d]["kT"].ap[:, h, :], O[d]["qT"].ap[:, h, :], True, True, [O[d]["kT"], O[d]["qT"]])
                v4 = lambda p_: p_.ap[:, :].rearrange("p (h c) -> p h c", h=4)
                P.op("act", lambda e: e.activation(out=t["Dm"].ap, in_=v4(pD), func=AF.Exp), writes=[pD, t["Dm"]])
                P.op("act", lambda e: e.activation(out=t["DT"].ap, in_=v4(pDT), func=AF.Exp), writes=[pDT, t["DT"]])
                tt("pool", t["DT"].ap, t["DT"].ap, bcm(tri[d], 4, 128), ALU.mult, [t["DT"], cst], [t["DT"]])
                tt("dve", t["KKm"].ap, v4(pK), bcm(smask[d], 4, 128), ALU.mult, [cst], [pK, t["KKm"]])
                tt("pool", t["KKm"].ap, t["KKm"].ap, t["Dm"].ap, ALU.mult, [t["KKm"], t["Dm"]], [t["KKm"]])
                tt("pool", t["M0"].ap, t["KKm"].ap, bc3(t["nb"].ap[:, 0, :], 4, 128), ALU.mult, [t["KKm"], t["nb"]], [t["M0"]])
                tt("dve", t["qkdT"].ap, v4(pQ), t["DT"].ap, ALU.mult, [t["DT"]], [pQ, t["qkdT"]])
                tt("pool", t["rw"].ap, O[d]["kt"].ap, bc3(t["be"].ap[:, 0, :], 4, 128), ALU.mult, [O[d]["kt"], t["be"]], [t["rw"]])
                tt("pool", t["ktl"].ap, O[d]["kt"].ap, bc3(t["etl"].ap[:, 0, :], 4, 128), ALU.mult, [O[d]["kt"], t["etl"]], [t["ktl"]])
                pN = nps()
                for h in range(4):
                    tr(pN, pN.ap[:, h * 128:(h + 1) * 128], t["M0"].ap[:, h, :], ident, [t["M0"], cst])
                P.op("act", lambda e: e.activation(out=t["N0"].ap, in_=v4(pN), func=AF.Copy), writes=[pN, t["N0"]])
            both(decay)

            cur = 0
            for lv in range(7):
                nxt = 1 - cur
                for d in range(2):
                    t = T[d]
                    v4 = lambda p_: p_.ap[:, :].rearrange("p (h c) -> p h c", h=4)
                    mk = cst.ap[:, 6 + lv, :] if d == 0 else cst.ap[:, 13 + lv, :]
                    mkT = cst.ap[:, 13 + lv, :] if d == 0 else cst.ap[:, 6 + lv, :]
                    Tc, TTc = t[f"T{cur}"], t[f"TT{cur}"]
                    Tn, TTn = t[f"T{nxt}"], t[f"TT{nxt}"]
                    tt("pool", r32(t["Mo"].ap), t["M0"].ap, bcm(mk, 4, 128), ALU.mult, [t["M0"], cst], [t["Mo"]])
                    tt("pool", r32(t["No"].ap), t["N0"].ap, bcm(mkT, 4, 128), ALU.mult, [t["N0"], cst], [t["No"]])
                    if lv == 0:
                        tt("dve", r32(TTn.ap), t["No"].ap, bcm(ident, 4, 128), ALU.add, [t["No"], cst], [TTn])
                        tt("dve", r32(Tn.ap), t["Mo"].ap, bcm(ident, 4, 128), ALU.add, [t["Mo"], cst], [Tn])
                        continue
                    pY1 = nps()
                    for h in range(4):
                        mmr(pY1, pY1.ap[:, h * 128:(h + 1) * 128], t["Mo"].ap[:, h, :], TTc.ap[:, h, :], [t["Mo"], TTc])
                    if lv < 6:
                        pY2 = nps()
                        for h in range(4):
                            mmr(pY2, pY2.ap[:, h * 128:(h + 1) * 128], t["No"].ap[:, h, :], Tc.ap[:, h, :], [t["No"], Tc])
                    P.op("act", lambda e, t=t, pY1=pY1, v4=v4: e.activation(out=r32(t["Y1"].ap), in_=v4(pY1), func=AF.Copy), writes=[pY1, t["Y1"]])
                    if lv < 6:
                        P.op("act", lambda e, t=t, pY2=pY2, v4=v4: e.activation(out=r32(t["Y2"].ap), in_=v4(pY2), func=AF.Copy), writes=[pY2, t["Y2"]])
                    pZ = nps()
                    for h in range(4):
                        mmr(pZ, pZ.ap[:, h * 128:(h + 1) * 128], Tc.ap[:, h, :], t["Y1"].ap[:, h, :], [Tc, t["Y1"]])
                    if lv < 6:
                        pZ2 = nps()
                        for h in range(4):
                            mmr(pZ2, pZ2.ap[:, h * 128:(h + 1) * 128], TTc.ap[:, h, :], t["Y2"].ap[:, h, :], [TTc, t["Y2"]])
                    tt("dve", r32(TTn.ap), TTc.ap, v4(pZ), ALU.add, [TTc], [pZ, TTn])
                    if lv < 6:
                        tt("dve", r32(Tn.ap), Tc.ap, v4(pZ2), ALU.add, [Tc], [pZ2, Tn])
                cur = nxt

            def apply_T(d):
                t = T[d]
                Xf = t[f"TT{cur}"]
                v4 = lambda p_: p_.ap[:, :].rearrange("p (h c) -> p h c", h=4)
                P.op("act", lambda e: e.activation(out=t["TTb"].ap, in_=Xf.ap, func=AF.Copy), reads=[Xf], writes=[t["TTb"]])
                pU, pW = nps(), nps()
                for h in range(4):
                    mm(pU, pU.ap[:, h * 128:(h + 1) * 128], t["TTb"].ap[:, h, :], t["vb"].ap[:, h, :], True, True, [t["TTb"], t["vb"]])
                for h in range(4):
                    mm(pW, pW.ap[:, h * 128:(h + 1) * 128], t["rw"].ap[:, h, :], t["TTb"].ap[:, h, :], True, True, [t["TTb"], t["rw"]])
                P.op("act", lambda e: e.activation(out=t["u"].ap, in_=v4(pU), func=AF.Copy), writes=[pU, t["u"]])
                P.op("dve", lambda e: e.tensor_copy(out=t["wT"].ap, in_=v4(pW)), writes=[pW, t["wT"]])
            both(apply_T)

            def scan(d):
                t = T[d]
                v4 = lambda p_: p_.ap[:, :].rearrange("p (h c) -> p h c", h=4)
                ob = t["o"][i % 2]
                pWS, pQS = nps(), nps()
                for h in range(4):
                    mm(pWS, pWS.ap[:, h * 128:(h + 1) * 128], t["wT"].ap[:, h, :], Sbf[d].ap[:, h, :], True, True, [t["wT"], Sbf[d]])
                for h in range(4):
                    mm(pQS, pQS.ap[:, h * 128:(h + 1) * 128], O[d]["qT"].ap[:, h, :], Sbf[d].ap[:, h, :], True, True, [O[d]["qT"], Sbf[d]])
                tt("dve", t["vnew"].ap, t["u"].ap, v4(pWS), ALU.subtract, [t["u"]], [pWS, t["vnew"]])
                pO2, pDS = nps(), nps()
                for h in range(4):
                    mm(pO2, pO2.ap[:, h * 128:(h + 1) * 128], t["qkdT"].ap[:, h, :], t["vnew"].ap[:, h, :], True, True, [t["qkdT"], t["vnew"]])
                for h in range(4):
                    mm(pDS, pDS.ap[:, h * 128:(h + 1) * 128], t["ktl"].ap[:, h, :], t["vnew"].ap[:, h, :], True, True, [t["ktl"], t["vnew"]])
                tt("dve", t["tmp"].ap, v4(pQS), bc3(t["egc"].ap[:, 0, :], 4, 128), ALU.mult, [t["egc"]], [pQS, t["tmp"]])
                tt("dve", ob.ap, t["tmp"].ap, v4(pO2), ALU.add, [t["tmp"]], [pO2, ob])
                tb = sq0 + blks[d] * 128
                P.dma("act", OFB[d][tb:tb + 128, :].rearrange("p (h c) -> p h c", h=4), ob.ap, DS(f"ost{d}{i % 2}"),
                      reads=[ob], writes=[OFBb[d][tb // 128]])
                tt("pool", S[d].ap, S[d].ap, bc3(t["gl"].ap[:, 0, :], 4, 128), ALU.mult, [S[d], t["gl"]], [S[d]])
                tt("dve", S[d].ap, S[d].ap, v4(pDS), ALU.add, [S[d]], [pDS, S[d]])
                P.op("act", lambda e: e.activation(out=Sbf[d].ap, in_=S[d].ap, func=AF.Copy), reads=[S[d]], writes=[Sbf[d]])
            both(scan)
            if debug and i == 0:
                for d in range(2):
                    for n_, nm in enumerate(("G2", "Dm", "DT", "KKm", "M0", "N0", "TT1", "u", "tmp")):
                        P.dma("act", DBG[d, n_].rearrange("p (h c) -> p h c", h=4), T[d][nm].ap, DS("dbg"), reads=[T[d][nm]], writes=[P.logical("x")])
                    P.dma("act", DBG[d, 9][:, 0:8], T[d]["sc"].ap[:, 0, :], DS("dbg"), reads=[T[d]["sc"]], writes=[P.logical("x")])
                    P.dma("act", DBG[d, 9][:, 8:12], T[d]["egc"].ap[:, 0, :], DS("dbg"), reads=[T[d]["egc"]], writes=[P.logical("x")])
                    P.dma("act", DBG[d, 9][:, 12:16], T[d]["be"].ap[:, 0, :], DS("dbg"), reads=[T[d]["be"]], writes=[P.logical("x")])
                    P.dma("act", DBG[d, 9][:, 16:24], BETA_all.ap[:, blks[d], :], DS("dbg"), reads=[BETA_all], writes=[P.logical("x")])
                    P.dma("act", DBG[d, 9][:, 24:32], G_all.ap[:, blks[d], :], DS("dbg"), reads=[G_all], writes=[P.logical("x")])

    def stageD(l, sq0, L):
        P.barrier()
        AR.off = BASE
        fb = alloc_ffn()
        lntmp = fb[5]
        xres = [AR.alloc(f"xres{i}", [4, D], F32) for i in range(2)]
        of = AR.alloc("of", [4, 512], F32)
        ob = AR.alloc("ob", [4, 512], F32)
        zs = AR.alloc("zs", [4, 512], F32)
        cvn = AR.alloc("cvn", [4, TT], BF16)
        oT = AR.alloc("oT", [4, TT], BF16)
        wo = AR.alloc("wo", [8, D], BF16)
        ssm = AR.alloc("ssm", [1, 16], F32)
        dst = XL if l == 0 else y_out
        nT = L // TT
        for t in range(nT):
            tok0 = sq0 + t * TT
            gt = tok0 // TT
            xr = xres[t % 2]
            if EXP1:
                P.barrier()
            P.dma("sp", xr.ap, X1[tok0:tok0 + TT, :].rearrange("(s p) d -> p s d", p=128), DS(f"xres{t % 2}"),
                  reads=[X1b[gt]], writes=[xr])
            bl = range(tok0 // 128, tok0 // 128 + 4)
            P.dma("sp", of.ap, OFB[0][tok0:tok0 + TT, :].rearrange("(s p) n -> p s n", p=128), DS("ofl"),
                  reads=[OFBb[0][b] for b in bl], writes=[of])
            P.dma("sp", ob.ap, OFB[1][tok0:tok0 + TT, :].rearrange("(s p) n -> p s n", p=128), DS("obl"),
                  reads=[OFBb[1][b] for b in bl], writes=[ob])
            P.dma("sp", zs.ap, ZS[tok0:tok0 + TT, :].rearrange("(s p) n -> p s n", p=128), DS("zsl"), reads=[ZSb[gt]], writes=[zs])
            P.dma("sp", cvn.ap, CVN[:, :, tok0:tok0 + TT].rearrange("c p n -> p c n"), DS("cvnl"), reads=[CVNb[gt]], writes=[cvn])
            if t == 0:
                P.dma("sp", wo.ap, WO[l], DS("wo"), reads=WOb[l], writes=[wo])
            tt("pool", of.ap, of.ap, ob.ap, ALU.add, [of, ob], [of])
            tt("pool", ob.ap, of.ap, of.ap, ALU.mult, [of], [ob])
            v16 = lambda b_: b_.ap.rearrange("p s (h c) -> p (s h) c", h=4)
            P.op("dve", lambda e: e.tensor_reduce(out=ssm.ap[:, 0, :], in_=v16(ob), axis=AX, op=ALU.add), reads=[ob], writes=[ssm])
            P.op("act", lambda e: e.activation(out=ssm.ap[:, 0, :], in_=ssm.ap[:, 0, :], func=AF.Sqrt, bias=epsap(EPS_NORM), scale=1.0 / 128),
                 reads=[ssm, epsb], writes=[ssm])
            P.op("dve", lambda e: e.reciprocal(out=ssm.ap[:, 0, :], in_=ssm.ap[:, 0, :]), reads=[ssm], writes=[ssm])
            tt("pool", v16(of), v16(of), bc3(ssm.ap[:, 0, :], 16, 128), ALU.mult, [of, ssm], [of])
            tt("pool", v16(of), v16(of), bcm(onw.ap[:, 0, :], 16, 128), ALU.mult, [of, onw], [of])
            tt("dve", of.ap, of.ap, zs.ap, ALU.mult, [of, zs], [of])
            for h in range(4):
                pb = nps()
                for s in range(4):
                    tr(pb, pb.ap[:, s * 128:(s + 1) * 128], of.ap[:, s, h * 128:(h + 1) * 128], ident, [of, cst])
                evac_copy(oT.ap[:, h, :], pb.ap[:, :], [], [pb, oT])
            for hf in range(2):
                for s in range(4):
                    pb = nps()
                    for c in range(8):
                        lhs = cvn.ap[:, c, s * 128:(s + 1) * 128] if c < 4 else oT.ap[:, c - 4, s * 128:(s + 1) * 128]
                        mm(pb, pb.ap[:, :], lhs, wo.ap[:, c, hf * 512:(hf + 1) * 512], c == 0, c == 7, [cvn, oT, wo])
                    P.op("dve", lambda e, s=s, hf=hf, pb=pb, xr=xr: e.scalar_tensor_tensor(
                        out=xr.ap[:, s, hf * 512:(hf + 1) * 512], in0=xr.ap[:, s, hf * 512:(hf + 1) * 512], scalar=ALPHA,
                        in1=pb.ap[:, :], op0=ALU.mult, op1=ALU.add), reads=[xr], writes=[xr, pb])
            if debug:
                P.dma("act", DBG4[tok0:tok0 + TT, :].rearrange("(s p) d -> p s d", p=128), xr.ap, DS("dbg4"), reads=[xr], writes=[P.logical("x")])
                P.dma("act", DBG5[:, :, tok0:tok0 + TT].rearrange("c p n -> p c n"), cvn.ap, DS("dbg5"), reads=[cvn], writes=[P.logical("x")])
                P.dma("act", DBG5[:, :, NTOK + tok0:NTOK + tok0 + TT].rearrange("c p n -> p c n"), oT.ap, DS("dbg5"), reads=[oT], writes=[P.logical("x")])
                P.dma("act", DBG6[gt], wo.ap, DS("dbg5"), reads=[wo], writes=[P.logical("x")])
            ln_apply(xr, lnp[1], EPS_LN, lntmp)
            if debug:
                P.dma("act", DBG2[tok0:tok0 + TT, :].rearrange("(s p) d -> p s d", p=128), xr.ap, DS("dbg2"), reads=[xr], writes=[P.logical("x")])
                P.dma("act", DBG3[tok0:tok0 + TT, :].rearrange("(s p) d -> p s d", p=128), of.ap, DS("dbg3"), reads=[of], writes=[P.logical("x")])
            ffn_ln(l, 1, xr, lnp[2], fb)
            P.dma("act", dst[tok0:tok0 + TT, :].rearrange("(s p) d -> p s d", p=128), xr.ap, DS(f"xst{t % 2}"),
                  reads=[xr], writes=[XLb[gt]])

    if "0" in STAGES:
        stage0()
    P.barrier()
    for l in range(depth):
        load_layer_params(l)
        sq0 = 0
        for L in seq_lens:
            if "A" in STAGES:
                stageA(l, sq0, L)
            if "B" in STAGES:
                stageB(l, sq0, L)
            if "C" in STAGES:
                stageC(l, sq0, L)
            if "D" in STAGES:
                stageD(l, sq0, L)
            sq0 += L
    P.emit()
    return nc


_NC_CACHE = {}


def kernel(**inputs):
    xp = np.ascontiguousarray(inputs["x_prompt"], dtype=np.float32)
    xs = np.ascontiguousarray(inputs["x_sample"], dtype=np.float32)
    n = 8
    seq_lens = (xp.shape[1], xp.shape[1], xs.shape[1])
    nc = build(list(seq_lens))
    consts = make_consts()
    in_maps = []
    for c in range(n):
        xc = np.concatenate([xp[2 * c], xp[2 * c + 1], xs[c]], axis=0)
        m = {"x": np.ascontiguousarray(xc), "consts": consts}
        for name, _ in WSHAPES:
            m[name] = np.ascontiguousarray(inputs[name], dtype=np.float32)
        in_maps.append(m)
    res = run_bass_kernel_spmd(nc, in_maps, core_ids=list(range(n)))
    yp = np.empty_like(xp)
    ys = np.empty_like(xs)
    Lp = xp.shape[1]
    for c in range(n):
        y = res.results[c]["y"]
        yp[2 * c] = y[0:Lp]
        yp[2 * c + 1] = y[Lp:2 * Lp]
        ys[c] = y[2 * Lp:]
    return (yp, ys)
```

```python
import numpy as np
from contextlib import ExitStack
import concourse.bass as bass
import concourse.mybir as mybir
from concourse.bass_utils import run_bass_kernel_spmd

F32 = mybir.dt.float32
BF16 = mybir.dt.bfloat16
AF = mybir.ActivationFunctionType
ALU = mybir.AluOpType


class Buf:
    __slots__ = ("name", "ap", "w", "r", "const")

    def __init__(self, name, ap=None, const=False):
        self.name = name
        self.ap = ap
        self.w = None
        self.r = {}
        self.const = const


class DSem:
    __slots__ = ("name", "handle", "count")

    def __init__(self, name):
        self.name = name
        self.handle = None
        self.count = 0


class Rec:
    __slots__ = ("eng", "fn", "deps", "signal", "semval", "dsem")

    def __init__(self, eng, fn, dsem=None):
        self.eng = eng
        self.fn = fn
        self.deps = []
        self.signal = False
        self.semval = 0
        self.dsem = dsem


ENGS = ("pe", "dve", "act", "pool", "sp")


class Prog:
    def __init__(self, nc):
        self.nc = nc
        self.stack = ExitStack()
        self.recs = {e: [] for e in ENGS}
        self.dsems = []
        self.nbuf = 0

    def sbuf(self, name, shape, dtype, const=False):
        t = self.stack.enter_context(self.nc.sbuf_tensor(name, list(shape), dtype))
        return Buf(name, t, const)

    def psum(self, name, shape, dtype):
        t = self.stack.enter_context(self.nc.psum_tensor(name, list(shape), dtype))
        return Buf(name, t)

    def dram(self, name, shape, dtype):
        t = self.nc.dram_tensor(name, list(shape), dtype, kind="Internal").ap()
        return Buf(name, t)

    def logical(self, name, const=False):
        return Buf(name, None, const)

    def dsem(self, name):
        s = DSem(name)
        self.dsems.append(s)
        return s

    def _record(self, rec, reads, writes):
        eng = rec.eng
        deps = {}

        def add(d, raw):
            if d is None:
                return
            if d.dsem is None and d.eng == eng:
                if eng == "pe" or not raw:
                    return
            deps[id(d)] = d

        for b in reads:
            add(b.w, True)
        for b in writes:
            add(b.w, False)
            for r in b.r.values():
                add(r, False)
        rec.deps = list(deps.values())
        key = eng if rec.dsem is None else ("d", id(rec.dsem))
        for b in reads:
            if not b.const:
                b.r[key] = rec
        for b in writes:
            b.w = rec
            b.r = {}
        self.recs[eng].append(rec)
        return rec

    def op(self, eng, fn, reads=(), writes=()):
        return self._record(Rec(eng, fn), reads, writes)

    def dma(self, eng, out, in_, dsem, reads=(), writes=()):
        rec = Rec(eng, lambda e: e.dma_start(out=out, in_=in_), dsem=dsem)
        dsem.count += 16
        rec.semval = dsem.count
        return self._record(rec, reads, writes)

    def barrier(self):
        comp = []
        for e in ENGS:
            for rec in reversed(self.recs[e]):
                if rec.dsem is None and rec.fn is not None:
                    comp.append(rec)
                    break
        dfakes = []
        for s in self.dsems:
            if s.count > 0:
                r = Rec("sp", None, dsem=s)
                r.semval = s.count
                dfakes.append(r)
        for e in ENGS:
            rec = Rec(e, None)
            rec.deps = [x for x in comp if x.eng != e] + dfakes
            self.recs[e].append(rec)

    def emit(self):
        nc = self.nc
        for e in ENGS:
            for rec in self.recs[e]:
                for d in rec.deps:
                    if d.dsem is None:
                        d.signal = True
        for e in ENGS:
            c = 0
            for rec in self.recs[e]:
                if rec.dsem is None and rec.signal:
                    c += 1
                    rec.semval = c
        esem = {}
        for e in ENGS:
            esem[e] = self.stack.enter_context(nc.semaphore("es_" + e))
        for s in self.dsems:
            s.handle = self.stack.enter_context(nc.semaphore("ds_" + s.name))
        recs = self.recs
        dsems = self.dsems

        def run(engname, e):
            seen = {}
            for rec in recs[engname]:
                need = {}
                for d in rec.deps:
                    if d.dsem is None:
                        key, h = d.eng, esem[d.eng]
                    else:
                        key, h = id(d.dsem), d.dsem.handle
                    v = d.semval
                    if seen.get(key, 0) >= v:
                        continue
                    if key not in need or need[key][1] < v:
                        need[key] = (h, v)
                for key, (h, v) in need.items():
                    e.wait_ge(h, v)
                    seen[key] = v
                if rec.fn is None:
                    continue
                ins = rec.fn(e)
                if rec.dsem is not None:
                    ins.then_inc(rec.dsem.handle, 16)
                elif rec.signal:
                    ins.then_inc(esem[engname], 1)
            if engname == "sp":
                for s in dsems:
                    if s.count > 0:
                        e.wait_ge(s.handle, s.count)

        with nc.Block() as block:
            @block.sync
            def _(e):
                run("sp", e)

            @block.tensor
            def _(e):
                run("pe", e)

            @block.vector
            def _(e):
                run("dve", e)

            @block.scalar
            def _(e):
                run("act", e)

            @block.gpsimd
            def _(e):
                run("pool", e)
        self.stack.close()


D = 1024
FF = 2816
NF = 22
DIN = 3088
TT = 512
DEPTH = 2
ALPHA = float((2 * DEPTH) ** 0.25)
LN_EPS = 1e-5
NORM_EPS = 1e-6
AX = mybir.AxisListType.X
F32R = mybir.dt.float32r
USE_F32R = False
EXP1 = False
STAGES = "0ABCD"

WSHAPES = [
    ("ffn1_wg", (DEPTH, D, FF)), ("ffn1_wu", (DEPTH, D, FF)), ("ffn1_wd", (DEPTH, FF, D)),
    ("ln1_g", (DEPTH, D)), ("ln1_b", (DEPTH, D)), ("w_in", (DEPTH, D, DIN)),
    ("conv_w", (DEPTH, 31, 512)), ("conv_b", (DEPTH, 512)), ("conv_ln_g", (DEPTH, 512)),
    ("conv_ln_b", (DEPTH, 512)), ("sconv_w", (DEPTH, 5, 1536)), ("a_log", (DEPTH, 2, 4)),
    ("dt_bias", (DEPTH, 2, 4)), ("o_norm_w", (DEPTH, 128)), ("w_out", (DEPTH, D, D)),
    ("ln2_g", (DEPTH, D)), ("ln2_b", (DEPTH, D)), ("ffn2_wg", (DEPTH, D, FF)),
    ("ffn2_wu", (DEPTH, D, FF)), ("ffn2_wd", (DEPTH, FF, D)), ("ln3_g", (DEPTH, D)), ("ln3_b", (DEPTH, D)),
]


def make_consts():
    p = np.arange(128)[:, None]
    i = np.arange(128)[None, :]
    c = np.zeros((20, 128, 128), np.float32)
    c[0] = (p == i)
    c[1] = 1.0
    c[2] = (p <= i)
    c[3] = (p >= i)
    c[4] = (p > i)
    c[5] = (p < i)
    for lv in range(7):
        sz = 1 << lv
        m = ((p // (2 * sz)) == (i // (2 * sz))) & ((p // sz) % 2 == 1) & ((i // sz) % 2 == 0)
        c[6 + lv] = m
        c[13 + lv] = m.T
    return c


class Arena:
    def __init__(self, P, nbytes):
        self.P = P
        self.t = P.stack.enter_context(P.nc.sbuf_tensor("arena", [128, nbytes // 4], F32))
        self.cap = nbytes
        self.off = 0

    def alloc(self, name, shape, dtype, const=False):
        n = 1
        for s in shape:
            n *= s
        esz = 4 if dtype == F32 else 2
        nb = (n * esz + 63) // 64 * 64
        assert self.off + nb <= self.cap, (name, self.off, nb, self.cap)
        ap = self.t[:, self.off // 4:(self.off + nb) // 4]
        if dtype != F32:
            ap = ap.bitcast(dtype)
        ap = ap[:, 0:n]
        if len(shape) == 2:
            ap = ap.rearrange("p (a b) -> p a b", a=shape[0])
        elif len(shape) == 3:
            ap = ap.rearrange("p (a b c) -> p a b c", a=shape[0], b=shape[1])
        elif len(shape) == 4:
            ap = ap.rearrange("p (a b c d) -> p a b c d", a=shape[0], b=shape[1], c=shape[2])
        self.off += nb
        return Buf(name, ap, const)


def build(seq_lens, depth=DEPTH, debug=False):
    nc = bass.Bass("TRN2", target_bir_lowering=False)
    NTOK = sum(seq_lens)

    def din(name, shape):
        return nc.dram_tensor(name, list(shape), F32, kind="ExternalInput").ap()

    x_in = din("x", [NTOK, D])
    y_out = nc.dram_tensor("y", [NTOK, D], F32, kind="ExternalOutput").ap()
    W = {name: din(name, shape) for name, shape in WSHAPES}
    consts_in = din("consts", [20, 128, 128])

    P = Prog(nc)
    AR = Arena(P, 204 * 1024)
    dsn = {}

    def DS(name):
        if name not in dsn:
            dsn[name] = P.dsem(name)
        return dsn[name]

    ps = [P.psum(f"ps{i}", [128, 512], F32) for i in range(8)]
    pctr = [0]

    def nps():
        b = ps[pctr[0] % 8]
        pctr[0] += 1
        return b

    rr = [0]

    def evac_copy(out_ap, in_ap, reads, writes, engs=("act", "dve")):
        e = engs[rr[0] % len(engs)]
        rr[0] += 1
        if e == "act":
            P.op("act", lambda e_: e_.activation(out=out_ap, in_=in_ap, func=AF.Copy), reads=reads, writes=writes)
        else:
            P.op(e, lambda e_: e_.tensor_copy(out=out_ap, in_=in_ap), reads=reads, writes=writes)

    def mm(pb, out_ap, lhsT, rhs, start, stop, reads):
        P.op("pe", lambda e: e.matmul(out_ap, lhsT, rhs, start=start, stop=stop), reads=reads, writes=[pb])

    def mmr(pb, out_ap, lhsT, rhs, reads):
        if USE_F32R:
            lhsT, rhs = lhsT.bitcast(F32R), rhs.bitcast(F32R)
        P.op("pe", lambda e: e.matmul(out_ap, lhsT, rhs, start=True, stop=True), reads=reads, writes=[pb])

    def r32(ap):
        return ap.bitcast(F32R) if USE_F32R else ap

    def tr(pb, out_ap, in_ap, ident_ap, reads):
        P.op("pe", lambda e: e.transpose(out_ap, in_ap, ident_ap), reads=reads, writes=[pb])

    def tt(eng, out, in0, in1, op, reads, writes):
        P.op(eng, lambda e: e.tensor_tensor(out=out, in0=in0, in1=in1, op=op), reads=reads, writes=writes)

    def bc3(ap2, n, inner):
        return ap2.unsqueeze(2).to_broadcast([128, n, inner])

    def bcm(ap2, n, inner):
        return ap2.unsqueeze(1).to_broadcast([128, n, inner])

    def dscr(name, shape, dt):
        return nc.dram_tensor(name, list(shape), dt, kind=("ExternalOutput" if debug else "Internal")).ap()

    WGU = [[dscr(f"WGU{l}{k}", [NF, 128, 2, 8, 128], BF16) for k in range(2)] for l in range(depth)]
    WGUb = [[[P.logical("wgu") for _ in range(16)] for k in range(2)] for l in range(depth)]
    WD = [[dscr(f"WD{l}{k}", [2, NF, 128, 512], BF16) for k in range(2)] for l in range(depth)]
    WDb = [[[P.logical("wd") for _ in range(22)] for k in range(2)] for l in range(depth)]
    WIN = [dscr(f"WIN{l}", [20, 128, 8, 128], BF16) for l in range(depth)]
    WZ = [dscr(f"WZ{l}", [128, 8, 512], BF16) for l in range(depth)]
    WBA = [dscr(f"WBA{l}", [128, 8, 16], BF16) for l in range(depth)]
    WINb = [[P.logical("win") for _ in range(8)] for l in range(depth)]
    WO = [dscr(f"WO{l}", [128, 8, 1024], BF16) for l in range(depth)]
    WOb = [[P.logical("wo") for _ in range(8)] for l in range(depth)]
    X1 = dscr("X1", [NTOK, D], F32)
    XL = dscr("XL", [NTOK, D], F32)
    PRE = dscr("PRE", [16, 128, NTOK], BF16)
    ZS = dscr("ZS", [NTOK, 512], F32)
    CVN = dscr("CVN", [4, 128, NTOK], BF16)
    KT = dscr("KT", [4, 128, NTOK], BF16)
    QT = dscr("QT", [4, 128, NTOK], BF16)
    KTOK = dscr("KTOK", [NTOK, 512], BF16)
    VTOK = dscr("VTOK", [NTOK, 512], BF16)
    OFB = [dscr("OF", [NTOK, 512], F32), dscr("OB", [NTOK, 512], F32)]
    DBG = dscr("DBG", [2, 10, 128, 512], F32) if debug else None
    DBG4 = dscr("DBG4", [NTOK, D], F32) if debug else None
    DBG5 = dscr("DBG5", [4, 128, 2 * NTOK], BF16) if debug else None
    DBG6 = dscr("DBG6", [NTOK // TT, 128, 8, 1024], BF16) if debug else None
    DBG2 = dscr("DBG2", [NTOK, D], F32) if debug else None
    DBG3 = dscr("DBG3", [NTOK, 512], F32) if debug else None
    NT_ALL = NTOK // TT
    NB_ALL = NTOK // 128
    X1b = [P.logical("x1") for _ in range(NT_ALL)]
    XLb = [P.logical("xl") for _ in range(NT_ALL)]
    PREb = [P.logical("pre") for _ in range(NT_ALL)]
    ZSb = [P.logical("zs") for _ in range(NT_ALL)]
    CVNb = [P.logical("cvn") for _ in range(NT_ALL)]
    QKVb = [P.logical("qkv") for _ in range(NT_ALL)]
    OFBb = [[P.logical("of") for _ in range(NB_ALL)] for _ in range(2)]

    cst = AR.alloc("cst", [20, 128], F32, const=True)
    ident = cst.ap[:, 0, :]
    ones_f = cst.ap[:, 1, :]
    UT = cst.ap[:, 2, :]
    LT = cst.ap[:, 3, :]
    SL = cst.ap[:, 4, :]
    SU = cst.ap[:, 5, :]
    P.dma("sp", cst.ap, consts_in.rearrange("c p j -> p c j"), DS("cst"), writes=[cst])
    epsb = AR.alloc("epsb", [1, 8], F32)
    EPS_FFN, EPS_LN, EPS_NORM, ONE = 0, 1, 2, 3
    for col, val in ((EPS_FFN, 4 * LN_EPS), (EPS_LN, LN_EPS), (EPS_NORM, NORM_EPS), (ONE, 1.0)):
        P.op("pool", lambda e, col=col, val=val: e.memset(epsb.ap[:, 0, col:col + 1], val), writes=[epsb])

    def epsap(col):
        return epsb.ap[:, 0, col:col + 1]

    NBMAX = max(seq_lens) // 128
    BETA_all = AR.alloc("beta_all", [NBMAX, 8], F32)
    G_all = AR.alloc("g_all", [NBMAX, 8], F32)
    lnp = [AR.alloc(f"lnp{i}", [2, D], F32) for i in range(3)]
    cw = AR.alloc("cw", [4, 34], F32)
    scw = AR.alloc("scw", [12, 5], F32)
    onw = AR.alloc("onw", [1, 128], F32)
    nA = AR.alloc("nA", [1, 8], F32)
    dtb = AR.alloc("dtb", [1, 8], F32)
    BASE = AR.off

    def stage0():
        AR.off = BASE
        sf = [AR.alloc(f"stgf{i}", [1, DIN], F32) for i in range(2)]
        sb = [AR.alloc(f"stgb{i}", [1, DIN], BF16) for i in range(2)]
        cnt = [0]
        cengs = ("act", "dve", "pool")

        def conv(src_ap, ncols, stores, view=None):
            i = cnt[0] % 2
            f, b = sf[i], sb[i]
            fa = f.ap[:, 0, 0:ncols]
            ba = b.ap[:, 0, 0:ncols]
            dst_f = fa if view is None else view(fa)
            P.dma("sp", dst_f, src_ap, DS(f"stgf{i}"), writes=[f])
            ce = cengs[cnt[0] % 3]
            if ce == "act":
                P.op("act", lambda e: e.activation(out=ba, in_=fa, func=AF.Copy), reads=[f], writes=[b])
            else:
                P.op(ce, lambda e: e.tensor_copy(out=ba, in_=fa), reads=[f], writes=[b])
            for (dst, srcfn, lb) in stores:
                P.dma("act", dst, srcfn(ba), DS(f"stgb{i}"), reads=[b], writes=[lb])
            cnt[0] += 1

        for l in range(depth):
            for k, pre in enumerate(("ffn1", "ffn2")):
                for a, nm in enumerate(("_wg", "_wu")):
                    for kc in range(8):
                        conv(W[pre + nm][l, kc * 128:(kc + 1) * 128, :], FF,
                             [(WGU[l][k][:, :, a, kc, :].rearrange("f p j -> p f j"),
                               lambda ba: ba.rearrange("p (f j) -> p f j", j=128), WGUb[l][k][a * 8 + kc])])
                for f0 in range(0, NF, 2):
                    conv(W[pre + "_wd"][l, f0 * 128:(f0 + 2) * 128, :].rearrange("(f p) j -> p f j", p=128), 2048,
                         [(WD[l][k][h, f0:f0 + 2].rearrange("f p j -> p f j"),
                           (lambda ba, h=h: ba.rearrange("p (f j) -> p f j", f=2)[:, :, h * 512:(h + 1) * 512]),
                           WDb[l][k][h * 11 + f0 // 2]) for h in range(2)],
                         view=lambda fa: fa.rearrange("p (f j) -> p f j", f=2))
            for kc in range(8):
                conv(W["w_in"][l, kc * 128:(kc + 1) * 128, :], DIN,
                     [(WIN[l][:, :, kc, :].rearrange("c p j -> p c j"),
                       lambda ba: ba[:, 0:2560].rearrange("p (c j) -> p c j", j=128), WINb[l][kc]),
                      (WZ[l][:, kc, :], lambda ba: ba[:, 2560:3072], P.logical("wz")),
                      (WBA[l][:, kc, :], lambda ba: ba[:, 3072:3088], P.logical("wba"))])
            for c in range(8):
                conv(W["w_out"][l, c * 128:(c + 1) * 128, :], D,
                     [(WO[l][:, c, :], lambda ba: ba, WOb[l][c])])

    def load_layer_params(l):
        P.barrier()
        AR.off = BASE
        craw = AR.alloc("craw", [1, 512], F32)
        sraw = AR.alloc("sraw", [1, 1536], F32)
        for i, nm in enumerate(("ln1", "ln2", "ln3")):
            P.dma("sp", lnp[i].ap[:, 0, :], W[nm + "_g"][l].partition_broadcast(128), DS(f"lnp{i}"), writes=[lnp[i]])
            P.dma("sp", lnp[i].ap[:, 1, :], W[nm + "_b"][l].partition_broadcast(128), DS(f"lnp{i}"), writes=[lnp[i]])
        P.dma("sp", craw.ap[0:31, 0, :], W["conv_w"][l], DS("craw"), writes=[craw])
        P.dma("sp", craw.ap[31:32, 0, :], W["conv_b"][l:l + 1, :], DS("craw"), writes=[craw])
        P.dma("sp", craw.ap[32:33, 0, :], W["conv_ln_g"][l:l + 1, :], DS("craw"), writes=[craw])
        P.dma("sp", craw.ap[33:34, 0, :], W["conv_ln_b"][l:l + 1, :], DS("craw"), writes=[craw])
        P.dma("sp", sraw.ap[0:5, 0, :], W["sconv_w"][l], DS("sraw"), writes=[sraw])
        P.dma("sp", onw.ap[:, 0, :], W["o_norm_w"][l].partition_broadcast(128), DS("onw"), writes=[onw])
        P.dma("sp", nA.ap[:, 0, :], W["a_log"][l].rearrange("a h -> (a h)").partition_broadcast(128), DS("nA"), writes=[nA])
        P.dma("sp", dtb.ap[:, 0, :], W["dt_bias"][l].rearrange("a h -> (a h)").partition_broadcast(128), DS("dtb"), writes=[dtb])
        P.op("act", lambda e: e.activation(out=nA.ap[:, 0, :], in_=nA.ap[:, 0, :], func=AF.Exp), reads=[nA], writes=[nA])
        P.op("dve", lambda e: e.tensor_scalar(out=nA.ap[:, 0, :], in0=nA.ap[:, 0, :], scalar1=-1.0, scalar2=None, op0=ALU.mult),
             reads=[nA], writes=[nA])
        for c in range(4):
            pb = nps()
            tr(pb, pb.ap[:, 0:34], craw.ap[0:34, 0, c * 128:(c + 1) * 128], cst.ap[0:34, 0, 0:34], [craw, cst])
            P.op("dve", lambda e, c=c, pb=pb: e.tensor_copy(out=cw.ap[:, c, :], in_=pb.ap[:, 0:34]), writes=[pb, cw])
        for c in range(12):
            pb = nps()
            tr(pb, pb.ap[:, 0:5], sraw.ap[0:5, 0, c * 128:(c + 1) * 128], cst.ap[0:5, 0, 0:5], [sraw, cst])
            P.op("dve", lambda e, c=c, pb=pb: e.tensor_copy(out=scw.ap[:, c, :], in_=pb.ap[:, 0:5]), writes=[pb, scw])
        P.barrier()

    def make_xT(xr, xT):
        for kc in range(8):
            pb = nps()
            for s in range(4):
                tr(pb, pb.ap[:, s * 128:(s + 1) * 128], xr.ap[:, s, kc * 128:(kc + 1) * 128], ident, [xr, cst])
            evac_copy(xT.ap[:, kc, :], pb.ap[:, :], [], [pb, xT])

    def ln_apply(xr, lp, epscol, tmp):
        st, mv, rs = tmp
        for s in range(4):
            for c in range(2):
                P.op("dve", lambda e, s=s, c=c: e.bn_stats(out=st.ap[:, s, c, :], in_=xr.ap[:, s, c * 512:(c + 1) * 512]),
                     reads=[xr], writes=[st])
            P.op("dve", lambda e, s=s: e.bn_aggr(out=mv.ap[:, s, :], in_=st.ap[:, s, :, :].rearrange("p c k -> p (c k)")),
                 reads=[st], writes=[mv])
        P.op("act", lambda e: e.activation(out=rs.ap[:, 0, :], in_=mv.ap[:, :, 1], func=AF.Sqrt, bias=epsap(epscol), scale=1.0),
             reads=[mv, epsb], writes=[rs])
        P.op("dve", lambda e: e.reciprocal(out=rs.ap[:, 0, :], in_=rs.ap[:, 0, :]), reads=[rs], writes=[rs])
        for s in range(4):
            P.op("dve", lambda e, s=s: e.tensor_scalar(out=xr.ap[:, s, :], in0=xr.ap[:, s, :], scalar1=mv.ap[:, s, 0:1],
                                                       scalar2=rs.ap[:, 0, s:s + 1], op0=ALU.subtract, op1=ALU.mult),
                 reads=[xr, mv, rs], writes=[xr])
        tt("pool", xr.ap, xr.ap, bcm(lp.ap[:, 0, :], 4, D), ALU.mult, [xr, lp], [xr])
        tt("pool", xr.ap, xr.ap, bcm(lp.ap[:, 1, :], 4, D), ALU.add, [xr, lp], [xr])

    def ffn_ln(l, k, xr, lp, fb):
        xT, hT, wgu, wdb, sg, lntmp = fb
        make_xT(xr, xT)

        def load_wgu(g):
            b = wgu[g % 2]
            P.dma("sp", b.ap.rearrange("p f a k j -> p f (a k j)"),
                  WGU[l][k][2 * g:2 * g + 2].rearrange("f p a k j -> p f (a k j)"), DS(f"wgu{g % 2}"),
                  reads=WGUb[l][k], writes=[b])

        load_wgu(0)
        for g in range(11):
            if g + 1 < 11:
                load_wgu(g + 1)
            b = wgu[g % 2]
            for f in range(2):
                ffc = 2 * g + f
                pg, pu = nps(), nps()
                for kc in range(8):
                    mm(pg, pg.ap[:, :], b.ap[:, f, 0, kc, :], xT.ap[:, kc, :], kc == 0, kc == 7, [b, xT])
                for kc in range(8):
                    mm(pu, pu.ap[:, :], b.ap[:, f, 1, kc, :], xT.ap[:, kc, :], kc == 0, kc == 7, [b, xT])
                s_ = sg[ffc % 2]
                P.op("act", lambda e, s_=s_, pg=pg: e.activation(out=s_.ap[:, 0, :], in_=pg.ap[:, :], func=AF.Silu),
                     writes=[pg, s_])
                tt("dve", hT.ap[:, ffc, :], pu.ap[:, :], s_.ap[:, 0, :], ALU.mult, [s_], [pu, hT])
        groups = [(0, 4), (4, 4), (8, 4), (12, 4), (16, 4), (20, 2)]
        seqs = [(h, gi) for h in range(2) for gi in range(len(groups))]

        def load_wd(idx):
            h, gi = seqs[idx]
            f0, n = groups[gi]
            b = wdb[idx % 2]
            P.dma("sp", b.ap[:, 0:n, :], WD[l][k][h, f0:f0 + n].rearrange("f p j -> p f j"), DS(f"wd{idx % 2}"),
                  reads=WDb[l][k], writes=[b])

        load_wd(0)
        idx = 0
        for h in range(2):
            py = [nps() for _ in range(4)]
            for gi, (f0, n) in enumerate(groups):
                if idx + 1 < len(seqs):
                    load_wd(idx + 1)
                b = wdb[idx % 2]
                for f in range(n):
                    ffc = f0 + f
                    for s in range(4):
                        mm(py[s], py[s].ap[:, :], hT.ap[:, ffc, s * 128:(s + 1) * 128], b.ap[:, f, :],
                           ffc == 0, ffc == NF - 1, [hT, b])
                idx += 1
            for s in range(4):
                P.op("dve", lambda e, s=s, h=h, p_=py[s]: e.scalar_tensor_tensor(
                    out=xr.ap[:, s, h * 512:(h + 1) * 512], in0=xr.ap[:, s, h * 512:(h + 1) * 512], scalar=2.0 * ALPHA,
                    in1=p_.ap[:, :], op0=ALU.mult, op1=ALU.add), reads=[xr], writes=[xr, py[s]])
        ln_apply(xr, lp, EPS_FFN, lntmp)

    def alloc_ffn():
        xT = AR.alloc("xT", [8, TT], BF16)
        hT = AR.alloc("hT", [NF, TT], BF16)
        wgu = [AR.alloc(f"wgu{i}", [2, 2, 8, 128], BF16) for i in range(2)]
        wdb = [AR.alloc(f"wdb{i}", [4, 512], BF16) for i in range(2)]
        sg = [AR.alloc(f"sg{i}", [1, TT], F32) for i in range(2)]
        st = AR.alloc("lnst", [4, 2, 6], F32)
        mv = AR.alloc("lnmv", [4, 2], F32)
        rs = AR.alloc("lnrs", [1, 4], F32)
        return (xT, hT, wgu, wdb, sg, (st, mv, rs))

    def stageA(l, sq0, L):
        P.barrier()
        AR.off = BASE
        fb = alloc_ffn()
        xT, wgu = fb[0], fb[2]
        xres = [AR.alloc(f"xres{i}", [4, D], F32) for i in range(2)]
        pre = AR.alloc("pre", [16, TT], BF16)
        zs = AR.alloc("zs", [4, TT], F32)
        sgm = AR.alloc("sgm", [4, TT], F32)
        wz = AR.alloc("wz", [8, 512], BF16)
        wba = AR.alloc("wba", [8, 16], BF16)
        t8 = AR.alloc("t8", [4, 8], F32)
        src = x_in if l == 0 else XL
        srcb = None if l == 0 else XLb
        P.dma("sp", wz.ap, WZ[l], DS("wz"), reads=WINb[l], writes=[wz])
        P.dma("sp", wba.ap, WBA[l], DS("wba"), reads=WINb[l], writes=[wba])
        nT = L // TT
        for t in range(nT):
            tok0 = sq0 + t * TT
            gt = tok0 // TT
            xr = xres[t % 2]
            P.dma("sp", xr.ap, src[tok0:tok0 + TT, :].rearrange("(s p) d -> p s d", p=128), DS(f"xres{t % 2}"),
                  reads=([] if srcb is None else [srcb[gt]]), writes=[xr])
            ffn_ln(l, 0, xr, lnp[0], fb)
            P.dma("act", X1[tok0:tok0 + TT, :].rearrange("(s p) d -> p s d", p=128), xr.ap, DS(f"xst{t % 2}"),
                  reads=[xr], writes=[X1b[gt]])
            make_xT(xr, xT)
            gorder = [1, 0, 2, 3, 4]

            def load_win(i):
                b = wgu[i % 2]
                c0 = gorder[i] * 4
                P.dma("sp", b.ap.rearrange("p f a k j -> p (f a) (k j)"),
                      WIN[l][c0:c0 + 4].rearrange("c p k j -> p c (k j)"), DS(f"wgu{i % 2}"),
                      reads=WINb[l], writes=[b])

            load_win(0)
            for i in range(5):
                if i + 1 < 5:
                    load_win(i + 1)
                b = wgu[i % 2]
                bv = b.ap.rearrange("p f a k j -> p (f a) k j")
                for cc in range(4):
                    c = gorder[i] * 4 + cc
                    pb = nps()
                    for kc in range(8):
                        mm(pb, pb.ap[:, :], bv[:, cc, kc, :], xT.ap[:, kc, :], kc == 0, kc == 7, [b, xT])
                    if gorder[i] == 1:
                        P.op("act", lambda e, cc=cc, pb=pb: e.activation(out=sgm.ap[:, cc, :], in_=pb.ap[:, :], func=AF.Sigmoid),
                             writes=[pb, sgm])
                    elif gorder[i] == 0:
                        tt("dve", pre.ap[:, cc, :], pb.ap[:, :], sgm.ap[:, cc, :], ALU.mult, [sgm], [pb, pre])
                    else:
                        evac_copy(pre.ap[:, c - 4, :], pb.ap[:, :], [], [pb, pre])
            P.dma("act", PRE[:, :, tok0:tok0 + TT].rearrange("c p n -> p c n"), pre.ap, DS("prest"),
                  reads=[pre], writes=[PREb[gt]])
            for s in range(4):
                pb = nps()
                for kc in range(8):
                    mm(pb, pb.ap[:, :], xT.ap[:, kc, s * 128:(s + 1) * 128], wz.ap[:, kc, :], kc == 0, kc == 7, [xT, wz])
                P.op("act", lambda e, s=s, pb=pb: e.activation(out=zs.ap[:, s, :], in_=pb.ap[:, :], func=AF.Silu),
                     writes=[pb, zs])
            P.dma("act", ZS[tok0:tok0 + TT, :].rearrange("(s p) n -> p s n", p=128), zs.ap, DS("zsst"),
                  reads=[zs], writes=[ZSb[gt]])
            pb = nps()
            for s in range(4):
                for kc in range(8):
                    mm(pb, pb.ap[:, s * 16:(s + 1) * 16], xT.ap[:, kc, s * 128:(s + 1) * 128], wba.ap[:, kc, :],
                       kc == 0, kc == 7, [xT, wba])
            pv = pb.ap[:, 0:64].rearrange("p (s c) -> p s c", s=4)
            b0 = (t * TT) // 128
            P.op("act", lambda e, pv=pv, b0=b0: e.activation(out=BETA_all.ap[:, b0:b0 + 4, :], in_=pv[:, :, 0:8], func=AF.Sigmoid),
                 writes=[pb, BETA_all])
            tt("dve", t8.ap, pv[:, :, 8:16], bcm(dtb.ap[:, 0, :], 4, 8), ALU.add, [dtb], [pb, t8])
            P.op("act", lambda e: e.activation(out=t8.ap, in_=t8.ap, func=AF.Exp), reads=[t8], writes=[t8])
            P.op("act", lambda e: e.activation(out=t8.ap, in_=t8.ap, func=AF.Ln, bias=epsap(ONE), scale=1.0),
                 reads=[t8, epsb], writes=[t8])
            tt("dve", G_all.ap[:, b0:b0 + 4, :], t8.ap, bcm(nA.ap[:, 0, :], 4, 8), ALU.mult, [t8, nA], [G_all])

    def stageB(l, sq0, L):
        P.barrier()
        AR.off = BASE
        wc = AR.alloc("wc", [4, TT + 30], BF16)
        wq = AR.alloc("wq", [12, TT + 4], BF16)
        cacc = AR.alloc("cacc", [4, TT], F32)
        csq = AR.alloc("csq", [4, TT], F32)
        mean = AR.alloc("mean", [1, TT], F32)
        var = AR.alloc("var", [1, TT], F32)
        cvn = AR.alloc("cvn", [4, TT], BF16)
        xf = [AR.alloc(f"xf{i}", [1, TT], F32) for i in range(12)]
        sqt_l = [AR.alloc(f"sqt{i}", [1, TT], F32) for i in range(2)]
        rn_l = [AR.alloc(f"rn{i}", [1, TT], F32) for i in range(8)]
        xn_l = [AR.alloc(f"xn{i}", [1, TT], F32) for i in range(4)]
        qTo = AR.alloc("qTo", [4, TT], BF16)
        kTo = AR.alloc("kTo", [4, TT], BF16)
        ktok = AR.alloc("ktok", [4, 512], BF16)
        vtok = AR.alloc("vtok", [4, 512], BF16)
        dg31 = AR.alloc("dg31", [4 * 31, 128], BF16)
        dg5 = AR.alloc("dg5", [12 * 5, 128], BF16)
        dctr = 0
        for c in range(4):
            for j in range(31):
                e_ = ("dve", "pool", "act")[dctr % 3]
                dctr += 1
                if e_ == "act":
                    P.op("act", lambda e, c=c, j=j: e.activation(out=dg31.ap[:, c * 31 + j, :], in_=ident, func=AF.Copy, scale=cw.ap[:, c, j:j + 1]),
                         reads=[cst, cw], writes=[dg31])
                else:
                    P.op(e_, lambda e, c=c, j=j: e.tensor_scalar(out=dg31.ap[:, c * 31 + j, :], in0=ident, scalar1=cw.ap[:, c, j:j + 1], scalar2=None, op0=ALU.mult),
                         reads=[cst, cw], writes=[dg31])
        for i in range(12):
            for j in range(5):
                e_ = ("dve", "pool", "act")[dctr % 3]
                dctr += 1
                if e_ == "act":
                    P.op("act", lambda e, i=i, j=j: e.activation(out=dg5.ap[:, i * 5 + j, :], in_=ident, func=AF.Copy, scale=scw.ap[:, i, j:j + 1]),
                         reads=[cst, scw], writes=[dg5])
                else:
                    P.op(e_, lambda e, i=i, j=j: e.tensor_scalar(out=dg5.ap[:, i * 5 + j, :], in0=ident, scalar1=scw.ap[:, i, j:j + 1], scalar2=None, op0=ALU.mult),
                         reads=[cst, scw], writes=[dg5])
        nT = L // TT
        sq1 = sq0 + L
        for t in range(nT):
            tok0 = sq0 + t * TT
            gt = tok0 // TT
            nb = [PREb[g] for g in (gt - 1, gt, gt + 1) if sq0 // TT <= g < sq1 // TT]
            for (w, c0, c1, halo) in ((wc, 0, 4, 15), (wq, 4, 16, 2)):
                lo, hi = max(tok0 - halo, sq0), min(tok0 + TT + halo, sq1)
                a = lo - (tok0 - halo)
                if a > 0:
                    P.op("pool", lambda e, w=w, a=a: e.memset(w.ap[:, :, 0:a], 0.0), writes=[w])
                if hi < tok0 + TT + halo:
                    P.op("pool", lambda e, w=w, hi=hi, lo=lo, a=a, halo=halo: e.memset(w.ap[:, :, a + hi - lo:TT + 2 * halo], 0.0), writes=[w])
                P.dma("sp", w.ap[:, :, a:a + hi - lo], PRE[c0:c1, :, lo:hi].rearrange("c p n -> p c n"), DS("w" + str(c0)),
                      reads=nb, writes=[w])
            for c in range(4):
                pb = nps()
                for j in range(31):
                    mm(pb, pb.ap[:, :], dg31.ap[:, c * 31 + j, :], wc.ap[:, c, j:j + TT], j == 0, j == 30, [dg31, wc])
                if c % 2 == 0:
                    P.op("act", lambda e, c=c, pb=pb: e.activation(out=cacc.ap[:, c, :], in_=pb.ap[:, :], func=AF.Identity,
                                                                    bias=cw.ap[:, c, 31:32], scale=1.0), reads=[cw], writes=[pb, cacc])
                else:
                    P.op("dve", lambda e, c=c, pb=pb: e.tensor_scalar(out=cacc.ap[:, c, :], in0=pb.ap[:, :], scalar1=cw.ap[:, c, 31:32],
                                                                       scalar2=None, op0=ALU.add), reads=[cw], writes=[pb, cacc])
            for i in range(12):
                x_ = xf[i]
                pc = nps()
                for j in range(5):
                    mm(pc, pc.ap[:, :], dg5.ap[:, i * 5 + j, :], wq.ap[:, i, j:j + TT], j == 0, j == 4, [dg5, wq])
                P.op("act", lambda e, x_=x_, pc=pc: e.activation(out=x_.ap[:, 0, :], in_=pc.ap[:, :], func=AF.Silu), writes=[pc, x_])
            tt("pool", csq.ap, cacc.ap, cacc.ap, ALU.mult, [cacc], [csq])
            p1, p2 = nps(), nps()
            for c in range(4):
                mm(p1, p1.ap[:, :], ones_f, cacc.ap[:, c, :], c == 0, c == 3, [cst, cacc])
            for c in range(4):
                mm(p2, p2.ap[:, :], ones_f, csq.ap[:, c, :], c == 0, c == 3, [cst, csq])
            P.op("act", lambda e, p1=p1: e.activation(out=mean.ap[:, 0, :], in_=p1.ap[:, :], func=AF.Copy, scale=1.0 / 512),
                 writes=[p1, mean])
            tt("dve", var.ap[:, 0, :], mean.ap[:, 0, :], mean.ap[:, 0, :], ALU.mult, [mean], [var])
            P.op("dve", lambda e, p2=p2: e.scalar_tensor_tensor(out=var.ap[:, 0, :], in0=p2.ap[:, :], scalar=1.0 / 512,
                                                               in1=var.ap[:, 0, :], op0=ALU.mult, op1=ALU.subtract),
                 reads=[var], writes=[var, p2])
            for i in range(8):
                x_, sqt, rn = xf[i], sqt_l[i % 2], rn_l[i]
                tt("pool", sqt.ap[:, 0, :], x_.ap[:, 0, :], x_.ap[:, 0, :], ALU.mult, [x_], [sqt])
                pb = nps()
                mm(pb, pb.ap[:, :], ones_f, sqt.ap[:, 0, :], True, True, [cst, sqt])
                P.op("act", lambda e, pb=pb, rn=rn: e.activation(out=rn.ap[:, 0, :], in_=pb.ap[:, :], func=AF.Ln, bias=epsap(EPS_NORM), scale=1.0),
                     reads=[epsb], writes=[pb, rn])
            P.op("act", lambda e: e.activation(out=var.ap[:, 0, :], in_=var.ap[:, 0, :], func=AF.Ln, bias=epsap(EPS_LN), scale=1.0),
                 reads=[var, epsb], writes=[var])
            for i in range(8):
                rn = rn_l[i]
                P.op("act", lambda e, rn=rn: e.activation(out=rn.ap[:, 0, :], in_=rn.ap[:, 0, :], func=AF.Exp, scale=-0.5), reads=[rn], writes=[rn])
            P.op("act", lambda e: e.activation(out=var.ap[:, 0, :], in_=var.ap[:, 0, :], func=AF.Exp, scale=-0.5), reads=[var], writes=[var])
            for i in range(8):
                x_, rn, h = xf[i], rn_l[i], i % 4
                if i < 4:
                    P.op("dve", lambda e, h=h, x_=x_, rn=rn: e.scalar_tensor_tensor(out=qTo.ap[:, h, :], in0=x_.ap[:, 0, :], scalar=float(128 ** -0.5),
                                                                                 in1=rn.ap[:, 0, :], op0=ALU.mult, op1=ALU.mult),
                         reads=[x_, rn], writes=[qTo])
                else:
                    xn = xn_l[h]
                    tt("dve", xn.ap[:, 0, :], x_.ap[:, 0, :], rn.ap[:, 0, :], ALU.mult, [x_, rn], [xn])
            tt("dve", cacc.ap, cacc.ap, bcm(mean.ap[:, 0, :], 4, TT), ALU.subtract, [cacc, mean], [cacc])
            tt("pool", cacc.ap, cacc.ap, bcm(var.ap[:, 0, :], 4, TT), ALU.mult, [cacc, var], [cacc])
            for h in range(4):
                xn = xn_l[h]
                P.op("act", lambda e, h=h, xn=xn: e.activation(out=kTo.ap[:, h, :], in_=xn.ap[:, 0, :], func=AF.Copy), reads=[xn], writes=[kTo])
                pb = nps()
                for s_ in range(4):
                    tr(pb, pb.ap[:, s_ * 128:(s_ + 1) * 128], xn.ap[:, 0, s_ * 128:(s_ + 1) * 128], ident, [xn, cst])
                evac_copy(ktok.ap[:, :, h * 128:(h + 1) * 128], pb.ap[:, :].rearrange("p (s c) -> p s c", s=4), [], [pb, ktok])
            for h in range(4):
                x_ = xf[8 + h]
                pb = nps()
                for s_ in range(4):
                    tr(pb, pb.ap[:, s_ * 128:(s_ + 1) * 128], x_.ap[:, 0, s_ * 128:(s_ + 1) * 128], ident, [x_, cst])
                evac_copy(vtok.ap[:, :, h * 128:(h + 1) * 128], pb.ap[:, :].rearrange("p (s c) -> p s c", s=4), [], [pb, vtok])
            for c in range(4):
                P.op("act", lambda e, c=c: e.activation(out=cvn.ap[:, c, :], in_=cacc.ap[:, c, :], func=AF.Silu,
                                                        scale=cw.ap[:, c, 32:33], bias=cw.ap[:, c, 33:34]),
                     reads=[cacc, cw], writes=[cvn])
            P.dma("act", CVN[:, :, tok0:tok0 + TT].rearrange("c p n -> p c n"), cvn.ap, DS("cvnst"),
                  reads=[cvn], writes=[CVNb[gt]])
            lb = QKVb[gt]
            P.dma("act", QT[:, :, tok0:tok0 + TT].rearrange("h p n -> p h n"), qTo.ap, DS("qst"), reads=[qTo], writes=[P.logical("x")])
            P.dma("act", KT[:, :, tok0:tok0 + TT].rearrange("h p n -> p h n"), kTo.ap, DS("kst"), reads=[kTo], writes=[P.logical("x")])
            P.dma("act", KTOK[tok0:tok0 + TT, :].rearrange("(s p) n -> p s n", p=128), ktok.ap, DS("ktst"), reads=[ktok], writes=[P.logical("x")])
            P.dma("act", VTOK[tok0:tok0 + TT, :].rearrange("(s p) n -> p s n", p=128), vtok.ap, DS("vtst"), reads=[vtok], writes=[lb])

    def stageC(l, sq0, L):
        P.barrier()
        AR.off = BASE
        nB = L // 128
        H4 = [4, 128]
        NOP = 3
        S = [AR.alloc(f"S{d}", H4, F32) for d in range(2)]
        Sbf = [AR.alloc(f"Sbf{d}", H4, BF16) for d in range(2)]
        for d in range(2):
            P.op("pool", lambda e, d=d: e.memset(S[d].ap, 0.0), writes=[S[d]])
            P.op("pool", lambda e, d=d: e.memset(Sbf[d].ap, 0.0), writes=[Sbf[d]])
        opnd = [[{nm: AR.alloc(f"{nm}{d}{i}", H4, BF16) for nm in ("kT", "qT", "kt", "vt")} for i in range(NOP)] for d in range(2)]
        PR = [[{} for _ in range(2)] for _ in range(2)]
        T = [{} for _ in range(2)]
        for d in range(2):
            for par in range(2):
                for nm in ("M0",):
                    PR[d][par][nm] = AR.alloc(f"{nm}{d}{par}", H4, F32)
                for nm in ("N0", "qkdT", "rw", "ktl", "vb"):
                    PR[d][par][nm] = AR.alloc(f"{nm}{d}{par}", H4, BF16)
                for nm in ("egc", "gl"):
                    PR[d][par][nm] = AR.alloc(f"{nm}{d}{par}", [1, 4], F32)
            for nm in ("G2", "Dm", "DT", "KKm", "u", "tmp"):
                T[d][nm] = AR.alloc(f"{nm}{d}", H4, F32)
            for nm in ("wT", "vnew", "Mo", "No", "Y1", "Y2", "T0", "T1", "TT0", "TT1"):
                T[d][nm] = AR.alloc(f"{nm}{d}", H4, BF16)
            T[d]["o"] = [AR.alloc(f"o{d}{i}", H4, F32) for i in range(2)]
            T[d]["sc"] = AR.alloc(f"sc{d}", [1, 8], F32)
            for nm in ("etl", "nb", "be"):
                T[d][nm] = AR.alloc(f"{nm}{d}", [1, 4], F32)
        tri = [UT, LT]
        smask = [SL, SU]

        def v4(p_):
            return p_.ap[:, :].rearrange("p (h c) -> p h c", h=4)

        def blk_of(i, d):
            return i if d == 0 else nB - 1 - i

        def load_ops(i):
            for d in range(2):
                tb = sq0 + blk_of(i, d) * 128
                o = opnd[d][i % NOP]
                dsf = lambda nm: DS(f"op{nm}{d}{i % NOP}")
                rb = [QKVb[tb // TT]]
                P.dma("sp", o["kT"].ap, KT[:, :, tb:tb + 128].rearrange("h p n -> p h n"), dsf("kT"), reads=rb, writes=[o["kT"]])
                P.dma("sp", o["qT"].ap, QT[:, :, tb:tb + 128].rearrange("h p n -> p h n"), dsf("qT"), reads=rb, writes=[o["qT"]])
                P.dma("sp", o["kt"].ap, KTOK[tb:tb + 128, :].rearrange("p (h c) -> p h c", h=4), dsf("kt"), reads=rb, writes=[o["kt"]])
                P.dma("sp", o["vt"].ap, VTOK[tb:tb + 128, :].rearrange("p (h c) -> p h c", h=4), dsf("vt"), reads=rb, writes=[o["vt"]])

        def prep_pieces(i):
            par = i % 2

            def ctx(d):
                blk = blk_of(i, d)
                return (T[d], PR[d][par], opnd[d][i % NOP],
                        G_all.ap[:, blk, d * 4:(d + 1) * 4], BETA_all.ap[:, blk, d * 4:(d + 1) * 4])

            def piece_scal():
                for d in range(2):
                    t, pr, O, g, beta = ctx(d)
                    pb = nps()
                    mm(pb, pb.ap[:, 0:4], tri[d], g, True, True, [cst, G_all])
                    mm(pb, pb.ap[:, 4:8], ones_f, g, True, True, [cst, G_all])
                    sc = t["sc"]
                    P.op("dve", lambda e, sc=sc, pb=pb: e.tensor_copy(out=sc.ap[:, 0, :], in_=pb.ap[:, 0:8]), writes=[pb, sc])
                    P.op("act", lambda e, sc=sc, pr=pr: e.activation(out=pr["egc"].ap[:, 0, :], in_=sc.ap[:, 0, 0:4], func=AF.Exp), reads=[sc], writes=[pr["egc"]])
                    tt("dve", t["etl"].ap[:, 0, :], sc.ap[:, 0, 4:8], sc.ap[:, 0, 0:4], ALU.subtract, [sc], [t["etl"]])
                    P.op("act", lambda e, t=t: e.activation(out=t["etl"].ap[:, 0, :], in_=t["etl"].ap[:, 0, :], func=AF.Exp), reads=[t["etl"]], writes=[t["etl"]])
                    P.op("act", lambda e, sc=sc, pr=pr: e.activation(out=pr["gl"].ap[:, 0, :], in_=sc.ap[:, 0, 4:8], func=AF.Exp), reads=[sc], writes=[pr["gl"]])
                    P.op("dve", lambda e, t=t, beta=beta: e.tensor_scalar(out=t["nb"].ap[:, 0, :], in0=beta, scalar1=-1.0, scalar2=None, op0=ALU.mult),
                         reads=[BETA_all], writes=[t["nb"]])
                    tt("dve", t["be"].ap[:, 0, :], beta, pr["egc"].ap[:, 0, :], ALU.mult, [BETA_all, pr["egc"]], [t["be"]])
                    tt("pool", t["G2"].ap, bcm(smask[d], 4, 128), bc3(g, 4, 128), ALU.mult, [cst, G_all], [t["G2"]])
                    tt("pool", pr["vb"].ap, O["vt"].ap, bc3(beta, 4, 128), ALU.mult, [O["vt"], BETA_all], [pr["vb"]])

            def piece_decay():
                for d in range(2):
                    t, pr, O, g, beta = ctx(d)
                    pD, pDT = nps(), nps()
                    for h in range(4):
                        mm(pD, pD.ap[:, h * 128:(h + 1) * 128], tri[d], t["G2"].ap[:, h, :], True, True, [cst, t["G2"]])
                    for h in range(4):
                        mm(pDT, pDT.ap[:, h * 128:(h + 1) * 128], t["G2"].ap[:, h, :], tri[d], True, True, [cst, t["G2"]])
                    P.op("act", lambda e, t=t, pD=pD: e.activation(out=t["Dm"].ap, in_=v4(pD), func=AF.Exp), writes=[pD, t["Dm"]])
                    P.op("act", lambda e, t=t, pDT=pDT: e.activation(out=t["DT"].ap, in_=v4(pDT), func=AF.Exp), writes=[pDT, t["DT"]])
                    tt("pool", t["DT"].ap, t["DT"].ap, bcm(tri[d], 4, 128), ALU.mult, [t["DT"], cst], [t["DT"]])

            def piece_kk():
                for d in range(2):
                    t, pr, O, g, beta = ctx(d)
                    pK, pQ = nps(), nps()
                    for h in range(4):
                        mm(pK, pK.ap[:, h * 128:(h + 1) * 128], O["kT"].ap[:, h, :], O["kT"].ap[:, h, :], True, True, [O["kT"]])
                    for h in range(4):
                        mm(pQ, pQ.ap[:, h * 128:(h + 1) * 128], O["kT"].ap[:, h, :], O["qT"].ap[:, h, :], True, True, [O["kT"], O["qT"]])
                    tt("dve", t["KKm"].ap, v4(pK), bcm(smask[d], 4, 128), ALU.mult, [cst], [pK, t["KKm"]])
                    tt("dve", pr["qkdT"].ap, v4(pQ), t["DT"].ap, ALU.mult, [t["DT"]], [pQ, pr["qkdT"]])
                    tt("pool", t["KKm"].ap, t["KKm"].ap, t["Dm"].ap, ALU.mult, [t["KKm"], t["Dm"]], [t["KKm"]])
                    tt("pool", pr["M0"].ap, t["KKm"].ap, bc3(t["nb"].ap[:, 0, :], 4, 128), ALU.mult, [t["KKm"], t["nb"]], [pr["M0"]])

            def piece_n0():
                for d in range(2):
                    t, pr, O, g, beta = ctx(d)
                    pN = nps()
                    for h in range(4):
                        tr(pN, pN.ap[:, h * 128:(h + 1) * 128], pr["M0"].ap[:, h, :], ident, [pr["M0"], cst])
                    P.op("act", lambda e, pr=pr, pN=pN: e.activation(out=pr["N0"].ap, in_=v4(pN), func=AF.Copy), writes=[pN, pr["N0"]])
                    tt("pool", pr["rw"].ap, O["kt"].ap, bc3(t["be"].ap[:, 0, :], 4, 128), ALU.mult, [O["kt"], t["be"]], [pr["rw"]])
                    tt("pool", pr["ktl"].ap, O["kt"].ap, bc3(t["etl"].ap[:, 0, :], 4, 128), ALU.mult, [O["kt"], t["etl"]], [pr["ktl"]])

            return [piece_scal, piece_decay, piece_kk, piece_n0]

        def level(i, lv, cur):
            nxt = 1 - cur
            for d in range(2):
                t, pr = T[d], PR[d][i % 2]
                mk = cst.ap[:, 6 + lv, :] if d == 0 else cst.ap[:, 13 + lv, :]
                mkT = cst.ap[:, 13 + lv, :] if d == 0 else cst.ap[:, 6 + lv, :]
                Tc, TTc = t[f"T{cur}"], t[f"TT{cur}"]
                Tn, TTn = t[f"T{nxt}"], t[f"TT{nxt}"]
                tt("pool", t["Mo"].ap, pr["M0"].ap, bcm(mk, 4, 128), ALU.mult, [pr["M0"], cst], [t["Mo"]])
                tt("pool", t["No"].ap, pr["N0"].ap, bcm(mkT, 4, 128), ALU.mult, [pr["N0"], cst], [t["No"]])
                if lv == 0:
                    tt("dve", TTn.ap, t["No"].ap, bcm(ident, 4, 128), ALU.add, [t["No"], cst], [TTn])
                    tt("dve", Tn.ap, t["Mo"].ap, bcm(ident, 4, 128), ALU.add, [t["Mo"], cst], [Tn])
                    continue
                pY1 = nps()
                for h in range(4):
                    mm(pY1, pY1.ap[:, h * 128:(h + 1) * 128], t["Mo"].ap[:, h, :], TTc.ap[:, h, :], True, True, [t["Mo"], TTc])
                if lv < 6:
                    pY2 = nps()
                    for h in range(4):
                        mm(pY2, pY2.ap[:, h * 128:(h + 1) * 128], t["No"].ap[:, h, :], Tc.ap[:, h, :], True, True, [t["No"], Tc])
                P.op("act", lambda e, t=t, pY1=pY1: e.activation(out=t["Y1"].ap, in_=v4(pY1), func=AF.Copy), writes=[pY1, t["Y1"]])
                if lv < 6:
                    P.op("act", lambda e, t=t, pY2=pY2: e.activation(out=t["Y2"].ap, in_=v4(pY2), func=AF.Copy), writes=[pY2, t["Y2"]])
                pZ = nps()
                for h in range(4):
                    mm(pZ, pZ.ap[:, h * 128:(h + 1) * 128], Tc.ap[:, h, :], t["Y1"].ap[:, h, :], True, True, [Tc, t["Y1"]])
                if lv < 6:
                    pZ2 = nps()
                    for h in range(4):
                        mm(pZ2, pZ2.ap[:, h * 128:(h + 1) * 128], TTc.ap[:, h, :], t["Y2"].ap[:, h, :], True, True, [TTc, t["Y2"]])
                tt("dve", TTn.ap, TTc.ap, v4(pZ), ALU.add, [TTc], [pZ, TTn])
                if lv < 6:
                    tt("dve", Tn.ap, Tc.ap, v4(pZ2), ALU.add, [Tc], [pZ2, Tn])
            return nxt

        def apply_T(i, cur, d):
            t, pr = T[d], PR[d][i % 2]
            Xf = t[f"TT{cur}"]
            pU, pW = nps(), nps()
            for h in range(4):
                mm(pU, pU.ap[:, h * 128:(h + 1) * 128], Xf.ap[:, h, :], pr["vb"].ap[:, h, :], True, True, [Xf, pr["vb"]])
            for h in range(4):
                mm(pW, pW.ap[:, h * 128:(h + 1) * 128], pr["rw"].ap[:, h, :], Xf.ap[:, h, :], True, True, [Xf, pr["rw"]])
            P.op("act", lambda e: e.activation(out=t["u"].ap, in_=v4(pU), func=AF.Copy), writes=[pU, t["u"]])
            P.op("dve", lambda e: e.tensor_copy(out=t["wT"].ap, in_=v4(pW)), writes=[pW, t["wT"]])

        def scan(i, d):
            t, pr = T[d], PR[d][i % 2]
            O = opnd[d][i % NOP]
            ob = t["o"][i % 2]
            pWS, pQS = nps(), nps()
            for h in range(4):
                mm(pWS, pWS.ap[:, h * 128:(h + 1) * 128], t["wT"].ap[:, h, :], Sbf[d].ap[:, h, :], True, True, [t["wT"], Sbf[d]])
            for h in range(4):
                mm(pQS, pQS.ap[:, h * 128:(h + 1) * 128], O["qT"].ap[:, h, :], Sbf[d].ap[:, h, :], True, True, [O["qT"], Sbf[d]])
            tt("dve", t["vnew"].ap, t["u"].ap, v4(pWS), ALU.subtract, [t["u"]], [pWS, t["vnew"]])
            pO2, pDS = nps(), nps()
            for h in range(4):
                mm(pO2, pO2.ap[:, h * 128:(h + 1) * 128], pr["qkdT"].ap[:, h, :], t["vnew"].ap[:, h, :], True, True, [pr["qkdT"], t["vnew"]])
            for h in range(4):
                mm(pDS, pDS.ap[:, h * 128:(h + 1) * 128], pr["ktl"].ap[:, h, :], t["vnew"].ap[:, h, :], True, True, [pr["ktl"], t["vnew"]])
            tt("dve", t["tmp"].ap, v4(pQS), bc3(pr["egc"].ap[:, 0, :], 4, 128), ALU.mult, [pr["egc"]], [pQS, t["tmp"]])
            tt("dve", ob.ap, t["tmp"].ap, v4(pO2), ALU.add, [t["tmp"]], [pO2, ob])
            tb = sq0 + blk_of(i, d) * 128
            P.dma("act", OFB[d][tb:tb + 128, :].rearrange("p (h c) -> p h c", h=4), ob.ap, DS(f"ost{d}{i % 2}"),
                  reads=[ob], writes=[OFBb[d][tb // 128]])
            tt("pool", S[d].ap, S[d].ap, bc3(pr["gl"].ap[:, 0, :], 4, 128), ALU.mult, [S[d], pr["gl"]], [S[d]])
            tt("dve", S[d].ap, S[d].ap, v4(pDS), ALU.add, [S[d]], [pDS, S[d]])
            P.op("act", lambda e: e.activation(out=Sbf[d].ap, in_=S[d].ap, func=AF.Copy), reads=[S[d]], writes=[Sbf[d]])

        load_ops(0)
        if nB > 1:
            load_ops(1)
        for p_ in prep_pieces(0):
            p_()
        for i in range(nB):
            if i + 2 < nB:
                load_ops(i + 2)
            pend = prep_pieces(i + 1) if i + 1 < nB else []
            cur = 0
            for lv in range(7):
                cur = level(i, lv, cur)
                if pend and lv >= 1:
                    pend.pop(0)()
            while pend:
                pend.pop(0)()
            for d in range(2):
                apply_T(i, cur, d)
            for d in range(2):
                scan(i, d)

    def stageD(l, sq0, L):
        P.barrier()
        AR.off = BASE
        fb = alloc_ffn()
        lntmp = fb[5]
        xres = [AR.alloc(f"xres{i}", [4, D], F32) for i in range(2)]
        of = AR.alloc("of", [4, 512], F32)
        ob = AR.alloc("ob", [4, 512], F32)
        zs = AR.alloc("zs", [4, 512], F32)
        cvn = AR.alloc("cvn", [4, TT], BF16)
        oT = AR.alloc("oT", [4, TT], BF16)
        wo = AR.alloc("wo", [8, D], BF16)
        ssm = AR.alloc("ssm", [1, 16], F32)
        dst = XL if l == 0 else y_out
        nT = L // TT
        for t in range(nT):
            tok0 = sq0 + t * TT
            gt = tok0 // TT
            xr = xres[t % 2]
            if EXP1:
                P.barrier()
            P.dma("sp", xr.ap, X1[tok0:tok0 + TT, :].rearrange("(s p) d -> p s d", p=128), DS(f"xres{t % 2}"),
                  reads=[X1b[gt]], writes=[xr])
            bl = range(tok0 // 128, tok0 // 128 + 4)
            P.dma("sp", of.ap, OFB[0][tok0:tok0 + TT, :].rearrange("(s p) n -> p s n", p=128), DS("ofl"),
                  reads=[OFBb[0][b] for b in bl], writes=[of])
            P.dma("sp", ob.ap, OFB[1][tok0:tok0 + TT, :].rearrange("(s p) n -> p s n", p=128), DS("obl"),
                  reads=[OFBb[1][b] for b in bl], writes=[ob])
            P.dma("sp", zs.ap, ZS[tok0:tok0 + TT, :].rearrange("(s p) n -> p s n", p=128), DS("zsl"), reads=[ZSb[gt]], writes=[zs])
            P.dma("sp", cvn.ap, CVN[:, :, tok0:tok0 + TT].rearrange("c p n -> p c n"), DS("cvnl"), reads=[CVNb[gt]], writes=[cvn])
            if t == 0:
                P.dma("sp", wo.ap, WO[l], DS("wo"), reads=WOb[l], writes=[wo])
            tt("pool", of.ap, of.ap, ob.ap, ALU.add, [of, ob], [of])
            tt("pool", ob.ap, of.ap, of.ap, ALU.mult, [of], [ob])
            v16 = lambda b_: b_.ap.rearrange("p s (h c) -> p (s h) c", h=4)
            P.op("dve", lambda e: e.tensor_reduce(out=ssm.ap[:, 0, :], in_=v16(ob), axis=AX, op=ALU.add), reads=[ob], writes=[ssm])
            P.op("act", lambda e: e.activation(out=ssm.ap[:, 0, :], in_=ssm.ap[:, 0, :], func=AF.Sqrt, bias=epsap(EPS_NORM), scale=1.0 / 128),
                 reads=[ssm, epsb], writes=[ssm])
            P.op("dve", lambda e: e.reciprocal(out=ssm.ap[:, 0, :], in_=ssm.ap[:, 0, :]), reads=[ssm], writes=[ssm])
            tt("pool", v16(of), v16(of), bc3(ssm.ap[:, 0, :], 16, 128), ALU.mult, [of, ssm], [of])
            tt("pool", v16(of), v16(of), bcm(onw.ap[:, 0, :], 16, 128), ALU.mult, [of, onw], [of])
            tt("dve", of.ap, of.ap, zs.ap, ALU.mult, [of, zs], [of])
            for h in range(4):
                pb = nps()
                for s in range(4):
                    tr(pb, pb.ap[:, s * 128:(s + 1) * 128], of.ap[:, s, h * 128:(h + 1) * 128], ident, [of, cst])
                evac_copy(oT.ap[:, h, :], pb.ap[:, :], [], [pb, oT])
            for hf in range(2):
                for s in range(4):
                    pb = nps()
                    for c in range(8):
                        lhs = cvn.ap[:, c, s * 128:(s + 1) * 128] if c < 4 else oT.ap[:, c - 4, s * 128:(s + 1) * 128]
                        mm(pb, pb.ap[:, :], lhs, wo.ap[:, c, hf * 512:(hf + 1) * 512], c == 0, c == 7, [cvn, oT, wo])
                    P.op("dve", lambda e, s=s, hf=hf, pb=pb, xr=xr: e.scalar_tensor_tensor(
                        out=xr.ap[:, s, hf * 512:(hf + 1) * 512], in0=xr.ap[:, s, hf * 512:(hf + 1) * 512], scalar=ALPHA,
                        in1=pb.ap[:, :], op0=ALU.mult, op1=ALU.add), reads=[xr], writes=[xr, pb])
            if debug:
                P.dma("act", DBG4[tok0:tok0 + TT, :].rearrange("(s p) d -> p s d", p=128), xr.ap, DS("dbg4"), reads=[xr], writes=[P.logical("x")])
                P.dma("act", DBG5[:, :, tok0:tok0 + TT].rearrange("c p n -> p c n"), cvn.ap, DS("dbg5"), reads=[cvn], writes=[P.logical("x")])
                P.dma("act", DBG5[:, :, NTOK + tok0:NTOK + tok0 + TT].rearrange("c p n -> p c n"), oT.ap, DS("dbg5"), reads=[oT], writes=[P.logical("x")])
                P.dma("act", DBG6[gt], wo.ap, DS("dbg5"), reads=[wo], writes=[P.logical("x")])
            ln_apply(xr, lnp[1], EPS_LN, lntmp)
            if debug:
                P.dma("act", DBG2[tok0:tok0 + TT, :].rearrange("(s p) d -> p s d", p=128), xr.ap, DS("dbg2"), reads=[xr], writes=[P.logical("x")])
                P.dma("act", DBG3[tok0:tok0 + TT, :].rearrange("(s p) d -> p s d", p=128), of.ap, DS("dbg3"), reads=[of], writes=[P.logical("x")])
            ffn_ln(l, 1, xr, lnp[2], fb)
            P.dma("act", dst[tok0:tok0 + TT, :].rearrange("(s p) d -> p s d", p=128), xr.ap, DS(f"xst{t % 2}"),
                  reads=[xr], writes=[XLb[gt]])

    if "0" in STAGES:
        stage0()
    P.barrier()
    for l in range(depth):
        load_layer_params(l)
        sq0 = 0
        for L in seq_lens:
            if "A" in STAGES:
                stageA(l, sq0, L)
            if "B" in STAGES:
                stageB(l, sq0, L)
            if "C" in STAGES:
                stageC(l, sq0, L)
            if "D" in STAGES:
                stageD(l, sq0, L)
            sq0 += L
    P.emit()
    return nc


_NC_CACHE = {}


def kernel(**inputs):
    xp = np.ascontiguousarray(inputs["x_prompt"], dtype=np.float32)
    xs = np.ascontiguousarray(inputs["x_sample"], dtype=np.float32)
    n = 8
    seq_lens = (xp.shape[1], xp.shape[1], xs.shape[1])
    nc = build(list(seq_lens))
    consts = make_consts()
    in_maps = []
    for c in range(n):
        xc = np.concatenate([xp[2 * c], xp[2 * c + 1], xs[c]], axis=0)
        m = {"x": np.ascontiguousarray(xc), "consts": consts}
        for name, _ in WSHAPES:
            m[name] = np.ascontiguousarray(inputs[name], dtype=np.float32)
        in_maps.append(m)
    res = run_bass_kernel_spmd(nc, in_maps, core_ids=list(range(n)))
    yp = np.empty_like(xp)
    ys = np.empty_like(xs)
    Lp = xp.shape[1]
    for c in range(n):
        y = res.results[c]["y"]
        yp[2 * c] = y[0:Lp]
        yp[2 * c + 1] = y[Lp:2 * Lp]
        ys[c] = y[2 * Lp:]
    return (yp, ys)
```

```python
import numpy as np
from contextlib import ExitStack
import concourse.bass as bass
import concourse.mybir as mybir
from concourse.bass_utils import run_bass_kernel_spmd

F32 = mybir.dt.float32
BF16 = mybir.dt.bfloat16
AF = mybir.ActivationFunctionType
ALU = mybir.AluOpType


class Buf:
    __slots__ = ("name", "ap", "w", "r", "const")

    def __init__(self, name, ap=None, const=False):
        self.name = name
        self.ap = ap
        self.w = None
        self.r = {}
        self.const = const


class DSem:
    __slots__ = ("name", "handle", "count")

    def __init__(self, name):
        self.name = name
        self.handle = None
        self.count = 0


class Rec:
    __slots__ = ("eng", "fn", "deps", "signal", "semval", "dsem")

    def __init__(self, eng, fn, dsem=None):
        self.eng = eng
        self.fn = fn
        self.deps = []
        self.signal = False
        self.semval = 0
        self.dsem = dsem


ENGS = ("pe", "dve", "act", "pool", "sp")


class Prog:
    def __init__(self, nc):
        self.nc = nc
        self.stack = ExitStack()
        self.recs = {e: [] for e in ENGS}
        self.dsems = []
        self.nbuf = 0

    def sbuf(self, name, shape, dtype, const=False):
        t = self.stack.enter_context(self.nc.sbuf_tensor(name, list(shape), dtype))
        return Buf(name, t, const)

    def psum(self, name, shape, dtype):
        t = self.stack.enter_context(self.nc.psum_tensor(name, list(shape), dtype))
        return Buf(name, t)

    def dram(self, name, shape, dtype):
        t = self.nc.dram_tensor(name, list(shape), dtype, kind="Internal").ap()
        return Buf(name, t)

    def logical(self, name, const=False):
        return Buf(name, None, const)

    def dsem(self, name):
        s = DSem(name)
        self.dsems.append(s)
        return s

    def _record(self, rec, reads, writes):
        eng = rec.eng
        deps = {}

        def add(d, raw):
            if d is None:
                return
            if d.dsem is None and d.eng == eng:
                if eng == "pe" or not raw:
                    return
            deps[id(d)] = d

        for b in reads:
            add(b.w, True)
        for b in writes:
            add(b.w, False)
            for r in b.r.values():
                add(r, False)
        rec.deps = list(deps.values())
        key = eng if rec.dsem is None else ("d", id(rec.dsem))
        for b in reads:
            if not b.const:
                b.r[key] = rec
        for b in writes:
            b.w = rec
            b.r = {}
        self.recs[eng].append(rec)
        return rec

    def op(self, eng, fn, reads=(), writes=()):
        return self._record(Rec(eng, fn), reads, writes)

    def dma(self, eng, out, in_, dsem, reads=(), writes=()):
        rec = Rec(eng, lambda e: e.dma_start(out=out, in_=in_), dsem=dsem)
        dsem.count += 16
        rec.semval = dsem.count
        return self._record(rec, reads, writes)

    def barrier(self):
        comp = []
        for e in ENGS:
            for rec in reversed(self.recs[e]):
                if rec.dsem is None and rec.fn is not None:
                    comp.append(rec)
                    break
        dfakes = []
        for s in self.dsems:
            if s.count > 0:
                r = Rec("sp", None, dsem=s)
                r.semval = s.count
                dfakes.append(r)
        for e in ENGS:
            rec = Rec(e, None)
            rec.deps = [x for x in comp if x.eng != e] + dfakes
            self.recs[e].append(rec)

    def emit(self):
        nc = self.nc
        for e in ENGS:
            for rec in self.recs[e]:
                for d in rec.deps:
                    if d.dsem is None:
                        d.signal = True
        for e in ENGS:
            c = 0
            for rec in self.recs[e]:
                if rec.dsem is None and rec.signal:
                    c += 1
                    rec.semval = c
        esem = {}
        for e in ENGS:
            esem[e] = self.stack.enter_context(nc.semaphore("es_" + e))
        for s in self.dsems:
            s.handle = self.stack.enter_context(nc.semaphore("ds_" + s.name))
        recs = self.recs
        dsems = self.dsems

        def run(engname, e):
            seen = {}
            for rec in recs[engname]:
                need = {}
                for d in rec.deps:
                    if d.dsem is None:
                        key, h = d.eng, esem[d.eng]
                    else:
                        key, h = id(d.dsem), d.dsem.handle
                    v = d.semval
                    if seen.get(key, 0) >= v:
                        continue
                    if key not in need or need[key][1] < v:
                        need[key] = (h, v)
                for key, (h, v) in need.items():
                    e.wait_ge(h, v)
                    seen[key] = v
                if rec.fn is None:
                    continue
                ins = rec.fn(e)
                if rec.dsem is not None:
                    ins.then_inc(rec.dsem.handle, 16)
                elif rec.signal:
                    ins.then_inc(esem[engname], 1)
            if engname == "sp":
                for s in dsems:
                    if s.count > 0:
                        e.wait_ge(s.handle, s.count)

        with nc.Block() as block:
            @block.sync
            def _(e):
                run("sp", e)

            @block.tensor
            def _(e):
                run("pe", e)

            @block.vector
            def _(e):
                run("dve", e)

            @block.scalar
            def _(e):
                run("act", e)

            @block.gpsimd
            def _(e):
                run("pool", e)
        self.stack.close()


D = 1024
FF = 2816
NF = 22
DIN = 3088
TT = 512
DEPTH = 2
ALPHA = float((2 * DEPTH) ** 0.25)
LN_EPS = 1e-5
NORM_EPS = 1e-6
AX = mybir.AxisListType.X
F32R = mybir.dt.float32r
USE_F32R = False
EXP1 = False
STAGES = "0ABCD"

WSHAPES = [
    ("ffn1_wg", (DEPTH, D, FF)), ("ffn1_wu", (DEPTH, D, FF)), ("ffn1_wd", (DEPTH, FF, D)),
    ("ln1_g", (DEPTH, D)), ("ln1_b", (DEPTH, D)), ("w_in", (DEPTH, D, DIN)),
    ("conv_w", (DEPTH, 31, 512)), ("conv_b", (DEPTH, 512)), ("conv_ln_g", (DEPTH, 512)),
    ("conv_ln_b", (DEPTH, 512)), ("sconv_w", (DEPTH, 5, 1536)), ("a_log", (DEPTH, 2, 4)),
    ("dt_bias", (DEPTH, 2, 4)), ("o_norm_w", (DEPTH, 128)), ("w_out", (DEPTH, D, D)),
    ("ln2_g", (DEPTH, D)), ("ln2_b", (DEPTH, D)), ("ffn2_wg", (DEPTH, D, FF)),
    ("ffn2_wu", (DEPTH, D, FF)), ("ffn2_wd", (DEPTH, FF, D)), ("ln3_g", (DEPTH, D)), ("ln3_b", (DEPTH, D)),
]


def make_consts():
    p = np.arange(128)[:, None]
    i = np.arange(128)[None, :]
    c = np.zeros((20, 128, 128), np.float32)
    c[0] = (p == i)
    c[1] = 1.0
    c[2] = (p <= i)
    c[3] = (p >= i)
    c[4] = (p > i)
    c[5] = (p < i)
    for lv in range(7):
        sz = 1 << lv
        m = ((p // (2 * sz)) == (i // (2 * sz))) & ((p // sz) % 2 == 1) & ((i // sz) % 2 == 0)
        c[6 + lv] = m
        c[13 + lv] = m.T
    return c


class Arena:
    def __init__(self, P, nbytes):
        self.P = P
        self.t = P.stack.enter_context(P.nc.sbuf_tensor("arena", [128, nbytes // 4], F32))
        self.cap = nbytes
        self.off = 0

    def alloc(self, name, shape, dtype, const=False):
        n = 1
        for s in shape:
            n *= s
        esz = 4 if dtype == F32 else 2
        nb = (n * esz + 63) // 64 * 64
        assert self.off + nb <= self.cap, (name, self.off, nb, self.cap)
        ap = self.t[:, self.off // 4:(self.off + nb) // 4]
        if dtype != F32:
            ap = ap.bitcast(dtype)
        ap = ap[:, 0:n]
        if len(shape) == 2:
            ap = ap.rearrange("p (a b) -> p a b", a=shape[0])
        elif len(shape) == 3:
            ap = ap.rearrange("p (a b c) -> p a b c", a=shape[0], b=shape[1])
        elif len(shape) == 4:
            ap = ap.rearrange("p (a b c d) -> p a b c d", a=shape[0], b=shape[1], c=shape[2])
        self.off += nb
        return Buf(name, ap, const)


def build(seq_lens, depth=DEPTH, debug=False):
    nc = bass.Bass("TRN2", target_bir_lowering=False)
    NTOK = sum(seq_lens)

    def din(name, shape):
        return nc.dram_tensor(name, list(shape), F32, kind="ExternalInput").ap()

    x_in = din("x", [NTOK, D])
    y_out = nc.dram_tensor("y", [NTOK, D], F32, kind="ExternalOutput").ap()
    W = {name: din(name, shape) for name, shape in WSHAPES}
    consts_in = din("consts", [20, 128, 128])

    P = Prog(nc)
    AR = Arena(P, 204 * 1024)
    dsn = {}

    def DS(name):
        if name not in dsn:
            dsn[name] = P.dsem(name)
        return dsn[name]

    ps = [P.psum(f"ps{i}", [128, 512], F32) for i in range(8)]
    pctr = [0]

    def nps():
        b = ps[pctr[0] % 8]
        pctr[0] += 1
        return b

    rr = [0]

    def evac_copy(out_ap, in_ap, reads, writes, engs=("act", "dve")):
        e = engs[rr[0] % len(engs)]
        rr[0] += 1
        if e == "act":
            P.op("act", lambda e_: e_.activation(out=out_ap, in_=in_ap, func=AF.Copy), reads=reads, writes=writes)
        else:
            P.op(e, lambda e_: e_.tensor_copy(out=out_ap, in_=in_ap), reads=reads, writes=writes)

    def mm(pb, out_ap, lhsT, rhs, start, stop, reads):
        P.op("pe", lambda e: e.matmul(out_ap, lhsT, rhs, start=start, stop=stop), reads=reads, writes=[pb])

    def mmr(pb, out_ap, lhsT, rhs, reads):
        if USE_F32R:
            lhsT, rhs = lhsT.bitcast(F32R), rhs.bitcast(F32R)
        P.op("pe", lambda e: e.matmul(out_ap, lhsT, rhs, start=True, stop=True), reads=reads, writes=[pb])

    def r32(ap):
        return ap.bitcast(F32R) if USE_F32R else ap

    def tr(pb, out_ap, in_ap, ident_ap, reads):
        P.op("pe", lambda e: e.transpose(out_ap, in_ap, ident_ap), reads=reads, writes=[pb])

    def tt(eng, out, in0, in1, op, reads, writes):
        P.op(eng, lambda e: e.tensor_tensor(out=out, in0=in0, in1=in1, op=op), reads=reads, writes=writes)

    def bc3(ap2, n, inner):
        return ap2.unsqueeze(2).to_broadcast([128, n, inner])

    def bcm(ap2, n, inner):
        return ap2.unsqueeze(1).to_broadcast([128, n, inner])

    def dscr(name, shape, dt):
        return nc.dram_tensor(name, list(shape), dt, kind=("ExternalOutput" if debug else "Internal")).ap()

    WGU = [[dscr(f"WGU{l}{k}", [NF, 128, 2, 8, 128], BF16) for k in range(2)] for l in range(depth)]
    WGUb = [[[P.logical("wgu") for _ in range(16)] for k in range(2)] for l in range(depth)]
    WD = [[dscr(f"WD{l}{k}", [2, NF, 128, 512], BF16) for k in range(2)] for l in range(depth)]
    WDb = [[[P.logical("wd") for _ in range(22)] for k in range(2)] for l in range(depth)]
    WIN = [dscr(f"WIN{l}", [20, 128, 8, 128], BF16) for l in range(depth)]
    WZ = [dscr(f"WZ{l}", [128, 8, 512], BF16) for l in range(depth)]
    WBA = [dscr(f"WBA{l}", [128, 8, 16], BF16) for l in range(depth)]
    WINb = [[P.logical("win") for _ in range(8)] for l in range(depth)]
    WO = [dscr(f"WO{l}", [128, 8, 1024], BF16) for l in range(depth)]
    WOb = [[P.logical("wo") for _ in range(8)] for l in range(depth)]
    X1 = dscr("X1", [NTOK, D], F32)
    XL = dscr("XL", [NTOK, D], F32)
    PRE = dscr("PRE", [16, 128, NTOK], BF16)
    ZS = dscr("ZS", [NTOK, 512], F32)
    CVN = dscr("CVN", [4, 128, NTOK], BF16)
    KT = dscr("KT", [4, 128, NTOK], BF16)
    QT = dscr("QT", [4, 128, NTOK], BF16)
    KTOK = dscr("KTOK", [NTOK, 512], BF16)
    VTOK = dscr("VTOK", [NTOK, 512], BF16)
    OFB = [dscr("OF", [NTOK, 512], F32), dscr("OB", [NTOK, 512], F32)]
    DBG = dscr("DBG", [2, 10, 128, 512], F32) if debug else None
    DBG4 = dscr("DBG4", [NTOK, D], F32) if debug else None
    DBG5 = dscr("DBG5", [4, 128, 2 * NTOK], BF16) if debug else None
    DBG6 = dscr("DBG6", [NTOK // TT, 128, 8, 1024], BF16) if debug else None
    DBG2 = dscr("DBG2", [NTOK, D], F32) if debug else None
    DBG3 = dscr("DBG3", [NTOK, 512], F32) if debug else None
    NT_ALL = NTOK // TT
    NB_ALL = NTOK // 128
    X1b = [P.logical("x1") for _ in range(NT_ALL)]
    XLb = [P.logical("xl") for _ in range(NT_ALL)]
    PREb = [P.logical("pre") for _ in range(NT_ALL)]
    ZSb = [P.logical("zs") for _ in range(NT_ALL)]
    CVNb = [P.logical("cvn") for _ in range(NT_ALL)]
    QKVb = [P.logical("qkv") for _ in range(NT_ALL)]
    OFBb = [[P.logical("of") for _ in range(NB_ALL)] for _ in range(2)]

    cst = AR.alloc("cst", [20, 128], F32, const=True)
    ident = cst.ap[:, 0, :]
    ones_f = cst.ap[:, 1, :]
    UT = cst.ap[:, 2, :]
    LT = cst.ap[:, 3, :]
    SL = cst.ap[:, 4, :]
    SU = cst.ap[:, 5, :]
    P.dma("sp", cst.ap, consts_in.rearrange("c p j -> p c j"), DS("cst"), writes=[cst])
    epsb = AR.alloc("epsb", [1, 8], F32)
    EPS_FFN, EPS_LN, EPS_NORM, ONE = 0, 1, 2, 3
    for col, val in ((EPS_FFN, 4 * LN_EPS), (EPS_LN, LN_EPS), (EPS_NORM, NORM_EPS), (ONE, 1.0)):
        P.op("pool", lambda e, col=col, val=val: e.memset(epsb.ap[:, 0, col:col + 1], val), writes=[epsb])

    def epsap(col):
        return epsb.ap[:, 0, col:col + 1]

    NBMAX = max(seq_lens) // 128
    BETA_all = AR.alloc("beta_all", [NBMAX, 8], F32)
    G_all = AR.alloc("g_all", [NBMAX, 8], F32)
    lnp = [AR.alloc(f"lnp{i}", [2, D], F32) for i in range(3)]
    cw = AR.alloc("cw", [4, 34], F32)
    scw = AR.alloc("scw", [12, 5], F32)
    onw = AR.alloc("onw", [1, 128], F32)
    nA = AR.alloc("nA", [1, 8], F32)
    dtb = AR.alloc("dtb", [1, 8], F32)
    BASE = AR.off

    def stage0():
        AR.off = BASE
        sf = [AR.alloc(f"stgf{i}", [1, DIN], F32) for i in range(2)]
        sb = [AR.alloc(f"stgb{i}", [1, DIN], BF16) for i in range(2)]
        cnt = [0]
        cengs = ("act", "dve", "pool")

        def conv(src_ap, ncols, stores, view=None):
            i = cnt[0] % 2
            f, b = sf[i], sb[i]
            fa = f.ap[:, 0, 0:ncols]
            ba = b.ap[:, 0, 0:ncols]
            dst_f = fa if view is None else view(fa)
            P.dma("sp", dst_f, src_ap, DS(f"stgf{i}"), writes=[f])
            ce = cengs[cnt[0] % 3]
            if ce == "act":
                P.op("act", lambda e: e.activation(out=ba, in_=fa, func=AF.Copy), reads=[f], writes=[b])
            else:
                P.op(ce, lambda e: e.tensor_copy(out=ba, in_=fa), reads=[f], writes=[b])
            for (dst, srcfn, lb) in stores:
                P.dma("act", dst, srcfn(ba), DS(f"stgb{i}"), reads=[b], writes=[lb])
            cnt[0] += 1

        for l in range(depth):
            for k, pre in enumerate(("ffn1", "ffn2")):
                for a, nm in enumerate(("_wg", "_wu")):
                    for kc in range(8):
                        conv(W[pre + nm][l, kc * 128:(kc + 1) * 128, :], FF,
                             [(WGU[l][k][:, :, a, kc, :].rearrange("f p j -> p f j"),
                               lambda ba: ba.rearrange("p (f j) -> p f j", j=128), WGUb[l][k][a * 8 + kc])])
                for f0 in range(0, NF, 2):
                    conv(W[pre + "_wd"][l, f0 * 128:(f0 + 2) * 128, :].rearrange("(f p) j -> p f j", p=128), 2048,
                         [(WD[l][k][h, f0:f0 + 2].rearrange("f p j -> p f j"),
                           (lambda ba, h=h: ba.rearrange("p (f j) -> p f j", f=2)[:, :, h * 512:(h + 1) * 512]),
                           WDb[l][k][h * 11 + f0 // 2]) for h in range(2)],
                         view=lambda fa: fa.rearrange("p (f j) -> p f j", f=2))
            for kc in range(8):
                conv(W["w_in"][l, kc * 128:(kc + 1) * 128, :], DIN,
                     [(WIN[l][:, :, kc, :].rearrange("c p j -> p c j"),
                       lambda ba: ba[:, 0:2560].rearrange("p (c j) -> p c j", j=128), WINb[l][kc]),
                      (WZ[l][:, kc, :], lambda ba: ba[:, 2560:3072], P.logical("wz")),
                      (WBA[l][:, kc, :], lambda ba: ba[:, 3072:3088], P.logical("wba"))])
            for c in range(8):
                conv(W["w_out"][l, c * 128:(c + 1) * 128, :], D,
                     [(WO[l][:, c, :], lambda ba: ba, WOb[l][c])])

    def load_layer_params(l):
        P.barrier()
        AR.off = BASE
        craw = AR.alloc("craw", [1, 512], F32)
        sraw = AR.alloc("sraw", [1, 1536], F32)
        for i, nm in enumerate(("ln1", "ln2", "ln3")):
            P.dma("sp", lnp[i].ap[:, 0, :], W[nm + "_g"][l].partition_broadcast(128), DS(f"lnp{i}"), writes=[lnp[i]])
            P.dma("sp", lnp[i].ap[:, 1, :], W[nm + "_b"][l].partition_broadcast(128), DS(f"lnp{i}"), writes=[lnp[i]])
        P.dma("sp", craw.ap[0:31, 0, :], W["conv_w"][l], DS("craw"), writes=[craw])
        P.dma("sp", craw.ap[31:32, 0, :], W["conv_b"][l:l + 1, :], DS("craw"), writes=[craw])
        P.dma("sp", craw.ap[32:33, 0, :], W["conv_ln_g"][l:l + 1, :], DS("craw"), writes=[craw])
        P.dma("sp", craw.ap[33:34, 0, :], W["conv_ln_b"][l:l + 1, :], DS("craw"), writes=[craw])
        P.dma("sp", sraw.ap[0:5, 0, :], W["sconv_w"][l], DS("sraw"), writes=[sraw])
        P.dma("sp", onw.ap[:, 0, :], W["o_norm_w"][l].partition_broadcast(128), DS("onw"), writes=[onw])
        P.dma("sp", nA.ap[:, 0, :], W["a_log"][l].rearrange("a h -> (a h)").partition_broadcast(128), DS("nA"), writes=[nA])
        P.dma("sp", dtb.ap[:, 0, :], W["dt_bias"][l].rearrange("a h -> (a h)").partition_broadcast(128), DS("dtb"), writes=[dtb])
        P.op("act", lambda e: e.activation(out=nA.ap[:, 0, :], in_=nA.ap[:, 0, :], func=AF.Exp), reads=[nA], writes=[nA])
        P.op("dve", lambda e: e.tensor_scalar(out=nA.ap[:, 0, :], in0=nA.ap[:, 0, :], scalar1=-1.0, scalar2=None, op0=ALU.mult),
             reads=[nA], writes=[nA])
        for c in range(4):
            pb = nps()
            tr(pb, pb.ap[:, 0:34], craw.ap[0:34, 0, c * 128:(c + 1) * 128], cst.ap[0:34, 0, 0:34], [craw, cst])
            P.op("dve", lambda e, c=c, pb=pb: e.tensor_copy(out=cw.ap[:, c, :], in_=pb.ap[:, 0:34]), writes=[pb, cw])
        for c in range(12):
            pb = nps()
            tr(pb, pb.ap[:, 0:5], sraw.ap[0:5, 0, c * 128:(c + 1) * 128], cst.ap[0:5, 0, 0:5], [sraw, cst])
            P.op("dve", lambda e, c=c, pb=pb: e.tensor_copy(out=scw.ap[:, c, :], in_=pb.ap[:, 0:5]), writes=[pb, scw])
        P.barrier()

    def make_xT(xr, xT):
        for kc in range(8):
            pb = nps()
            for s in range(4):
                tr(pb, pb.ap[:, s * 128:(s + 1) * 128], xr.ap[:, s, kc * 128:(kc + 1) * 128], ident, [xr, cst])
            evac_copy(xT.ap[:, kc, :], pb.ap[:, :], [], [pb, xT])

    def ln_apply(xr, lp, epscol, tmp):
        st, mv, rs = tmp
        for s in range(4):
            for c in range(2):
                P.op("dve", lambda e, s=s, c=c: e.bn_stats(out=st.ap[:, s, c, :], in_=xr.ap[:, s, c * 512:(c + 1) * 512]),
                     reads=[xr], writes=[st])
            P.op("dve", lambda e, s=s: e.bn_aggr(out=mv.ap[:, s, :], in_=st.ap[:, s, :, :].rearrange("p c k -> p (c k)")),
                 reads=[st], writes=[mv])
        P.op("act", lambda e: e.activation(out=rs.ap[:, 0, :], in_=mv.ap[:, :, 1], func=AF.Sqrt, bias=epsap(epscol), scale=1.0),
             reads=[mv, epsb], writes=[rs])
        P.op("dve", lambda e: e.reciprocal(out=rs.ap[:, 0, :], in_=rs.ap[:, 0, :]), reads=[rs], writes=[rs])
        for s in range(4):
            P.op("dve", lambda e, s=s: e.tensor_scalar(out=xr.ap[:, s, :], in0=xr.ap[:, s, :], scalar1=mv.ap[:, s, 0:1],
                                                       scalar2=rs.ap[:, 0, s:s + 1], op0=ALU.subtract, op1=ALU.mult),
                 reads=[xr, mv, rs], writes=[xr])
        tt("pool", xr.ap, xr.ap, bcm(lp.ap[:, 0, :], 4, D), ALU.mult, [xr, lp], [xr])
        tt("pool", xr.ap, xr.ap, bcm(lp.ap[:, 1, :], 4, D), ALU.add, [xr, lp], [xr])

    def ffn_ln(l, k, xr, lp, fb, do_ln=True, mid_hook=None):
        xT, hT, wgu, wdb, sg, lntmp = fb
        make_xT(xr, xT)

        def load_wgu(g):
            b = wgu[g % 2]
            P.dma("sp", b.ap.rearrange("p f a k j -> p f (a k j)"),
                  WGU[l][k][2 * g:2 * g + 2].rearrange("f p a k j -> p f (a k j)"), DS(f"wgu{g % 2}"),
                  reads=WGUb[l][k], writes=[b])

        load_wgu(0)
        for g in range(11):
            if g + 1 < 11:
                load_wgu(g + 1)
            b = wgu[g % 2]
            for f in range(2):
                ffc = 2 * g + f
                pg, pu = nps(), nps()
                for kc in range(8):
                    mm(pg, pg.ap[:, :], b.ap[:, f, 0, kc, :], xT.ap[:, kc, :], kc == 0, kc == 7, [b, xT])
                for kc in range(8):
                    mm(pu, pu.ap[:, :], b.ap[:, f, 1, kc, :], xT.ap[:, kc, :], kc == 0, kc == 7, [b, xT])
                s_ = sg[ffc % 2]
                P.op("act", lambda e, s_=s_, pg=pg: e.activation(out=s_.ap[:, 0, :], in_=pg.ap[:, :], func=AF.Silu),
                     writes=[pg, s_])
                tt("dve", hT.ap[:, ffc, :], pu.ap[:, :], s_.ap[:, 0, :], ALU.mult, [s_], [pu, hT])
        if mid_hook is not None:
            mid_hook()
        groups = [(0, 4), (4, 4), (8, 4), (12, 4), (16, 4), (20, 2)]
        seqs = [(h, gi) for h in range(2) for gi in range(len(groups))]

        def load_wd(idx):
            h, gi = seqs[idx]
            f0, n = groups[gi]
            b = wdb[idx % 2]
            P.dma("sp", b.ap[:, 0:n, :], WD[l][k][h, f0:f0 + n].rearrange("f p j -> p f j"), DS(f"wd{idx % 2}"),
                  reads=WDb[l][k], writes=[b])

        load_wd(0)
        idx = 0
        for h in range(2):
            py = [nps() for _ in range(4)]
            for gi, (f0, n) in enumerate(groups):
                if idx + 1 < len(seqs):
                    load_wd(idx + 1)
                b = wdb[idx % 2]
                for f in range(n):
                    ffc = f0 + f
                    for s in range(4):
                        mm(py[s], py[s].ap[:, :], hT.ap[:, ffc, s * 128:(s + 1) * 128], b.ap[:, f, :],
                           ffc == 0, ffc == NF - 1, [hT, b])
                idx += 1
            for s in range(4):
                P.op("dve", lambda e, s=s, h=h, p_=py[s]: e.scalar_tensor_tensor(
                    out=xr.ap[:, s, h * 512:(h + 1) * 512], in0=xr.ap[:, s, h * 512:(h + 1) * 512], scalar=2.0 * ALPHA,
                    in1=p_.ap[:, :], op0=ALU.mult, op1=ALU.add), reads=[xr], writes=[xr, py[s]])
        if do_ln:
            ln_apply(xr, lp, EPS_FFN, lntmp)

    def alloc_ffn():
        xT = AR.alloc("xT", [8, TT], BF16)
        hT = AR.alloc("hT", [NF, TT], BF16)
        wgu = [AR.alloc(f"wgu{i}", [2, 2, 8, 128], BF16) for i in range(2)]
        wdb = [AR.alloc(f"wdb{i}", [4, 512], BF16) for i in range(2)]
        sg = [AR.alloc(f"sg{i}", [1, TT], F32) for i in range(2)]
        st = AR.alloc("lnst", [4, 2, 6], F32)
        mv = AR.alloc("lnmv", [4, 2], F32)
        rs = AR.alloc("lnrs", [1, 4], F32)
        return (xT, hT, wgu, wdb, sg, (st, mv, rs))

    def stageA(l, sq0, L):
        P.barrier()
        AR.off = BASE
        fb = alloc_ffn()
        xT, wgu = fb[0], fb[2]
        xres = [AR.alloc(f"xres{i}", [4, D], F32) for i in range(2)]
        pre = AR.alloc("pre", [16, TT], BF16)
        zs = AR.alloc("zs", [4, TT], F32)
        sgm = AR.alloc("sgm", [4, TT], F32)
        wz = AR.alloc("wz", [8, 512], BF16)
        wba = AR.alloc("wba", [8, 16], BF16)
        t8 = AR.alloc("t8", [4, 8], F32)
        src = x_in if l == 0 else XL
        srcb = None if l == 0 else XLb
        P.dma("sp", wz.ap, WZ[l], DS("wz"), reads=WINb[l], writes=[wz])
        P.dma("sp", wba.ap, WBA[l], DS("wba"), reads=WINb[l], writes=[wba])
        nT = L // TT

        def a_load(t):
            tok0 = sq0 + t * TT
            gt = tok0 // TT
            xr = xres[t % 2]
            P.dma("sp", xr.ap, src[tok0:tok0 + TT, :].rearrange("(s p) d -> p s d", p=128), DS(f"xres{t % 2}"),
                  reads=([] if srcb is None else [srcb[gt]]), writes=[xr])

        def a_ln(t):
            ln_apply(xres[t % 2], lnp[0], EPS_FFN, fb[5])

        a_load(0)
        ffn_ln(l, 0, xres[0], lnp[0], fb, do_ln=False)
        for t in range(nT):
            tok0 = sq0 + t * TT
            gt = tok0 // TT
            xr = xres[t % 2]
            if t + 1 < nT:
                a_load(t + 1)
                ffn_ln(l, 0, xres[(t + 1) % 2], lnp[0], fb, do_ln=False, mid_hook=(lambda t=t: a_ln(t)))
            else:
                a_ln(t)
            P.dma("act", X1[tok0:tok0 + TT, :].rearrange("(s p) d -> p s d", p=128), xr.ap, DS(f"xst{t % 2}"),
                  reads=[xr], writes=[X1b[gt]])
            make_xT(xr, xT)
            gorder = [1, 0, 2, 3, 4]

            def load_win(i):
                b = wgu[i % 2]
                c0 = gorder[i] * 4
                P.dma("sp", b.ap.rearrange("p f a k j -> p (f a) (k j)"),
                      WIN[l][c0:c0 + 4].rearrange("c p k j -> p c (k j)"), DS(f"wgu{i % 2}"),
                      reads=WINb[l], writes=[b])

            load_win(0)
            for i in range(5):
                if i + 1 < 5:
                    load_win(i + 1)
                b = wgu[i % 2]
                bv = b.ap.rearrange("p f a k j -> p (f a) k j")
                for cc in range(4):
                    c = gorder[i] * 4 + cc
                    pb = nps()
                    for kc in range(8):
                        mm(pb, pb.ap[:, :], bv[:, cc, kc, :], xT.ap[:, kc, :], kc == 0, kc == 7, [b, xT])
                    if gorder[i] == 1:
                        P.op("act", lambda e, cc=cc, pb=pb: e.activation(out=sgm.ap[:, cc, :], in_=pb.ap[:, :], func=AF.Sigmoid),
                             writes=[pb, sgm])
                    elif gorder[i] == 0:
                        tt("dve", pre.ap[:, cc, :], pb.ap[:, :], sgm.ap[:, cc, :], ALU.mult, [sgm], [pb, pre])
                    else:
                        evac_copy(pre.ap[:, c - 4, :], pb.ap[:, :], [], [pb, pre])
            P.dma("act", PRE[:, :, tok0:tok0 + TT].rearrange("c p n -> p c n"), pre.ap, DS("prest"),
                  reads=[pre], writes=[PREb[gt]])
            for s in range(4):
                pb = nps()
                for kc in range(8):
                    mm(pb, pb.ap[:, :], xT.ap[:, kc, s * 128:(s + 1) * 128], wz.ap[:, kc, :], kc == 0, kc == 7, [xT, wz])
                P.op("act", lambda e, s=s, pb=pb: e.activation(out=zs.ap[:, s, :], in_=pb.ap[:, :], func=AF.Silu),
                     writes=[pb, zs])
            P.dma("act", ZS[tok0:tok0 + TT, :].rearrange("(s p) n -> p s n", p=128), zs.ap, DS("zsst"),
                  reads=[zs], writes=[ZSb[gt]])
            pb = nps()
            for s in range(4):
                for kc in range(8):
                    mm(pb, pb.ap[:, s * 16:(s + 1) * 16], xT.ap[:, kc, s * 128:(s + 1) * 128], wba.ap[:, kc, :],
                       kc == 0, kc == 7, [xT, wba])
            pv = pb.ap[:, 0:64].rearrange("p (s c) -> p s c", s=4)
            b0 = (t * TT) // 128
            P.op("act", lambda e, pv=pv, b0=b0: e.activation(out=BETA_all.ap[:, b0:b0 + 4, :], in_=pv[:, :, 0:8], func=AF.Sigmoid),
                 writes=[pb, BETA_all])
            tt("dve", t8.ap, pv[:, :, 8:16], bcm(dtb.ap[:, 0, :], 4, 8), ALU.add, [dtb], [pb, t8])
            P.op("act", lambda e: e.activation(out=t8.ap, in_=t8.ap, func=AF.Exp), reads=[t8], writes=[t8])
            P.op("act", lambda e: e.activation(out=t8.ap, in_=t8.ap, func=AF.Ln, bias=epsap(ONE), scale=1.0),
                 reads=[t8, epsb], writes=[t8])
            tt("dve", G_all.ap[:, b0:b0 + 4, :], t8.ap, bcm(nA.ap[:, 0, :], 4, 8), ALU.mult, [t8, nA], [G_all])

    def stageB(l, sq0, L):
        P.barrier()
        AR.off = BASE
        wc = AR.alloc("wc", [4, TT + 30], BF16)
        wq = AR.alloc("wq", [12, TT + 4], BF16)
        cacc = AR.alloc("cacc", [4, TT], F32)
        csq = AR.alloc("csq", [4, TT], F32)
        mean = AR.alloc("mean", [1, TT], F32)
        var = AR.alloc("var", [1, TT], F32)
        cvn = AR.alloc("cvn", [4, TT], BF16)
        xf = [AR.alloc(f"xf{i}", [1, TT], F32) for i in range(12)]
        sqt_l = [AR.alloc(f"sqt{i}", [1, TT], F32) for i in range(2)]
        rn_l = [AR.alloc(f"rn{i}", [1, TT], F32) for i in range(8)]
        xn_l = [AR.alloc(f"xn{i}", [1, TT], F32) for i in range(4)]
        qTo = AR.alloc("qTo", [4, TT], BF16)
        kTo = AR.alloc("kTo", [4, TT], BF16)
        ktok = AR.alloc("ktok", [4, 512], BF16)
        vtok = AR.alloc("vtok", [4, 512], BF16)
        dg31 = AR.alloc("dg31", [4 * 31, 128], BF16)
        dg5 = AR.alloc("dg5", [12 * 5, 128], BF16)
        dctr = 0
        for c in range(4):
            for j in range(31):
                e_ = ("dve", "pool", "act")[dctr % 3]
                dctr += 1
                if e_ == "act":
                    P.op("act", lambda e, c=c, j=j: e.activation(out=dg31.ap[:, c * 31 + j, :], in_=ident, func=AF.Copy, scale=cw.ap[:, c, j:j + 1]),
                         reads=[cst, cw], writes=[dg31])
                else:
                    P.op(e_, lambda e, c=c, j=j: e.tensor_scalar(out=dg31.ap[:, c * 31 + j, :], in0=ident, scalar1=cw.ap[:, c, j:j + 1], scalar2=None, op0=ALU.mult),
                         reads=[cst, cw], writes=[dg31])
        for i in range(12):
            for j in range(5):
                e_ = ("dve", "pool", "act")[dctr % 3]
                dctr += 1
                if e_ == "act":
                    P.op("act", lambda e, i=i, j=j: e.activation(out=dg5.ap[:, i * 5 + j, :], in_=ident, func=AF.Copy, scale=scw.ap[:, i, j:j + 1]),
                         reads=[cst, scw], writes=[dg5])
                else:
                    P.op(e_, lambda e, i=i, j=j: e.tensor_scalar(out=dg5.ap[:, i * 5 + j, :], in0=ident, scalar1=scw.ap[:, i, j:j + 1], scalar2=None, op0=ALU.mult),
                         reads=[cst, scw], writes=[dg5])
        nT = L // TT
        sq1 = sq0 + L
        for t in range(nT):
            tok0 = sq0 + t * TT
            gt = tok0 // TT
            nb = [PREb[g] for g in (gt - 1, gt, gt + 1) if sq0 // TT <= g < sq1 // TT]
            for (w, c0, c1, halo) in ((wc, 0, 4, 15), (wq, 4, 16, 2)):
                lo, hi = max(tok0 - halo, sq0), min(tok0 + TT + halo, sq1)
                a = lo - (tok0 - halo)
                if a > 0:
                    P.op("pool", lambda e, w=w, a=a: e.memset(w.ap[:, :, 0:a], 0.0), writes=[w])
                if hi < tok0 + TT + halo:
                    P.op("pool", lambda e, w=w, hi=hi, lo=lo, a=a, halo=halo: e.memset(w.ap[:, :, a + hi - lo:TT + 2 * halo], 0.0), writes=[w])
                P.dma("sp", w.ap[:, :, a:a + hi - lo], PRE[c0:c1, :, lo:hi].rearrange("c p n -> p c n"), DS("w" + str(c0)),
                      reads=nb, writes=[w])
            for c in range(4):
                pb = nps()
                for j in range(31):
                    mm(pb, pb.ap[:, :], dg31.ap[:, c * 31 + j, :], wc.ap[:, c, j:j + TT], j == 0, j == 30, [dg31, wc])
                if c % 2 == 0:
                    P.op("act", lambda e, c=c, pb=pb: e.activation(out=cacc.ap[:, c, :], in_=pb.ap[:, :], func=AF.Identity,
                                                                    bias=cw.ap[:, c, 31:32], scale=1.0), reads=[cw], writes=[pb, cacc])
                else:
                    P.op("dve", lambda e, c=c, pb=pb: e.tensor_scalar(out=cacc.ap[:, c, :], in0=pb.ap[:, :], scalar1=cw.ap[:, c, 31:32],
                                                                       scalar2=None, op0=ALU.add), reads=[cw], writes=[pb, cacc])
            for i in range(12):
                x_ = xf[i]
                pc = nps()
                for j in range(5):
                    mm(pc, pc.ap[:, :], dg5.ap[:, i * 5 + j, :], wq.ap[:, i, j:j + TT], j == 0, j == 4, [dg5, wq])
                P.op("act", lambda e, x_=x_, pc=pc: e.activation(out=x_.ap[:, 0, :], in_=pc.ap[:, :], func=AF.Silu), writes=[pc, x_])
            tt("pool", csq.ap, cacc.ap, cacc.ap, ALU.mult, [cacc], [csq])
            p1, p2 = nps(), nps()
            for c in range(4):
                mm(p1, p1.ap[:, :], ones_f, cacc.ap[:, c, :], c == 0, c == 3, [cst, cacc])
            for c in range(4):
                mm(p2, p2.ap[:, :], ones_f, csq.ap[:, c, :], c == 0, c == 3, [cst, csq])
            P.op("act", lambda e, p1=p1: e.activation(out=mean.ap[:, 0, :], in_=p1.ap[:, :], func=AF.Copy, scale=1.0 / 512),
                 writes=[p1, mean])
            tt("dve", var.ap[:, 0, :], mean.ap[:, 0, :], mean.ap[:, 0, :], ALU.mult, [mean], [var])
            P.op("dve", lambda e, p2=p2: e.scalar_tensor_tensor(out=var.ap[:, 0, :], in0=p2.ap[:, :], scalar=1.0 / 512,
                                                               in1=var.ap[:, 0, :], op0=ALU.mult, op1=ALU.subtract),
                 reads=[var], writes=[var, p2])
            for i in range(8):
                x_, sqt, rn = xf[i], sqt_l[i % 2], rn_l[i]
                tt("pool", sqt.ap[:, 0, :], x_.ap[:, 0, :], x_.ap[:, 0, :], ALU.mult, [x_], [sqt])
                pb = nps()
                mm(pb, pb.ap[:, :], ones_f, sqt.ap[:, 0, :], True, True, [cst, sqt])
                P.op("act", lambda e, pb=pb, rn=rn: e.activation(out=rn.ap[:, 0, :], in_=pb.ap[:, :], func=AF.Ln, bias=epsap(EPS_NORM), scale=1.0),
                     reads=[epsb], writes=[pb, rn])
            P.op("act", lambda e: e.activation(out=var.ap[:, 0, :], in_=var.ap[:, 0, :], func=AF.Ln, bias=epsap(EPS_LN), scale=1.0),
                 reads=[var, epsb], writes=[var])
            for i in range(8):
                rn = rn_l[i]
                P.op("act", lambda e, rn=rn: e.activation(out=rn.ap[:, 0, :], in_=rn.ap[:, 0, :], func=AF.Exp, scale=-0.5), reads=[rn], writes=[rn])
            P.op("act", lambda e: e.activation(out=var.ap[:, 0, :], in_=var.ap[:, 0, :], func=AF.Exp, scale=-0.5), reads=[var], writes=[var])
            for i in range(8):
                x_, rn, h = xf[i], rn_l[i], i % 4
                if i < 4:
                    P.op("dve", lambda e, h=h, x_=x_, rn=rn: e.scalar_tensor_tensor(out=qTo.ap[:, h, :], in0=x_.ap[:, 0, :], scalar=float(128 ** -0.5),
                                                                                 in1=rn.ap[:, 0, :], op0=ALU.mult, op1=ALU.mult),
                         reads=[x_, rn], writes=[qTo])
                else:
                    xn = xn_l[h]
                    tt("dve", xn.ap[:, 0, :], x_.ap[:, 0, :], rn.ap[:, 0, :], ALU.mult, [x_, rn], [xn])
            tt("dve", cacc.ap, cacc.ap, bcm(mean.ap[:, 0, :], 4, TT), ALU.subtract, [cacc, mean], [cacc])
            tt("pool", cacc.ap, cacc.ap, bcm(var.ap[:, 0, :], 4, TT), ALU.mult, [cacc, var], [cacc])
            for h in range(4):
                xn = xn_l[h]
                P.op("act", lambda e, h=h, xn=xn: e.activation(out=kTo.ap[:, h, :], in_=xn.ap[:, 0, :], func=AF.Copy), reads=[xn], writes=[kTo])
                pb = nps()
                for s_ in range(4):
                    tr(pb, pb.ap[:, s_ * 128:(s_ + 1) * 128], xn.ap[:, 0, s_ * 128:(s_ + 1) * 128], ident, [xn, cst])
                evac_copy(ktok.ap[:, :, h * 128:(h + 1) * 128], pb.ap[:, :].rearrange("p (s c) -> p s c", s=4), [], [pb, ktok])
            for h in range(4):
                x_ = xf[8 + h]
                pb = nps()
                for s_ in range(4):
                    tr(pb, pb.ap[:, s_ * 128:(s_ + 1) * 128], x_.ap[:, 0, s_ * 128:(s_ + 1) * 128], ident, [x_, cst])
                evac_copy(vtok.ap[:, :, h * 128:(h + 1) * 128], pb.ap[:, :].rearrange("p (s c) -> p s c", s=4), [], [pb, vtok])
            for c in range(4):
                P.op("act", lambda e, c=c: e.activation(out=cvn.ap[:, c, :], in_=cacc.ap[:, c, :], func=AF.Silu,
                                                        scale=cw.ap[:, c, 32:33], bias=cw.ap[:, c, 33:34]),
                     reads=[cacc, cw], writes=[cvn])
            P.dma("act", CVN[:, :, tok0:tok0 + TT].rearrange("c p n -> p c n"), cvn.ap, DS("cvnst"),
                  reads=[cvn], writes=[CVNb[gt]])
            lb = QKVb[gt]
            P.dma("act", QT[:, :, tok0:tok0 + TT].rearrange("h p n -> p h n"), qTo.ap, DS("qst"), reads=[qTo], writes=[P.logical("x")])
            P.dma("act", KT[:, :, tok0:tok0 + TT].rearrange("h p n -> p h n"), kTo.ap, DS("kst"), reads=[kTo], writes=[P.logical("x")])
            P.dma("act", KTOK[tok0:tok0 + TT, :].rearrange("(s p) n -> p s n", p=128), ktok.ap, DS("ktst"), reads=[ktok], writes=[P.logical("x")])
            P.dma("act", VTOK[tok0:tok0 + TT, :].rearrange("(s p) n -> p s n", p=128), vtok.ap, DS("vtst"), reads=[vtok], writes=[lb])

    def stageC(l, sq0, L):
        P.barrier()
        AR.off = BASE
        nB = L // 128
        H4 = [4, 128]
        NOP = 3
        S = [AR.alloc(f"S{d}", H4, F32) for d in range(2)]
        Sbf = [AR.alloc(f"Sbf{d}", H4, BF16) for d in range(2)]
        for d in range(2):
            P.op("pool", lambda e, d=d: e.memset(S[d].ap, 0.0), writes=[S[d]])
            P.op("pool", lambda e, d=d: e.memset(Sbf[d].ap, 0.0), writes=[Sbf[d]])
        opnd = [[{nm: AR.alloc(f"{nm}{d}{i}", H4, BF16) for nm in ("kT", "qT", "kt", "vt")} for i in range(NOP)] for d in range(2)]
        PR = [[{} for _ in range(2)] for _ in range(2)]
        T = [{} for _ in range(2)]
        for d in range(2):
            for par in range(2):
                for nm in ("M0",):
                    PR[d][par][nm] = AR.alloc(f"{nm}{d}{par}", H4, F32)
                for nm in ("N0", "qkdT", "rw", "ktl", "vb"):
                    PR[d][par][nm] = AR.alloc(f"{nm}{d}{par}", H4, BF16)
                for nm in ("egc", "gl"):
                    PR[d][par][nm] = AR.alloc(f"{nm}{d}{par}", [1, 4], F32)
            for nm in ("G2", "Dm", "DT", "KKm", "u", "tmp"):
                T[d][nm] = AR.alloc(f"{nm}{d}", H4, F32)
            for nm in ("wT", "vnew", "Mo", "No", "Y1", "Y2", "T0", "T1", "TT0", "TT1"):
                T[d][nm] = AR.alloc(f"{nm}{d}", H4, BF16)
            T[d]["o"] = [AR.alloc(f"o{d}{i}", H4, F32) for i in range(2)]
            T[d]["sc"] = AR.alloc(f"sc{d}", [1, 8], F32)
            for nm in ("etl", "nb", "be"):
                T[d][nm] = AR.alloc(f"{nm}{d}", [1, 4], F32)
        tri = [UT, LT]
        smask = [SL, SU]

        def v4(p_):
            return p_.ap[:, :].rearrange("p (h c) -> p h c", h=4)

        def blk_of(i, d):
            return i if d == 0 else nB - 1 - i

        def load_ops(i):
            for d in range(2):
                tb = sq0 + blk_of(i, d) * 128
                o = opnd[d][i % NOP]
                dsf = lambda nm: DS(f"op{nm}{d}{i % NOP}")
                rb = [QKVb[tb // TT]]
                P.dma("sp", o["kT"].ap, KT[:, :, tb:tb + 128].rearrange("h p n -> p h n"), dsf("kT"), reads=rb, writes=[o["kT"]])
                P.dma("sp", o["qT"].ap, QT[:, :, tb:tb + 128].rearrange("h p n -> p h n"), dsf("qT"), reads=rb, writes=[o["qT"]])
                P.dma("sp", o["kt"].ap, KTOK[tb:tb + 128, :].rearrange("p (h c) -> p h c", h=4), dsf("kt"), reads=rb, writes=[o["kt"]])
                P.dma("sp", o["vt"].ap, VTOK[tb:tb + 128, :].rearrange("p (h c) -> p h c", h=4), dsf("vt"), reads=rb, writes=[o["vt"]])

        def prep_pieces(i):
            par = i % 2

            def ctx(d):
                blk = blk_of(i, d)
                return (T[d], PR[d][par], opnd[d][i % NOP],
                        G_all.ap[:, blk, d * 4:(d + 1) * 4], BETA_all.ap[:, blk, d * 4:(d + 1) * 4])

            def piece_scal():
                for d in range(2):
                    t, pr, O, g, beta = ctx(d)
                    pb = nps()
                    mm(pb, pb.ap[:, 0:4], tri[d], g, True, True, [cst, G_all])
                    mm(pb, pb.ap[:, 4:8], ones_f, g, True, True, [cst, G_all])
                    sc = t["sc"]
                    P.op("dve", lambda e, sc=sc, pb=pb: e.tensor_copy(out=sc.ap[:, 0, :], in_=pb.ap[:, 0:8]), writes=[pb, sc])
                    P.op("act", lambda e, sc=sc, pr=pr: e.activation(out=pr["egc"].ap[:, 0, :], in_=sc.ap[:, 0, 0:4], func=AF.Exp), reads=[sc], writes=[pr["egc"]])
                    tt("dve", t["etl"].ap[:, 0, :], sc.ap[:, 0, 4:8], sc.ap[:, 0, 0:4], ALU.subtract, [sc], [t["etl"]])
                    P.op("act", lambda e, t=t: e.activation(out=t["etl"].ap[:, 0, :], in_=t["etl"].ap[:, 0, :], func=AF.Exp), reads=[t["etl"]], writes=[t["etl"]])
                    P.op("act", lambda e, sc=sc, pr=pr: e.activation(out=pr["gl"].ap[:, 0, :], in_=sc.ap[:, 0, 4:8], func=AF.Exp), reads=[sc], writes=[pr["gl"]])
                    P.op("dve", lambda e, t=t, beta=beta: e.tensor_scalar(out=t["nb"].ap[:, 0, :], in0=beta, scalar1=-1.0, scalar2=None, op0=ALU.mult),
                         reads=[BETA_all], writes=[t["nb"]])
                    tt("dve", t["be"].ap[:, 0, :], beta, pr["egc"].ap[:, 0, :], ALU.mult, [BETA_all, pr["egc"]], [t["be"]])
                    tt("pool", t["G2"].ap, bcm(smask[d], 4, 128), bc3(g, 4, 128), ALU.mult, [cst, G_all], [t["G2"]])
                    tt("pool", pr["vb"].ap, O["vt"].ap, bc3(beta, 4, 128), ALU.mult, [O["vt"], BETA_all], [pr["vb"]])

            def piece_decay():
                for d in range(2):
                    t, pr, O, g, beta = ctx(d)
                    pD, pDT = nps(), nps()
                    for h in range(4):
                        mm(pD, pD.ap[:, h * 128:(h + 1) * 128], tri[d], t["G2"].ap[:, h, :], True, True, [cst, t["G2"]])
                    for h in range(4):
                        mm(pDT, pDT.ap[:, h * 128:(h + 1) * 128], t["G2"].ap[:, h, :], tri[d], True, True, [cst, t["G2"]])
                    P.op("act", lambda e, t=t, pD=pD: e.activation(out=t["Dm"].ap, in_=v4(pD), func=AF.Exp), writes=[pD, t["Dm"]])
                    P.op("act", lambda e, t=t, pDT=pDT: e.activation(out=t["DT"].ap, in_=v4(pDT), func=AF.Exp), writes=[pDT, t["DT"]])
                    tt("pool", t["DT"].ap, t["DT"].ap, bcm(tri[d], 4, 128), ALU.mult, [t["DT"], cst], [t["DT"]])

            def piece_kk():
                for d in range(2):
                    t, pr, O, g, beta = ctx(d)
                    pK, pQ = nps(), nps()
                    for h in range(4):
                        mm(pK, pK.ap[:, h * 128:(h + 1) * 128], O["kT"].ap[:, h, :], O["kT"].ap[:, h, :], True, True, [O["kT"]])
                    for h in range(4):
                        mm(pQ, pQ.ap[:, h * 128:(h + 1) * 128], O["kT"].ap[:, h, :], O["qT"].ap[:, h, :], True, True, [O["kT"], O["qT"]])
                    tt("dve", t["KKm"].ap, v4(pK), bcm(smask[d], 4, 128), ALU.mult, [cst], [pK, t["KKm"]])
                    tt("dve", pr["qkdT"].ap, v4(pQ), t["DT"].ap, ALU.mult, [t["DT"]], [pQ, pr["qkdT"]])
                    tt("pool", t["KKm"].ap, t["KKm"].ap, t["Dm"].ap, ALU.mult, [t["KKm"], t["Dm"]], [t["KKm"]])
                    tt("pool", pr["M0"].ap, t["KKm"].ap, bc3(t["nb"].ap[:, 0, :], 4, 128), ALU.mult, [t["KKm"], t["nb"]], [pr["M0"]])

            def piece_n0():
                for d in range(2):
                    t, pr, O, g, beta = ctx(d)
                    pN = nps()
                    for h in range(4):
                        tr(pN, pN.ap[:, h * 128:(h + 1) * 128], pr["M0"].ap[:, h, :], ident, [pr["M0"], cst])
                    P.op("act", lambda e, pr=pr, pN=pN: e.activation(out=pr["N0"].ap, in_=v4(pN), func=AF.Copy), writes=[pN, pr["N0"]])
                    tt("pool", pr["rw"].ap, O["kt"].ap, bc3(t["be"].ap[:, 0, :], 4, 128), ALU.mult, [O["kt"], t["be"]], [pr["rw"]])
                    tt("pool", pr["ktl"].ap, O["kt"].ap, bc3(t["etl"].ap[:, 0, :], 4, 128), ALU.mult, [O["kt"], t["etl"]], [pr["ktl"]])

            return [piece_scal, piece_decay, piece_kk, piece_n0]

        def level(i, lv, cur):
            nxt = 1 - cur
            for d in range(2):
                t, pr = T[d], PR[d][i % 2]
                mk = cst.ap[:, 6 + lv, :] if d == 0 else cst.ap[:, 13 + lv, :]
                mkT = cst.ap[:, 13 + lv, :] if d == 0 else cst.ap[:, 6 + lv, :]
                Tc, TTc = t[f"T{cur}"], t[f"TT{cur}"]
                Tn, TTn = t[f"T{nxt}"], t[f"TT{nxt}"]
                tt("pool", t["Mo"].ap, pr["M0"].ap, bcm(mk, 4, 128), ALU.mult, [pr["M0"], cst], [t["Mo"]])
                tt("pool", t["No"].ap, pr["N0"].ap, bcm(mkT, 4, 128), ALU.mult, [pr["N0"], cst], [t["No"]])
                if lv == 0:
                    tt("dve", TTn.ap, t["No"].ap, bcm(ident, 4, 128), ALU.add, [t["No"], cst], [TTn])
                    tt("dve", Tn.ap, t["Mo"].ap, bcm(ident, 4, 128), ALU.add, [t["Mo"], cst], [Tn])
                    continue
                pY1 = nps()
                for h in range(4):
                    mm(pY1, pY1.ap[:, h * 128:(h + 1) * 128], t["Mo"].ap[:, h, :], TTc.ap[:, h, :], True, True, [t["Mo"], TTc])
                if lv < 6:
                    pY2 = nps()
                    for h in range(4):
                        mm(pY2, pY2.ap[:, h * 128:(h + 1) * 128], t["No"].ap[:, h, :], Tc.ap[:, h, :], True, True, [t["No"], Tc])
                P.op("act", lambda e, t=t, pY1=pY1: e.activation(out=t["Y1"].ap, in_=v4(pY1), func=AF.Copy), writes=[pY1, t["Y1"]])
                if lv < 6:
                    P.op("act", lambda e, t=t, pY2=pY2: e.activation(out=t["Y2"].ap, in_=v4(pY2), func=AF.Copy), writes=[pY2, t["Y2"]])
                pZ = nps()
                for h in range(4):
                    mm(pZ, pZ.ap[:, h * 128:(h + 1) * 128], Tc.ap[:, h, :], t["Y1"].ap[:, h, :], True, True, [Tc, t["Y1"]])
                if lv < 6:
                    pZ2 = nps()
                    for h in range(4):
                        mm(pZ2, pZ2.ap[:, h * 128:(h + 1) * 128], TTc.ap[:, h, :], t["Y2"].ap[:, h, :], True, True, [TTc, t["Y2"]])
                tt("dve", TTn.ap, TTc.ap, v4(pZ), ALU.add, [TTc], [pZ, TTn])
                if lv < 6:
                    tt("dve", Tn.ap, Tc.ap, v4(pZ2), ALU.add, [Tc], [pZ2, Tn])
            return nxt

        def apply_T(i, cur, d):
            t, pr = T[d], PR[d][i % 2]
            Xf = t[f"TT{cur}"]
            pU, pW = nps(), nps()
            for h in range(4):
                mm(pU, pU.ap[:, h * 128:(h + 1) * 128], Xf.ap[:, h, :], pr["vb"].ap[:, h, :], True, True, [Xf, pr["vb"]])
            for h in range(4):
                mm(pW, pW.ap[:, h * 128:(h + 1) * 128], pr["rw"].ap[:, h, :], Xf.ap[:, h, :], True, True, [Xf, pr["rw"]])
            P.op("act", lambda e: e.activation(out=t["u"].ap, in_=v4(pU), func=AF.Copy), writes=[pU, t["u"]])
            P.op("dve", lambda e: e.tensor_copy(out=t["wT"].ap, in_=v4(pW)), writes=[pW, t["wT"]])

        def scan(i, d):
            t, pr = T[d], PR[d][i % 2]
            O = opnd[d][i % NOP]
            ob = t["o"][i % 2]
            pWS, pQS = nps(), nps()
            for h in range(4):
                mm(pWS, pWS.ap[:, h * 128:(h + 1) * 128], t["wT"].ap[:, h, :], Sbf[d].ap[:, h, :], True, True, [t["wT"], Sbf[d]])
            for h in range(4):
                mm(pQS, pQS.ap[:, h * 128:(h + 1) * 128], O["qT"].ap[:, h, :], Sbf[d].ap[:, h, :], True, True, [O["qT"], Sbf[d]])
            tt("dve", t["vnew"].ap, t["u"].ap, v4(pWS), ALU.subtract, [t["u"]], [pWS, t["vnew"]])
            pO2, pDS = nps(), nps()
            for h in range(4):
                mm(pO2, pO2.ap[:, h * 128:(h + 1) * 128], pr["qkdT"].ap[:, h, :], t["vnew"].ap[:, h, :], True, True, [pr["qkdT"], t["vnew"]])
            for h in range(4):
                mm(pDS, pDS.ap[:, h * 128:(h + 1) * 128], pr["ktl"].ap[:, h, :], t["vnew"].ap[:, h, :], True, True, [pr["ktl"], t["vnew"]])
            tt("dve", t["tmp"].ap, v4(pQS), bc3(pr["egc"].ap[:, 0, :], 4, 128), ALU.mult, [pr["egc"]], [pQS, t["tmp"]])
            tt("dve", ob.ap, t["tmp"].ap, v4(pO2), ALU.add, [t["tmp"]], [pO2, ob])
            tb = sq0 + blk_of(i, d) * 128
            P.dma("act", OFB[d][tb:tb + 128, :].rearrange("p (h c) -> p h c", h=4), ob.ap, DS(f"ost{d}{i % 2}"),
                  reads=[ob], writes=[OFBb[d][tb // 128]])
            tt("pool", S[d].ap, S[d].ap, bc3(pr["gl"].ap[:, 0, :], 4, 128), ALU.mult, [S[d], pr["gl"]], [S[d]])
            tt("dve", S[d].ap, S[d].ap, v4(pDS), ALU.add, [S[d]], [pDS, S[d]])
            P.op("act", lambda e: e.activation(out=Sbf[d].ap, in_=S[d].ap, func=AF.Copy), reads=[S[d]], writes=[Sbf[d]])

        load_ops(0)
        if nB > 1:
            load_ops(1)
        for p_ in prep_pieces(0):
            p_()
        for i in range(nB):
            if i + 2 < nB:
                load_ops(i + 2)
            pend = prep_pieces(i + 1) if i + 1 < nB else []
            cur = 0
            for lv in range(7):
                cur = level(i, lv, cur)
                if pend and lv >= 1:
                    pend.pop(0)()
            while pend:
                pend.pop(0)()
            for d in range(2):
                apply_T(i, cur, d)
            for d in range(2):
                scan(i, d)

    def stageD(l, sq0, L):
        P.barrier()
        AR.off = BASE
        fb = alloc_ffn()
        lntmp = fb[5]
        xres = [AR.alloc(f"xres{i}", [4, D], F32) for i in range(2)]
        of = AR.alloc("of", [4, 512], F32)
        ob = AR.alloc("ob", [4, 512], F32)
        zs = AR.alloc("zs", [4, 512], F32)
        cvn = AR.alloc("cvn", [4, TT], BF16)
        oT = AR.alloc("oT", [4, TT], BF16)
        wo = AR.alloc("wo", [8, D], BF16)
        ssm = AR.alloc("ssm", [1, 16], F32)
        dst = XL if l == 0 else y_out
        nT = L // TT
        v16 = lambda b_: b_.ap.rearrange("p s (h c) -> p (s h) c", h=4)

        def front_elem(t):
            tok0 = sq0 + t * TT
            gt = tok0 // TT
            xr = xres[t % 2]
            P.dma("sp", xr.ap, X1[tok0:tok0 + TT, :].rearrange("(s p) d -> p s d", p=128), DS(f"xres{t % 2}"),
                  reads=[X1b[gt]], writes=[xr])
            bl = range(tok0 // 128, tok0 // 128 + 4)
            P.dma("sp", of.ap, OFB[0][tok0:tok0 + TT, :].rearrange("(s p) n -> p s n", p=128), DS("ofl"),
                  reads=[OFBb[0][b] for b in bl], writes=[of])
            P.dma("sp", ob.ap, OFB[1][tok0:tok0 + TT, :].rearrange("(s p) n -> p s n", p=128), DS("obl"),
                  reads=[OFBb[1][b] for b in bl], writes=[ob])
            P.dma("sp", zs.ap, ZS[tok0:tok0 + TT, :].rearrange("(s p) n -> p s n", p=128), DS("zsl"), reads=[ZSb[gt]], writes=[zs])
            P.dma("sp", cvn.ap, CVN[:, :, tok0:tok0 + TT].rearrange("c p n -> p c n"), DS("cvnl"), reads=[CVNb[gt]], writes=[cvn])
            if t == 0:
                P.dma("sp", wo.ap, WO[l], DS("wo"), reads=WOb[l], writes=[wo])
            tt("dve", of.ap, of.ap, ob.ap, ALU.add, [of, ob], [of])
            tt("pool", ob.ap, of.ap, of.ap, ALU.mult, [of], [ob])
            P.op("dve", lambda e: e.tensor_reduce(out=ssm.ap[:, 0, :], in_=v16(ob), axis=AX, op=ALU.add), reads=[ob], writes=[ssm])
            P.op("act", lambda e: e.activation(out=ssm.ap[:, 0, :], in_=ssm.ap[:, 0, :], func=AF.Sqrt, bias=epsap(EPS_NORM), scale=1.0 / 128),
                 reads=[ssm, epsb], writes=[ssm])
            P.op("dve", lambda e: e.reciprocal(out=ssm.ap[:, 0, :], in_=ssm.ap[:, 0, :]), reads=[ssm], writes=[ssm])
            tt("pool", v16(of), v16(of), bc3(ssm.ap[:, 0, :], 16, 128), ALU.mult, [of, ssm], [of])
            tt("pool", v16(of), v16(of), bcm(onw.ap[:, 0, :], 16, 128), ALU.mult, [of, onw], [of])
            tt("dve", of.ap, of.ap, zs.ap, ALU.mult, [of, zs], [of])

        def front_pe(t):
            xr = xres[t % 2]
            for h in range(4):
                pb = nps()
                for s_ in range(4):
                    tr(pb, pb.ap[:, s_ * 128:(s_ + 1) * 128], of.ap[:, s_, h * 128:(h + 1) * 128], ident, [of, cst])
                evac_copy(oT.ap[:, h, :], pb.ap[:, :], [], [pb, oT])
            for hf in range(2):
                for s_ in range(4):
                    pb = nps()
                    for c in range(8):
                        lhs = cvn.ap[:, c, s_ * 128:(s_ + 1) * 128] if c < 4 else oT.ap[:, c - 4, s_ * 128:(s_ + 1) * 128]
                        mm(pb, pb.ap[:, :], lhs, wo.ap[:, c, hf * 512:(hf + 1) * 512], c == 0, c == 7, [cvn, oT, wo])
                    P.op("dve", lambda e, s_=s_, hf=hf, pb=pb, xr=xr: e.scalar_tensor_tensor(
                        out=xr.ap[:, s_, hf * 512:(hf + 1) * 512], in0=xr.ap[:, s_, hf * 512:(hf + 1) * 512], scalar=ALPHA,
                        in1=pb.ap[:, :], op0=ALU.mult, op1=ALU.add), reads=[xr], writes=[xr, pb])

        front_elem(0)
        front_pe(0)
        for t in range(nT):
            tok0 = sq0 + t * TT
            gt = tok0 // TT
            xr = xres[t % 2]
            ln_apply(xr, lnp[1], EPS_LN, lntmp)
            nxt_hook = (lambda t=t: front_elem(t + 1)) if t + 1 < nT else None
            ffn_ln(l, 1, xr, lnp[2], fb, do_ln=False, mid_hook=nxt_hook)
            ln_apply(xr, lnp[2], EPS_FFN, lntmp)
            if t + 1 < nT:
                front_pe(t + 1)
            P.dma("act", dst[tok0:tok0 + TT, :].rearrange("(s p) d -> p s d", p=128), xr.ap, DS(f"xst{t % 2}"),
                  reads=[xr], writes=[XLb[gt]])

    if "0" in STAGES:
        stage0()
    P.barrier()
    for l in range(depth):
        load_layer_params(l)
        sq0 = 0
        for L in seq_lens:
            if "A" in STAGES:
                stageA(l, sq0, L)
            if "B" in STAGES:
                stageB(l, sq0, L)
            if "C" in STAGES:
                stageC(l, sq0, L)
            if "D" in STAGES:
                stageD(l, sq0, L)
            sq0 += L
    P.emit()
    return nc


_NC_CACHE = {}


def kernel(**inputs):
    xp = np.ascontiguousarray(inputs["x_prompt"], dtype=np.float32)
    xs = np.ascontiguousarray(inputs["x_sample"], dtype=np.float32)
    n = 8
    seq_lens = (xp.shape[1], xp.shape[1], xs.shape[1])
    nc = build(list(seq_lens))
    consts = make_consts()
    in_maps = []
    for c in range(n):
        xc = np.concatenate([xp[2 * c], xp[2 * c + 1], xs[c]], axis=0)
        m = {"x": np.ascontiguousarray(xc), "consts": consts}
        for name, _ in WSHAPES:
            m[name] = np.ascontiguousarray(inputs[name], dtype=np.float32)
        in_maps.append(m)
    res = run_bass_kernel_spmd(nc, in_maps, core_ids=list(range(n)))
    yp = np.empty_like(xp)
    ys = np.empty_like(xs)
    Lp = xp.shape[1]
    for c in range(n):
        y = res.results[c]["y"]
        yp[2 * c] = y[0:Lp]
        yp[2 * c + 1] = y[Lp:2 * Lp]
        ys[c] = y[2 * Lp:]
    return (yp, ys)
```
